# Optimizing a Trainium2 kernel written in Bass

```python
import math
import jax, jax.numpy as jnp
from jax import lax
import numpy as np

D_MODEL = 1024
BATCH = 8
SEQ = 2048
DEPTH = 2
DEC_BATCH = 128
DEC_SEQ = 1
PAST_LEN = 16384
PAGE_SIZE = 128

DN_ALPHA = (2 * DEPTH) ** 0.25
DN_BETA = (8 * DEPTH) ** -0.25
LN_EPS = 1e-5
GDN_HEADS = 4
GDN_DK = 128
GDN_DV = 128
GDN_WIDTH = GDN_HEADS * GDN_DK
SCONV_W = 4
GDN_CHUNK = 64
CC_CH = D_MODEL // 2
CC_W = 31
GM_WIDTH = D_MODEL
GM_GROUPS = 4
GM_CHUNK = 128
N_MEM = 256
XA_HEADS = 4
XA_HEAD_DIM = D_MODEL // XA_HEADS
D_FF = 2816
AB_IN = 4 * GDN_WIDTH + 2 * GDN_HEADS + 2 * CC_CH

kernel_name = "hybrid_gdn_conformer_gmlp_decoder_step"


def layer_norm(x, g, b):
    xf = x.astype(jnp.float32)
    mu = jnp.mean(xf, -1, keepdims=True)
    var = jnp.mean(jnp.square(xf - mu), -1, keepdims=True)
    return ((xf - mu) * lax.rsqrt(var + LN_EPS) * g.astype(jnp.float32) + b.astype(jnp.float32)).astype(x.dtype)


def rms_norm(x, g):
    xf = x.astype(jnp.float32)
    return xf * lax.rsqrt(jnp.mean(xf * xf, -1, keepdims=True) + 1e-6) * g.astype(jnp.float32)


def l2_normalize(x):
    xf = x.astype(jnp.float32)
    return xf * lax.rsqrt(jnp.sum(xf * xf, -1, keepdims=True) + 1e-6)


def swiglu(x, wg, wu, wd):
    return (jax.nn.silu(x @ wg) * (x @ wu)) @ wd


def causal_depthwise_conv(x, buf, w):
    xx = jnp.concatenate([buf.astype(x.dtype), x], axis=1)
    y = lax.conv_general_dilated(xx, w[:, None, :].astype(x.dtype), (1,), 'VALID',
                                 dimension_numbers=('NWC', 'WIO', 'NWC'),
                                 feature_group_count=x.shape[-1])
    return y, xx[:, xx.shape[1] - (w.shape[0] - 1):]


def gated_delta_rule(q, k, v, g, beta, S0):
    N, L, H, dk = k.shape
    dv = v.shape[-1]
    c = min(GDN_CHUNK, L)
    pad = (-L) % c
    f32 = jnp.float32
    q, k, v, g, beta = (t.astype(f32) for t in (q, k, v, g, beta))
    if pad:
        q, k, v = (jnp.pad(t, ((0, 0), (0, pad), (0, 0), (0, 0))) for t in (q, k, v))
        g, beta = (jnp.pad(t, ((0, 0), (0, pad), (0, 0))) for t in (g, beta))
    n = (L + pad) // c
    chunks = lambda t: t.reshape(N, n, c, H, -1).transpose(0, 3, 1, 2, 4)
    q, k, v = chunks(q), chunks(k), chunks(v)
    g = jnp.cumsum(g.reshape(N, n, c, H).transpose(0, 3, 1, 2), axis=-1)
    beta = beta.reshape(N, n, c, H).transpose(0, 3, 1, 2)
    causal = jnp.tril(jnp.ones((c, c), bool))
    strict = jnp.tril(jnp.ones((c, c), bool), -1)
    diff = g[..., :, None] - g[..., None, :]
    decay = jnp.where(causal, jnp.exp(jnp.where(causal, diff, 0.0)), 0.0)
    kb = k * beta[..., None]
    M = jnp.where(strict, jnp.einsum('bhnid,bhnjd->bhnij', kb, k) * decay, 0.0)
    A = M + jnp.eye(c, dtype=f32)
    rhs = jnp.concatenate([v * beta[..., None], kb * jnp.exp(g)[..., None]], axis=-1)
    sol = lax.linalg.triangular_solve(A, rhs, left_side=True, lower=True, unit_diagonal=True)
    u, w = sol[..., :dv], sol[..., dv:]
    qk = jnp.einsum('bhnid,bhnjd->bhnij', q, k) * decay
    q_g = q * jnp.exp(g)[..., None]
    k_tail = k * jnp.exp(g[..., -1:] - g)[..., None]
    g_last = jnp.exp(g[..., -1])

    def step(S, inp):
        qg_i, kt_i, u_i, w_i, qk_i, gl_i = inp
        v_new = u_i - jnp.einsum('bhcd,bhde->bhce', w_i, S)
        o = jnp.einsum('bhcd,bhde->bhce', qg_i, S) + jnp.einsum('bhij,bhje->bhie', qk_i, v_new)
        S = S * gl_i[..., None, None] + jnp.einsum('bhcd,bhce->bhde', kt_i, v_new)
        return S, o

    xs = tuple(jnp.moveaxis(t, 2, 0) for t in (q_g, k_tail, u, w, qk, g_last))
    S, o = lax.scan(step, S0.astype(f32), xs)
    o = jnp.moveaxis(o, 0, 2).transpose(0, 2, 3, 1, 4).reshape(N, n * c, H, dv)[:, :L]
    return o, S


def gdn_conformer_mixer(h, p, dn_S, dn_conv, cc_conv):
    N, L, _ = h.shape
    proj = h @ p['ab_w_in']
    qkv_raw, z, b_raw, a_raw, glu_in = jnp.split(
        proj, [3 * GDN_WIDTH, 4 * GDN_WIDTH, 4 * GDN_WIDTH + GDN_HEADS, 4 * GDN_WIDTH + 2 * GDN_HEADS], axis=-1)
    qkv, new_dn_conv = causal_depthwise_conv(qkv_raw, dn_conv, p['dn_conv_w'])
    qkv = jax.nn.silu(qkv)
    q, k, v = jnp.split(qkv, 3, axis=-1)
    q = l2_normalize(q.reshape(N, L, GDN_HEADS, GDN_DK)) * (GDN_DK ** -0.5)
    k = l2_normalize(k.reshape(N, L, GDN_HEADS, GDN_DK))
    v = v.reshape(N, L, GDN_HEADS, GDN_DV)
    beta = jax.nn.sigmoid(b_raw.astype(jnp.float32))
    g = -jnp.exp(p['dn_A_log'].astype(jnp.float32)) * jax.nn.softplus(
        a_raw.astype(jnp.float32) + p['dn_dt_bias'].astype(jnp.float32))
    o, new_S = gated_delta_rule(q, k, v, g, beta, dn_S)
    o = rms_norm(o, p['dn_norm_g']) * jax.nn.silu(z.reshape(N, L, GDN_HEADS, GDN_DV).astype(jnp.float32))
    o = o.reshape(N, L, GDN_WIDTH).astype(h.dtype)
    ga, gb = jnp.split(glu_in, 2, axis=-1)
    glu = ga * jax.nn.sigmoid(gb)
    c, new_cc_conv = causal_depthwise_conv(glu, cc_conv, p['cc_conv_w'])
    c = jax.nn.silu(layer_norm(c + p['cc_conv_b'], p['cc_ln_g'], p['cc_ln_b']))
    y = jnp.concatenate([o, c], axis=-1) @ p['ab_w_out']
    return y, new_S.astype(dn_S.dtype), new_dn_conv, new_cc_conv


def chunk_spatial_mix(v, w_s, b_s):
    N, L, C = v.shape
    pad = (-L) % GM_CHUNK
    vp = jnp.pad(v, ((0, 0), (0, pad), (0, 0)))
    n = (L + pad) // GM_CHUNK
    vg = vp.reshape(N, n, GM_CHUNK, GM_GROUPS, C // GM_GROUPS)
    w = jnp.where(jnp.tril(jnp.ones((GM_CHUNK, GM_CHUNK), bool)), w_s, 0.0).astype(v.dtype)
    f = jnp.einsum('gij,bnjgc->bnigc', w, vg) + jnp.transpose(b_s)[:, :, None]
    return f.reshape(N, n * GM_CHUNK, C)[:, :L]


def chunk_mlp_mixer(h, p):
    pr = jax.nn.gelu(h @ p['gm_w_in'])
    u, v = jnp.split(pr, 2, axis=-1)
    v = layer_norm(v, p['gm_ln_g'], p['gm_ln_b'])
    f = chunk_spatial_mix(v, p['gm_w_s'], p['gm_b_s'])
    return (u * f) @ p['gm_w_out'], v


def cross_attend(h, mk, mv, wq, wo):
    N, L, _ = h.shape
    q = (h @ wq).reshape(N, L, XA_HEADS, XA_HEAD_DIM)
    s = jnp.einsum('blhd,bmhd->bhlm', q, mk.astype(h.dtype)).astype(jnp.float32) * (XA_HEAD_DIM ** -0.5)
    pr = jax.nn.softmax(s, axis=-1).astype(h.dtype)
    o = jnp.einsum('bhlm,bmhd->blhd', pr, mv.astype(h.dtype)).reshape(N, L, D_MODEL)
    return o @ wo


def trunk(x, mem_k, mem_v, dn_S, dn_conv, cc_conv, p):
    gm_v = None
    for l in range(DEPTH):
        x = layer_norm(DN_ALPHA * x + 0.5 * swiglu(x, p['ffn_w_gate'][l, 0], p['ffn_w_up'][l, 0], p['ffn_w_down'][l, 0]),
                       p['ln_g'][l, 0], p['ln_b'][l, 0])
        if l % 2 == 0:
            mix, dn_S, dn_conv, cc_conv = gdn_conformer_mixer(x, p, dn_S, dn_conv, cc_conv)
        else:
            mix, gm_v = chunk_mlp_mixer(x, p)
        x = layer_norm(DN_ALPHA * x + mix, p['ln_g'][l, 1], p['ln_b'][l, 1])
        x = layer_norm(DN_ALPHA * x + cross_attend(x, mem_k[l], mem_v[l], p['xa_wq'][l], p['xa_wo'][l]),
                       p['ln_g'][l, 2], p['ln_b'][l, 2])
        x = layer_norm(DN_ALPHA * x + 0.5 * swiglu(x, p['ffn_w_gate'][l, 1], p['ffn_w_up'][l, 1], p['ffn_w_down'][l, 1]),
                       p['ln_g'][l, 3], p['ln_b'][l, 3])
    return x, dn_S, dn_conv, cc_conv, gm_v


def setup_inputs(seed: int = 0) -> dict:
    key = jax.random.key(seed)
    ks = iter(jax.random.split(key, 48))
    nrm = lambda shape, scale: scale * jax.random.normal(next(ks), shape, jnp.float32)
    dense = lambda shape, s=1.0: nrm(shape, s * shape[-2] ** -0.5)
    dt = jnp.exp(jax.random.uniform(next(ks), (GDN_HEADS,), jnp.float32, math.log(1e-3), math.log(1e-1)))
    inp = {
        'x_prompt': nrm((BATCH, SEQ, D_MODEL), 1.0),
        'x_sample': nrm((DEC_BATCH, DEC_SEQ, D_MODEL), 1.0),
        'mem_prompt': nrm((BATCH, N_MEM, D_MODEL), 1.0),
        'cache_mem_k': nrm((DEPTH, DEC_BATCH, N_MEM, XA_HEADS, XA_HEAD_DIM), 1.0),
        'cache_mem_v': nrm((DEPTH, DEC_BATCH, N_MEM, XA_HEADS, XA_HEAD_DIM), 1.0),
        'state_dn_S': nrm((DEC_BATCH, GDN_HEADS, GDN_DK, GDN_DV), 0.1),
        'state_dn_conv': nrm((DEC_BATCH, SCONV_W - 1, 3 * GDN_WIDTH), 1.0),
        'state_cc_conv': nrm((DEC_BATCH, CC_W - 1, CC_CH), 0.5),
        'ln_g': 1.0 + nrm((DEPTH, 4, D_MODEL), 0.02),
        'ln_b': nrm((DEPTH, 4, D_MODEL), 0.02),
        'ffn_w_gate': dense((DEPTH, 2, D_MODEL, D_FF)),
        'ffn_w_up': dense((DEPTH, 2, D_MODEL, D_FF)),
        'ffn_w_down': dense((DEPTH, 2, D_FF, D_MODEL), DN_BETA),
        'xa_wq': dense((DEPTH, D_MODEL, D_MODEL)),
        'xa_wk': dense((DEPTH, D_MODEL, D_MODEL)),
        'xa_wv': dense((DEPTH, D_MODEL, D_MODEL)),
        'xa_wo': dense((DEPTH, D_MODEL, D_MODEL), DN_BETA),
        'ab_w_in': dense((D_MODEL, AB_IN)),
        'dn_conv_w': nrm((SCONV_W, 3 * GDN_WIDTH), SCONV_W ** -0.5),
        'dn_A_log': jnp.log(jax.random.uniform(next(ks), (GDN_HEADS,), jnp.float32, 1.0, 16.0)),
        'dn_dt_bias': dt + jnp.log(-jnp.expm1(-dt)),
        'dn_norm_g': 1.0 + nrm((GDN_DV,), 0.02),
        'cc_conv_w': nrm((CC_W, CC_CH), CC_W ** -0.5),
        'cc_conv_b': nrm((CC_CH,), 0.02),
        'cc_ln_g': 1.0 + nrm((CC_CH,), 0.02),
        'cc_ln_b': nrm((CC_CH,), 0.02),
        'ab_w_out': dense((GDN_WIDTH + CC_CH, D_MODEL), DN_BETA),
        'gm_w_in': dense((D_MODEL, 2 * GM_WIDTH)),
        'gm_ln_g': 1.0 + nrm((GM_WIDTH,), 0.02),
        'gm_ln_b': nrm((GM_WIDTH,), 0.02),
        'gm_w_s': nrm((GM_GROUPS, GM_CHUNK, GM_CHUNK), GM_CHUNK ** -0.5),
        'gm_b_s': 1.0 + nrm((GM_GROUPS, GM_CHUNK), 0.02),
        'gm_w_out': dense((GM_WIDTH, D_MODEL), DN_BETA),
    }
    return inp


def reference(x_prompt, x_sample, mem_prompt, cache_mem_k, cache_mem_v, state_dn_S, state_dn_conv, state_cc_conv,
              ln_g, ln_b, ffn_w_gate, ffn_w_up, ffn_w_down, xa_wq, xa_wk, xa_wv, xa_wo,
              ab_w_in, dn_conv_w, dn_A_log, dn_dt_bias, dn_norm_g, cc_conv_w, cc_conv_b, cc_ln_g, cc_ln_b, ab_w_out,
              gm_w_in, gm_ln_g, gm_ln_b, gm_w_s, gm_b_s, gm_w_out):
    p = dict(ln_g=ln_g, ln_b=ln_b, ffn_w_gate=ffn_w_gate, ffn_w_up=ffn_w_up, ffn_w_down=ffn_w_down,
             xa_wq=xa_wq, xa_wo=xa_wo, ab_w_in=ab_w_in, dn_conv_w=dn_conv_w, dn_A_log=dn_A_log,
             dn_dt_bias=dn_dt_bias, dn_norm_g=dn_norm_g, cc_conv_w=cc_conv_w, cc_conv_b=cc_conv_b,
             cc_ln_g=cc_ln_g, cc_ln_b=cc_ln_b, ab_w_out=ab_w_out, gm_w_in=gm_w_in, gm_ln_g=gm_ln_g,
             gm_ln_b=gm_ln_b, gm_w_s=gm_w_s, gm_b_s=gm_b_s, gm_w_out=gm_w_out)
    B = x_prompt.shape[0]
    dt = x_prompt.dtype
    mem_k_prompt = jnp.einsum('bmd,lde->lbme', mem_prompt, xa_wk).reshape(DEPTH, B, N_MEM, XA_HEADS, XA_HEAD_DIM)
    mem_v_prompt = jnp.einsum('bmd,lde->lbme', mem_prompt, xa_wv).reshape(DEPTH, B, N_MEM, XA_HEADS, XA_HEAD_DIM)
    y_prompt, dn_S_prompt, dn_conv_prompt, cc_conv_prompt, _ = trunk(
        x_prompt, mem_k_prompt, mem_v_prompt,
        jnp.zeros((B, GDN_HEADS, GDN_DK, GDN_DV), dt),
        jnp.zeros((B, SCONV_W - 1, 3 * GDN_WIDTH), dt),
        jnp.zeros((B, CC_W - 1, CC_CH), dt), p)
    y_sample, dn_S_sample, dn_conv_sample, cc_conv_sample, gm_v_sample = trunk(
        x_sample, cache_mem_k, cache_mem_v, state_dn_S, state_dn_conv, state_cc_conv, p)
    return (y_prompt, y_sample, mem_k_prompt, mem_v_prompt, dn_S_prompt, dn_conv_prompt, cc_conv_prompt,
            dn_S_sample, dn_conv_sample, cc_conv_sample, gm_v_sample)
```

```python
import numpy as np
from contextlib import ExitStack
import concourse.bass as bass
import concourse.mybir as mybir
from concourse.bass_utils import run_bass_kernel_spmd

F32 = mybir.dt.float32
BF16 = mybir.dt.bfloat16
AF = mybir.ActivationFunctionType
ALU = mybir.AluOpType
AX = mybir.AxisListType

PE, ACT, DVE, POOL, SP = "pe", "act", "dve", "pool", "sp"
ENGS = (PE, ACT, DVE, POOL, SP)

D = 1024
SEQ = 2048
NT = 1024
NS = 16
NC = NT + NS
DFF = 2816
NJ = 22
NMEM = 256
ALPHA = float(4 ** 0.25)
LN_EPS = 1e-5


class Res:
    __slots__ = ("name", "lw", "rd", "dsem", "dcnt", "drd", "pre")

    def __init__(self, name):
        self.pre = None
        self.name = name
        self.lw = None
        self.rd = {}
        self.dsem = None
        self.dcnt = 0
        self.drd = 0


class Op:
    __slots__ = ("eng", "fn", "waits", "signal", "dres")

    def __init__(self, eng, fn, waits, dres=None):
        self.eng = eng
        self.fn = fn
        self.waits = waits
        self.signal = False
        self.dres = dres


class Prog:
    def __init__(self, nc):
        self.nc = nc
        self.streams = {e: [] for e in ENGS}
        self.dma_res = []

    def _deps(self, eng, r, w):
        waits = []
        for res in r:
            if res.pre:
                waits.extend(res.pre)
            lw = res.lw
            if lw is not None:
                if lw[0] == "e":
                    if not (lw[1] == eng and eng == PE):
                        waits.append(lw)
                else:
                    waits.append(lw)
        for res in w:
            if res.pre:
                waits.extend(res.pre)
                res.pre = None
            lw = res.lw
            if lw is not None:
                if lw[0] == "e":
                    if lw[1] != eng:
                        waits.append(lw)
                else:
                    waits.append(lw)
            for e2, i2 in res.rd.items():
                if e2 != eng:
                    waits.append(("e", e2, i2))
            if res.drd:
                waits.append(("d", res, res.drd))
        return waits

    def op(self, eng, fn, r=(), w=()):
        idx = len(self.streams[eng])
        waits = self._deps(eng, r, w)
        for res in r:
            res.rd[eng] = idx
        for res in w:
            res.lw = ("e", eng, idx)
            res.rd = {}
            res.drd = 0
        self.streams[eng].append(Op(eng, fn, waits))

    def _dsem(self, res):
        if res.dsem is None:
            res.dsem = self.nc.alloc_semaphore("d_" + res.name)
            self.dma_res.append(res)
        return res.dsem

    def load(self, eng, out, in_, res, **kw):
        self._dsem(res)
        waits = self._deps(eng, (), (res,))
        res.dcnt += 16
        res.lw = ("d", res, res.dcnt)
        res.rd = {}
        res.drd = 0
        self.streams[eng].append(
            Op(eng, lambda e: e.dma_start(out=out, in_=in_, **kw), waits, dres=res))

    def store(self, eng, out, in_, res, **kw):
        self._dsem(res)
        waits = self._deps(eng, (res,), ())
        res.dcnt += 16
        res.drd = res.dcnt
        self.streams[eng].append(
            Op(eng, lambda e: e.dma_start(out=out, in_=in_, **kw), waits, dres=res))

    def d2d(self, eng, out, in_, res, **kw):
        self._dsem(res)
        res.dcnt += 16
        res.drd = res.dcnt
        self.streams[eng].append(
            Op(eng, lambda e: e.dma_start(out=out, in_=in_, **kw), [], dres=res))

    def emit(self, final_eng=SP):
        nc = self.nc
        fin = [("d", res, res.dcnt) for res in self.dma_res]
        self.streams[final_eng].append(Op(final_eng, None, fin))
        for e in ENGS:
            for op in self.streams[e]:
                for wt in op.waits:
                    if wt[0] == "e":
                        self.streams[wt[1]][wt[2]].signal = True
        cnt = {}
        for e in ENGS:
            c = 0
            arr = []
            for op in self.streams[e]:
                if op.signal:
                    c += 1
                arr.append(c)
            cnt[e] = arr
        sems = {e: nc.alloc_semaphore("s_" + e) for e in ENGS}
        stats = {e: [0, 0, (cnt[e][-1] if cnt[e] else 0)] for e in ENGS}
        stats['maxdma'] = max([r.dcnt for r in self.dma_res] + [0])

        def run(eng_name, engine):
            waited_e = {}
            waited_d = {}
            for op in self.streams[eng_name]:
                need_e = {}
                need_d = {}
                for wt in op.waits:
                    if wt[0] == "e":
                        c = cnt[wt[1]][wt[2]]
                        if c > need_e.get(wt[1], 0):
                            need_e[wt[1]] = c
                    else:
                        if wt[2] > need_d.get(wt[1], 0):
                            need_d[wt[1]] = wt[2]
                for e2, c in need_e.items():
                    if waited_e.get(e2, 0) < c:
                        engine.wait_ge(sems[e2], c)
                        waited_e[e2] = c
                        stats[eng_name][1] += 1
                for res, c in need_d.items():
                    if waited_d.get(res, 0) < c:
                        engine.wait_ge(res.dsem, c)
                        waited_d[res] = c
                        stats[eng_name][1] += 1
                if op.fn is None:
                    continue
                ins = op.fn(engine)
                stats[eng_name][0] += 1
                if op.signal:
                    ins.then_inc(sems[eng_name], 1)
                if op.dres is not None:
                    ins.then_inc(op.dres.dsem, 16)

        with nc.Block() as blk:
            @blk.tensor
            def _(e):
                run(PE, e)

            @blk.scalar
            def _(e):
                run(ACT, e)

            @blk.vector
            def _(e):
                run(DVE, e)

            @blk.gpsimd
            def _(e):
                run(POOL, e)

            @blk.sync
            def _(e):
                run(SP, e)
        return stats


class KB:
    def __init__(self, cfg):
        self.cfg = cfg
        self.nc = bass.Bass("TRN2", target_bir_lowering=False)
        self.P = Prog(self.nc)
        self.es = ExitStack()
        self.nbank = 0
        self.reserved = set()
        self.uid = 0

    def sb(self, name, shape, dt):
        return self.es.enter_context(self.nc.sbuf_tensor(name, list(shape), dt))

    def din(self, name, shape):
        return self.nc.dram_tensor(name, list(shape), F32, kind="ExternalInput").ap()

    def dout(self, name, shape):
        return self.nc.dram_tensor(name, list(shape), F32, kind="ExternalOutput").ap()

    def R(self, name):
        self.uid += 1
        return Res(f"{name}{self.uid}")

    def bank(self, reserve=False):
        while True:
            i = self.nbank % 8
            self.nbank += 1
            if i not in self.reserved:
                break
        if reserve:
            self.reserved.add(i)
        return self.banks[i], self.bres[i]

    def xfr(self, kc, c0):
        return self.Rxf[kc][min(c0 // 512, 2)]

    def xbr(self, kc, c0):
        return self.Rxb[kc][min(c0 // 512, 2)]

    def release(self, bk):
        self.reserved.discard(self.banks.index(bk))

    def declare_io(self):
        d = {}
        d["x_prompt"] = self.din("x_prompt", [SEQ, D])
        d["x_sample"] = self.din("x_sample", [NS, D])
        d["mem_prompt"] = self.din("mem_prompt", [NMEM, D])
        d["cache_mem_k"] = self.din("cache_mem_k", [2, NS, NMEM, D])
        d["cache_mem_v"] = self.din("cache_mem_v", [2, NS, NMEM, D])
        d["state_dn_S"] = self.din("state_dn_S", [NS, 4, 128, 128])
        d["state_dn_conv"] = self.din("state_dn_conv", [NS, 3, 1536])
        d["state_cc_conv"] = self.din("state_cc_conv", [NS, 30, 512])
        d["ln_g"] = self.din("ln_g", [8, D])
        d["ln_b"] = self.din("ln_b", [8, D])
        d["ffn_w_gate"] = self.din("ffn_w_gate", [4, D, DFF])
        d["ffn_w_up"] = self.din("ffn_w_up", [4, D, DFF])
        d["ffn_w_down"] = self.din("ffn_w_down", [4, DFF, D])
        for n in ("xa_wq", "xa_wk", "xa_wv", "xa_wo"):
            d[n] = self.din(n, [2, D, D])
        d["ab_w_in"] = self.din("ab_w_in", [D, 3080])
        d["dn_conv_w"] = self.din("dn_conv_w", [4, 1536])
        d["dn_A_log"] = self.din("dn_A_log", [1, 4])
        d["dn_dt_bias"] = self.din("dn_dt_bias", [1, 4])
        d["dn_norm_g"] = self.din("dn_norm_g", [1, 128])
        d["cc_conv_w"] = self.din("cc_conv_w", [31, 512])
        d["cc_conv_b"] = self.din("cc_conv_b", [1, 512])
        d["cc_ln_g"] = self.din("cc_ln_g", [1, 512])
        d["cc_ln_b"] = self.din("cc_ln_b", [1, 512])
        d["ab_w_out"] = self.din("ab_w_out", [D, D])
        d["gm_w_in"] = self.din("gm_w_in", [D, 2048])
        d["gm_ln_g"] = self.din("gm_ln_g", [1, D])
        d["gm_ln_b"] = self.din("gm_ln_b", [1, D])
        d["gm_w_s"] = self.din("gm_w_s", [4, 128, 128])
        d["gm_b_s"] = self.din("gm_b_s", [4, 128])
        d["gm_w_out"] = self.din("gm_w_out", [D, D])
        o = {}
        o["y_prompt"] = self.dout("y_prompt", [SEQ, D])
        o["y_sample"] = self.dout("y_sample", [NS, D])
        o["mem_k"] = self.dout("mem_k", [2, NMEM, D])
        o["mem_v"] = self.dout("mem_v", [2, NMEM, D])
        o["dn_S_prompt"] = self.dout("dn_S_prompt", [4, 128, 128])
        o["dn_conv_prompt"] = self.dout("dn_conv_prompt", [3, 1536])
        o["cc_conv_prompt"] = self.dout("cc_conv_prompt", [30, 512])
        o["dn_S_sample"] = self.dout("dn_S_sample", [NS, 4, 128, 128])
        o["dn_conv_sample"] = self.dout("dn_conv_sample", [NS, 3, 1536])
        o["cc_conv_sample"] = self.dout("cc_conv_sample", [NS, 30, 512])
        o["gm_v_sample"] = self.dout("gm_v_sample", [NS, D])
        self.d = d
        self.o = o

    def alloc_common(self):
        nc = self.nc
        self.banks = [self.es.enter_context(nc.psum_tensor(f"bank{i}", [128, 512], F32)) for i in range(8)]
        self.bres = [Res(f"bank{i}") for i in range(8)]
        self.xf = self.sb("xf", [128, 8, NC], F32)
        self.xb = self.sb("xb", [128, 8, NC], BF16)
        self.Rxf = [[Res(f"xf{c}_{si}") for si in range(3)] for c in range(8)]
        self.Rxb = [[Res(f"xb{c}_{si}") for si in range(3)] for c in range(8)]
        self.NA = 5
        self.ringA = [self.sb(f"wa{i}", [128, 8, 256], BF16) for i in range(self.NA)]
        self.RA = [Res(f"wa{i}") for i in range(self.NA)]
        self.iA = 0
        self.NB = 2
        self.ringB = [self.sb(f"wb{i}", [128, NJ, 128], BF16) for i in range(self.NB)]
        self.RB = [Res(f"wb{i}") for i in range(self.NB)]
        self.iB = 0
        self.SCR = 96 * 1024
        self.scr = self.sb("scr", [128, self.SCR // 4], F32)
        self.ident_f = self.sb("ident_f", [128, 128], F32)
        self.ident_b = self.sb("ident_b", [128, 128], BF16)
        self.onesD_b = self.sb("onesD_b", [128, 128], BF16)
        self.ones_b = self.sb("ones_b", [128, 128], BF16)
        self.ones_f = self.sb("ones_f", [128, 128], F32)
        self.Rconst = Res("const")
        self.lng = self.sb("lng", [128, 8, 8], F32)
        self.lnb = self.sb("lnb", [128, 8, 8], F32)
        self.KT = [self.sb(f"KT{l}", [128, 8, NMEM], BF16) for l in range(2)]
        self.Vb = [self.sb(f"Vb{l}", [128, 2, D], BF16) for l in range(2)]
        self.RKT = [Res("KT0"), Res("KT1")]
        self.RVb = [Res("Vb0"), Res("Vb1")]
        self.phase_res = []
        self.pending_pre = []
        self.ones512_b = self.sb("ones512_b", [128, 128], BF16)
        self.cc_halo = self.sb("cc_halo", [128, 4, 30], BF16)
        self.Rcch = Res("cc_halo")
        self.Rd2d = Res("d2d")
        self.Rbar = Res("barrier")
        self.Sf = self.sb("Sf", [128, 4, 128], F32)
        self.Sb = self.sb("Sb", [128, 4, 128], BF16)
        self.RS = [Res(f"S{h}") for h in range(4)]
        self.RSb = [Res(f"Sb{h}") for h in range(4)]
        self.dn_halo = self.sb("dn_halo", [128, 12, 3], BF16)
        self.Rdnh = Res("dn_halo")
        self.mask_c = self.sb("mask_c", [128, 128], F32)
        self.mask_s = self.sb("mask_s", [128, 128], F32)
        self.sel127 = self.sb("sel127", [128, 128], F32)
        self.stage = self.sb("stage", [128, 2, D], F32)
        self.Rstage = [Res("stage0"), Res("stage1")]
        self.istage = 0

    def scr_view(self, off_bytes, shape, dt):
        esz = 4 if dt == F32 else 2
        n = 1
        for s in shape[1:]:
            n *= s
        assert off_bytes % 4 == 0 and off_bytes + n * esz <= self.SCR, (off_bytes, shape)
        flat = self.scr[0:shape[0], off_bytes // 4: off_bytes // 4 + (n * esz + 3) // 4]
        if dt != F32:
            flat = flat.bitcast(dt)[:, 0:n]
        if len(shape) == 2:
            return flat
        names = " ".join(f"a{i}" for i in range(len(shape) - 1))
        kw = {f"a{i}": shape[i + 1] for i in range(len(shape) - 1)}
        return flat.rearrange(f"p ({names}) -> p {names}", **kw)

    def build_consts(self):
        P = self.P
        Rc = self.Rconst
        idf, idb = self.ident_f, self.ident_b
        P.op(POOL, lambda e: e.memset(idf[:], 1.0), w=[Rc])
        P.op(POOL, lambda e: e.affine_select(out=idf[:], in_=idf[:], pattern=[[1, 128]], compare_op=ALU.is_equal,
                                            fill=0.0, base=0, channel_multiplier=-1), r=[Rc], w=[Rc])
        P.op(POOL, lambda e: e.tensor_copy(out=idb[:], in_=idf[:]), r=[Rc], w=[Rc])
        P.op(POOL, lambda e: e.memset(self.onesD_b[:], 1.0 / D), w=[Rc])
        P.op(POOL, lambda e: e.memset(self.ones_b[:], 1.0), w=[Rc])
        P.op(POOL, lambda e: e.memset(self.ones_f[:], 1.0), w=[Rc])
        P.op(POOL, lambda e: e.memset(self.ones512_b[:], 1.0 / 512), w=[Rc])
        P.op(POOL, lambda e: e.memset(self.cc_halo[:], 0.0), w=[self.Rcch])
        P.op(POOL, lambda e: e.memset(self.dn_halo[:], 0.0), w=[self.Rdnh])
        P.op(POOL, lambda e: e.memset(self.Sf[:], 0.0), w=self.RS)
        P.op(POOL, lambda e: e.memset(self.Sb[:], 0.0), w=self.RSb)
        for tile_, base_ in ((self.mask_c, 0), (self.mask_s, -1)):
            P.op(POOL, lambda e, tile_=tile_: e.memset(tile_[:], 1.0), w=[Rc])
            P.op(POOL, lambda e, tile_=tile_, base_=base_: e.affine_select(
                out=tile_[:], in_=tile_[:], pattern=[[1, 128]], compare_op=ALU.is_ge, fill=0.0, base=base_,
                channel_multiplier=-1), r=[Rc], w=[Rc])
        P.op(POOL, lambda e: e.memset(self.sel127[:], 1.0), w=[Rc])
        P.op(POOL, lambda e: e.affine_select(out=self.sel127[:], in_=self.sel127[:], pattern=[[0, 128]], compare_op=ALU.is_ge,
                                            fill=0.0, base=-127, channel_multiplier=1), r=[Rc], w=[Rc])
        self.fm_load(self.d["ln_g"], 8, D, self.lng, Rc)
        self.fm_load(self.d["ln_b"], 8, D, self.lnb, Rc)

    def fm_load(self, rows_ap, r, n, dest, rdest):
        P = self.P
        st = self.stage
        i = self.istage % 2
        self.istage += 1
        Rs = self.Rstage[i]
        nchunk = n // 128
        done = 0
        while done < n:
            w = min(D, n - done)
            P.load(SP, st[0:r, i, 0:w], rows_ap[:, done:done + w], Rs)
            for c0 in range(0, w // 128, 4):
                nb = min(4, w // 128 - c0)
                bk, rb = self.bank()
                for j in range(nb):
                    P.op(PE, lambda e, bk=bk, j=j, c0=c0, i=i: e.transpose(
                        bk[:, j * 128: j * 128 + r], st[0:r, i, (c0 + j) * 128:(c0 + j + 1) * 128], self.ident_f[0:r, 0:r]),
                        r=[Rs, self.Rconst], w=[rb])
                gc = done // 128 + c0
                P.op(DVE, lambda e, bk=bk, nb=nb, gc=gc: e.tensor_copy(
                    out=dest[:, gc:gc + nb, 0:r],
                    in_=bk[:, 0:nb * 128].rearrange("p (j t) -> p j t", j=nb)[:, :, 0:r]), r=[rb], w=[rb, rdest])
            done += w

    def wloadA(self, w2d, col0, ncols):
        i = self.iA % self.NA
        self.iA += 1
        slot, res = self.ringA[i], self.RA[i]
        self.P.load(POOL, slot[:, :, 0:ncols], w2d.rearrange("(kc p) n -> p kc n", p=128)[:, :, col0:col0 + ncols], res)
        return slot, res

    def wloadB(self, w2d, col0):
        i = self.iB % self.NB
        self.iB += 1
        slot, res = self.ringB[i], self.RB[i]
        self.P.load(POOL, slot[:], w2d.rearrange("(jc p) n -> p jc n", p=128)[:, :, col0:col0 + 128], res)
        return slot, res

    def subtiles(self, t):
        s = [(0, 512), (512, 512)]
        if t == 1:
            s.append((NT, NS))
        return s

    def load_x(self, t):
        P = self.P
        st = self.stage
        for tb in range(NT // 128):
            i = self.istage % 2
            self.istage += 1
            Rs = self.Rstage[i]
            r0 = t * NT + tb * 128
            P.load(SP, st[:, i, :], self.d["x_prompt"][r0:r0 + 128, :], Rs)
            self._tr_in(st[:, i, :], 128, tb * 128, Rs)
        if t == 1:
            i = self.istage % 2
            self.istage += 1
            Rs = self.Rstage[i]
            P.load(SP, st[0:NS, i, :], self.d["x_sample"][:, :], Rs)
            self._tr_in(st[0:NS, i, :], NS, NT, Rs)

    def _tr_in(self, src, r, col0, Rs):
        P = self.P
        for g in range(2):
            bk, rb = self.bank()
            for j in range(4):
                kc = g * 4 + j
                P.op(PE, lambda e, bk=bk, j=j, kc=kc: e.transpose(
                    bk[:, j * 128: j * 128 + r], src[:, kc * 128:(kc + 1) * 128], self.ident_f[0:r, 0:r]),
                    r=[Rs, self.Rconst], w=[rb])
            pv = bk[:].rearrange("p (j t) -> p j t", j=4)[:, :, 0:r]
            wr = [self.xfr(g * 4 + j, col0) for j in range(4)]
            wb = [self.xbr(g * 4 + j, col0) for j in range(4)]
            P.op(DVE, lambda e, pv=pv, g=g: e.tensor_copy(out=self.xf[:, g * 4:(g + 1) * 4, col0:col0 + r], in_=pv),
                 r=[rb], w=[rb] + wr)
            P.op(ACT, lambda e, pv=pv, g=g: e.copy(out=self.xb[:, g * 4:(g + 1) * 4, col0:col0 + r], in_=pv),
                 r=[rb], w=[rb] + wb)

    def store_y(self, t):
        P = self.P
        st = self.stage
        blocks = [(tb * 128, 128, self.o["y_prompt"][t * NT + tb * 128: t * NT + (tb + 1) * 128, :]) for tb in range(NT // 128)]
        if t == 1:
            blocks.append((NT, NS, self.o["y_sample"][:, :]))
        for col0, r, dst in blocks:
            i = self.istage % 2
            self.istage += 1
            Rs = self.Rstage[i]
            for g in range(2):
                bk, rb = self.bank()
                for j in range(4):
                    kc = g * 4 + j
                    P.op(PE, lambda e, bk=bk, j=j, kc=kc, col0=col0, r=r: e.transpose(
                        bk[0:r, j * 128:(j + 1) * 128], self.xf[:, kc, col0:col0 + r], self.ident_f[:, :]),
                        r=[self.xfr(kc, col0), self.Rconst], w=[rb])
                P.op(DVE if g == 0 else ACT, (lambda e, bk=bk, g=g, i=i, r=r: e.tensor_copy(out=st[0:r, i, g * 512:(g + 1) * 512], in_=bk[0:r, :]))
                     if g == 0 else (lambda e, bk=bk, g=g, i=i, r=r: e.copy(out=st[0:r, i, g * 512:(g + 1) * 512], in_=bk[0:r, :])),
                     r=[rb], w=[rb, Rs])
            P.store(SP, dst, st[0:r, i, :], Rs)

    def layer_norm(self, t, li, subs=None, parts=(1, 2)):
        P = self.P
        if subs is None:
            subs = self.subtiles(t)

        def finish(kc, c0, n, tt, Rt):
            P.op(ACT, lambda e: e.activation(out=self.xf[:, kc, c0:c0 + n], in_=tt, func=AF.Identity,
                                             scale=self.lng[:, kc, li:li + 1], bias=self.lnb[:, kc, li:li + 1]),
                 r=[Rt, self.Rconst], w=[self.xfr(kc, c0)])
            P.op(ACT, lambda e: e.activation(out=self.xb[:, kc, c0:c0 + n], in_=tt, func=AF.Identity,
                                             scale=self.lng[:, kc, li:li + 1], bias=self.lnb[:, kc, li:li + 1]),
                 r=[Rt, self.Rconst], w=[self.xbr(kc, c0)])

        def finish_small(c0, n, tall, Rts):
            gb = self.lng[:, :, li:li + 1].to_broadcast([128, 8, n])
            bb = self.lnb[:, :, li:li + 1].to_broadcast([128, 8, n])
            wx = [self.xfr(kc, c0) for kc in range(8)]
            wb = [self.xbr(kc, c0) for kc in range(8)]
            P.op(DVE, lambda e: e.tensor_tensor(out=tall, in0=tall, in1=gb, op=ALU.mult), r=Rts + [self.Rconst], w=Rts)
            P.op(DVE, lambda e: e.tensor_tensor(out=self.xf[:, :, c0:c0 + n], in0=tall, in1=bb, op=ALU.add), r=Rts + [self.Rconst], w=wx)
            P.op(ACT, lambda e: e.copy(out=self.xb[:, :, c0:c0 + n], in_=self.xf[:, :, c0:c0 + n]), r=wx, w=wb)

        self.ln_fm(8, lambda kc, c0, n: self.xf[:, kc, c0:c0 + n], self.xfr, self.onesD_b, subs, finish,
                   src_all=lambda c0, n: self.xf[:, :, c0:c0 + n], parts=parts, finish_small=finish_small)

    def ln_fm(self, nch, src, Rsrc, ones_mat, subs, finish, src_all=None, parts=(1, 2), finish_small=None):
        P = self.P
        zb = self.scr_view(self.ln_off, [128, 8, 512], BF16)
        zsq = self.scr_view(self.ln_off + 8192, [128, 8, 512], BF16)
        mean = self.scr_view(self.ln_off + 16384, [128, 512], F32)
        rstd = self.scr_view(self.ln_off + 18432, [128, 512], F32)
        tmp = self.scr_view(self.ln_off + 20480, [128, 2, 512], F32)
        Rzb, Rzsq, Rmean, Rrstd = self.Rln
        held = {}

        def p1(c0, n):
            allsrc = [Rsrc(kc, c0) for kc in range(nch)]
            if src_all is not None:
                P.op(DVE, lambda e: e.tensor_copy(out=zb[:, 0:nch, 0:n], in_=src_all(c0, n)), r=allsrc, w=[Rzb])
                P.op(ACT, lambda e: e.activation(out=zsq[:, 0:nch, 0:n], in_=src_all(c0, n), func=AF.Square), r=allsrc, w=[Rzsq])
            else:
                for kc in range(nch):
                    P.op(DVE, lambda e, kc=kc: e.tensor_copy(out=zb[:, kc, 0:n], in_=src(kc, c0, n)), r=[Rsrc(kc, c0)], w=[Rzb])
                    P.op(ACT, lambda e, kc=kc: e.activation(out=zsq[:, kc, 0:n], in_=src(kc, c0, n), func=AF.Square),
                         r=[Rsrc(kc, c0)], w=[Rzsq])

        def p2a(c0, n):
            bm, rbm = self.bank()
            bq, rbq = self.bank()
            for kc in range(nch):
                P.op(PE, lambda e, kc=kc: e.matmul(bm[:, 0:n], lhsT=ones_mat[:], rhs=zb[:, kc, 0:n],
                                                 start=(kc == 0), stop=(kc == nch - 1)), r=[Rzb, self.Rconst], w=[rbm])
            for kc in range(nch):
                P.op(PE, lambda e, kc=kc: e.matmul(bq[:, 0:n], lhsT=ones_mat[:], rhs=zsq[:, kc, 0:n],
                                                 start=(kc == 0), stop=(kc == nch - 1)), r=[Rzsq, self.Rconst], w=[rbq])
            held[c0] = (bm, rbm, bq, rbq)

        def p2b(c0, n):
            bm, rbm, bq, rbq = held.pop(c0)
            P.op(DVE, lambda e: e.tensor_copy(out=mean[:, 0:n], in_=bm[:, 0:n]), r=[rbm], w=[rbm, Rmean])
            P.op(DVE, lambda e: e.tensor_tensor(out=rstd[:, 0:n], in0=mean[:, 0:n], in1=mean[:, 0:n], op=ALU.mult), r=[Rmean], w=[Rrstd])
            P.op(DVE, lambda e: e.tensor_tensor(out=rstd[:, 0:n], in0=bq[:, 0:n], in1=rstd[:, 0:n], op=ALU.subtract),
                 r=[rbq, Rrstd], w=[rbq, Rrstd])
            P.op(DVE, lambda e: e.tensor_scalar(out=rstd[:, 0:n], in0=rstd[:, 0:n], scalar1=0.0, scalar2=LN_EPS,
                                                op0=ALU.max, op1=ALU.add), r=[Rrstd], w=[Rrstd])
            P.op(ACT, lambda e: e.activation(out=rstd[:, 0:n], in_=rstd[:, 0:n], func=AF.Ln), r=[Rrstd], w=[Rrstd])
            P.op(ACT, lambda e: e.activation(out=rstd[:, 0:n], in_=rstd[:, 0:n], func=AF.Exp, scale=-0.5), r=[Rrstd], w=[Rrstd])
            if finish_small is not None and n <= 64:
                allsrc = [Rsrc(kc, c0) for kc in range(nch)]
                tall = tmp[:].rearrange("p a b -> p (a b)")[:, 0:nch * n].rearrange("p (c k) -> p c k", c=nch)
                bc = lambda ap2: ap2.unsqueeze(1).to_broadcast([128, nch, n])
                P.op(DVE, lambda e: e.tensor_tensor(out=tall, in0=src_all(c0, n), in1=bc(mean[:, 0:n]), op=ALU.subtract),
                     r=allsrc + [Rmean], w=list(self.Rlnt))
                P.op(DVE, lambda e: e.tensor_tensor(out=tall, in0=tall, in1=bc(rstd[:, 0:n]), op=ALU.mult), r=list(self.Rlnt) + [Rrstd], w=list(self.Rlnt))
                finish_small(c0, n, tall, list(self.Rlnt))
                return
            for kc in range(nch):
                tt = tmp[:, kc % 2, 0:n]
                Rt = self.Rlnt[kc % 2]
                P.op(DVE, lambda e, kc=kc, tt=tt: e.tensor_tensor(out=tt, in0=src(kc, c0, n), in1=mean[:, 0:n], op=ALU.subtract),
                     r=[Rsrc(kc, c0), Rmean], w=[Rt])
                P.op(DVE, lambda e, tt=tt: e.tensor_tensor(out=tt, in0=tt, in1=rstd[:, 0:n], op=ALU.mult), r=[Rt, Rrstd], w=[Rt])
                finish(kc, c0, n, tt, Rt)

        if 1 in parts and 2 in parts:
            for (c0, n) in subs:
                p1(c0, n)
                p2a(c0, n)
            for (c0, n) in subs:
                p2b(c0, n)
        else:
            for (c0, n) in subs:
                if 1 in parts:
                    p1(c0, n)
                if 2 in parts:
                    p2a(c0, n)
                    p2b(c0, n)

    def residual_from_psum(self, kc, c0, n, bk, rb, coef):
        P = self.P
        if coef == 1.0:
            P.op(DVE, lambda e: e.scalar_tensor_tensor(out=self.xf[:, kc, c0:c0 + n], in0=self.xf[:, kc, c0:c0 + n], scalar=ALPHA,
                                                       in1=bk[:, 0:n], op0=ALU.mult, op1=ALU.add),
                 r=[rb, self.xfr(kc, c0)], w=[rb, self.xfr(kc, c0)])
        else:
            P.op(ACT, lambda e: e.mul(out=self.xf[:, kc, c0:c0 + n], in_=self.xf[:, kc, c0:c0 + n], mul=ALPHA),
                 r=[self.xfr(kc, c0)], w=[self.xfr(kc, c0)])
            P.op(DVE, lambda e: e.scalar_tensor_tensor(out=self.xf[:, kc, c0:c0 + n], in0=bk[:, 0:n], scalar=coef,
                                                       in1=self.xf[:, kc, c0:c0 + n], op0=ALU.mult, op1=ALU.add),
                 r=[rb, self.xfr(kc, c0)], w=[rb, self.xfr(kc, c0)])

    def out_proj_ln(self, t, w2d, srcT, Rsrc, li):
        P = self.P
        subs = self.subtiles(t)
        slots = [self.wloadA(w2d, g * 256, 256) for g in range(4)]
        prev = None
        for (c0, n) in subs:
            for g in range(4):
                slot, rs = slots[g]
                for jj in range(2):
                    m = 2 * g + jj
                    bk, rb = self.bank()
                    for kc in range(8):
                        P.op(PE, lambda e, bk=bk, kc=kc, jj=jj, c0=c0, n=n, slot=slot: e.matmul(
                            bk[:, 0:n], lhsT=slot[:, kc, jj * 128:(jj + 1) * 128], rhs=srcT[:, kc, c0:c0 + n],
                            start=(kc == 0), stop=(kc == 7)), r=[rs, Rsrc[kc]], w=[rb])
                    self.residual_from_psum(m, c0, n, bk, rb, 1.0)
            if prev is not None:
                self.layer_norm(t, li, subs=[prev], parts=(2,))
            self.layer_norm(t, li, subs=[(c0, n)], parts=(1,))
            prev = (c0, n)
        self.layer_norm(t, li, subs=[prev], parts=(2,))

    def ffn(self, t, l, i):
        P = self.P
        f = l * 2 + i
        wg, wu, wd = self.d["ffn_w_gate"][f], self.d["ffn_w_up"][f], self.d["ffn_w_down"][f]
        hid = self.scr_view(0, [128, NJ, NC], BF16)
        sg = self.scr_view(self.sg_off, [128, 2, 512], F32)
        Rhid = [self.PR(f"hid{j}") for j in range(NJ)]
        self.Rsg = [self.PR("sg0"), self.PR("sg1")]
        subs = self.subtiles(t)
        for g in range(NJ // 2):
            sg_slot, rg = self.wloadA(wg, g * 256, 256)
            su_slot, ru = self.wloadA(wu, g * 256, 256)
            for jj in range(2):
                j = 2 * g + jj
                for (c0, n) in subs:
                    bg, rbg = self.bank()
                    bu, rbu = self.bank()
                    for kc in range(8):
                        P.op(PE, lambda e, kc=kc, jj=jj, c0=c0, n=n, bg=bg, s=sg_slot: e.matmul(
                            bg[:, 0:n], lhsT=s[:, kc, jj * 128:(jj + 1) * 128], rhs=self.xb[:, kc, c0:c0 + n],
                            start=(kc == 0), stop=(kc == 7)), r=[rg, self.xbr(kc, c0)], w=[rbg])
                    for kc in range(8):
                        P.op(PE, lambda e, kc=kc, jj=jj, c0=c0, n=n, bu=bu, s=su_slot: e.matmul(
                            bu[:, 0:n], lhsT=s[:, kc, jj * 128:(jj + 1) * 128], rhs=self.xb[:, kc, c0:c0 + n],
                            start=(kc == 0), stop=(kc == 7)), r=[ru, self.xbr(kc, c0)], w=[rbu])
                    k = self.isg % 2
                    self.isg += 1
                    Rs = self.Rsg[k]
                    P.op(ACT, lambda e, bg=bg, n=n, k=k: e.activation(out=sg[:, k, 0:n], in_=bg[:, 0:n], func=AF.Silu),
                         r=[rbg], w=[rbg, Rs])
                    P.op(DVE, lambda e, bu=bu, n=n, k=k, j=j, c0=c0: e.scalar_tensor_tensor(
                        out=hid[:, j, c0:c0 + n], in0=bu[:, 0:n], scalar=0.5, in1=sg[:, k, 0:n], op0=ALU.mult, op1=ALU.mult),
                        r=[rbu, Rs], w=[rbu, Rhid[j]])
        li = l * 4 + (0 if i == 0 else 3)
        for m in range(8):
            sd, rd = self.wloadB(wd, m * 128)
            for (c0, n) in subs:
                bk, rb = self.bank()
                for j in range(NJ):
                    P.op(PE, lambda e, j=j, c0=c0, n=n, bk=bk, sd=sd: e.matmul(
                        bk[:, 0:n], lhsT=sd[:, j, :], rhs=hid[:, j, c0:c0 + n], start=(j == 0), stop=(j == NJ - 1)),
                        r=[rd, Rhid[j]], w=[rb])
                self.residual_from_psum(m, c0, n, bk, rb, 1.0)
        self.layer_norm(t, li)


    def phase_barrier(self, extra_old=(), extra_new=()):
        best_e, best_d = {}, {}

        def add(wt):
            if wt[0] == "e":
                if wt[2] > best_e.get(wt[1], -1):
                    best_e[wt[1]] = wt[2]
            else:
                if wt[2] > best_d.get(wt[1], 0):
                    best_d[wt[1]] = wt[2]

        for res in list(self.phase_res) + list(extra_old):
            for wt in (res.pre or ()):
                add(wt)
            if res.lw is not None:
                add(res.lw)
            for e2, i2 in res.rd.items():
                add(("e", e2, i2))
            if res.drd:
                add(("d", res, res.drd))
        H = [("e", e, i) for e, i in best_e.items()] + [("d", r, c) for r, c in best_d.items()]
        self.pending_pre = H
        for res in extra_new:
            res.pre = list(res.pre or []) + list(H)
        self.phase_res = []

    def PR(self, name):
        r = self.R(name)
        r.pre = list(self.pending_pre)
        self.phase_res.append(r)
        return r

    def mem_kv(self):
        P = self.P
        st = self.stage
        memT = self.scr_view(16384, [128, 8, NMEM], BF16)
        RmemT = self.PR("memT")
        kvst = [self.scr_view(0, [128, 2, D], F32), self.scr_view(8192, [128, 2, D], F32)]
        Rkvst = [self.PR("kst"), self.PR("vst")]
        for mb in range(2):
            i = self.istage % 2
            self.istage += 1
            Rs = self.Rstage[i]
            P.load(SP, st[:, i, :], self.d["mem_prompt"][mb * 128:(mb + 1) * 128, :], Rs)
            for g in range(2):
                bk, rb = self.bank()
                for j in range(4):
                    kc = g * 4 + j
                    P.op(PE, lambda e, bk=bk, j=j, kc=kc, i=i: e.transpose(
                        bk[:, j * 128:(j + 1) * 128], st[:, i, kc * 128:(kc + 1) * 128], self.ident_f[:]),
                        r=[Rs, self.Rconst], w=[rb])
                P.op(ACT, lambda e, bk=bk, g=g, mb=mb: e.copy(
                    out=memT[:, g * 4:(g + 1) * 4, mb * 128:(mb + 1) * 128], in_=bk[:].rearrange("p (j t) -> p j t", j=4)),
                    r=[rb], w=[rb, RmemT])
        for l in range(2):
            for which, wname in ((0, "xa_wk"), (1, "xa_wv")):
                w2d = self.d[wname][l]
                for g in range(4):
                    slot, rs = self.wloadA(w2d, g * 256, 256)
                    for mb in range(2):
                        bk, rb = self.bank()
                        for kc in range(8):
                            P.op(PE, lambda e, bk=bk, kc=kc, mb=mb, slot=slot: e.matmul(
                                bk[:, 0:256], lhsT=memT[:, kc, mb * 128:(mb + 1) * 128], rhs=slot[:, kc, 0:256],
                                start=(kc == 0), stop=(kc == 7)), r=[RmemT, rs], w=[rb])
                        P.op(DVE, lambda e, bk=bk, mb=mb, g=g, which=which: e.tensor_copy(
                            out=kvst[which][:, mb, g * 256:(g + 1) * 256], in_=bk[:, 0:256]), r=[rb], w=[rb, Rkvst[which]])
                        if which == 1:
                            P.op(ACT, lambda e, bk=bk, mb=mb, g=g, l=l: e.copy(
                                out=self.Vb[l][:, mb, g * 256:(g + 1) * 256], in_=bk[:, 0:256]), r=[rb], w=[rb, self.RVb[l]])
                    if which == 0:
                        for jj in range(2):
                            m = 2 * g + jj
                            bk, rb = self.bank()
                            for kc in range(8):
                                P.op(PE, lambda e, bk=bk, kc=kc, jj=jj, slot=slot: e.matmul(
                                    bk[:, 0:256], lhsT=slot[:, kc, jj * 128:(jj + 1) * 128], rhs=memT[:, kc, :],
                                    start=(kc == 0), stop=(kc == 7)), r=[RmemT, rs], w=[rb])
                            P.op(ACT, lambda e, bk=bk, m=m, l=l: e.copy(out=self.KT[l][:, m, :], in_=bk[:, 0:256]),
                                 r=[rb], w=[rb, self.RKT[l]])
                dst = self.o["mem_k" if which == 0 else "mem_v"][l].rearrange("(mb p) d -> p mb d", p=128)
                P.store(SP, dst, kvst[which][:], Rkvst[which])

    def xattn(self, t, l):
        P = self.P
        subs = self.subtiles(t)
        psubs = [(0, 512), (512, 512)]
        qT = self.scr_view(0, [128, 8, NC], BF16)
        oT = self.scr_view(16640, [128, 8, NC], BF16)
        ebuf = self.scr_view(33280, [128, 2, 2, 512], BF16)
        rden = self.scr_view(37376, [128, 2, 512], F32)
        RqT = [self.PR(f"qT{c}") for c in range(8)]
        RoT = [self.PR(f"oT{c}") for c in range(8)]
        Re = [[self.PR(f"e{k}{mb}") for mb in range(2)] for k in range(2)]
        Rrden = [self.PR("rden0"), self.PR("rden1")]
        wq, wo = self.d["xa_wq"][l], self.d["xa_wo"][l]
        if t == 1:
            qtok = self.scr_view(74 * 1024, [NS, D], BF16)
            Rqtok = self.PR("qtok")
        for g in range(4):
            slot, rs = self.wloadA(wq, g * 256, 256)
            for jj in range(2):
                m = 2 * g + jj
                for (c0, n) in subs:
                    bk, rb = self.bank()
                    for kc in range(8):
                        P.op(PE, lambda e, bk=bk, kc=kc, jj=jj, c0=c0, n=n, slot=slot: e.matmul(
                            bk[:, 0:n], lhsT=slot[:, kc, jj * 128:(jj + 1) * 128], rhs=self.xb[:, kc, c0:c0 + n],
                            start=(kc == 0), stop=(kc == 7)), r=[rs, self.xbr(kc, c0)], w=[rb])
                    P.op(ACT, lambda e, bk=bk, m=m, c0=c0, n=n: e.copy(out=qT[:, m, c0:c0 + n], in_=bk[:, 0:n]),
                         r=[rb], w=[rb, RqT[m]])
            if t == 1:
                bk, rb = self.bank()
                for kc in range(8):
                    P.op(PE, lambda e, bk=bk, kc=kc, slot=slot: e.matmul(
                        bk[0:NS, 0:256], lhsT=self.xb[:, kc, NT:NT + NS], rhs=slot[:, kc, 0:256],
                        start=(kc == 0), stop=(kc == 7)), r=[rs, self.xbr(kc, NT)], w=[rb])
                P.op(ACT, lambda e, bk=bk, g=g: e.copy(out=qtok[:, g * 256:(g + 1) * 256], in_=bk[0:NS, 0:256]),
                     r=[rb], w=[rb, Rqtok])
        KT, Vb = self.KT[l], self.Vb[l]
        RKT, RVb = self.RKT[l], self.RVb[l]
        it = 0
        sgen = self.xattn_samples(l, qtok, Rqtok, oT, RoT) if t == 1 else None

        def sample_steps(nsteps):
            if sgen is None:
                return
            for _ in range(nsteps):
                try:
                    next(sgen)
                except StopIteration:
                    return

        def att_a(h, c0, n, k):
            for mb in range(2):
                bk, rb = self.bank()
                for cc in range(2):
                    c = 2 * h + cc
                    P.op(PE, lambda e, bk=bk, c=c, cc=cc, mb=mb: e.matmul(
                        bk[:, 0:n], lhsT=KT[:, c, mb * 128:(mb + 1) * 128], rhs=qT[:, c, c0:c0 + n],
                        start=(cc == 0), stop=(cc == 1)), r=[RKT, RqT[c]], w=[rb])
                P.op(ACT, lambda e, bk=bk, mb=mb: e.activation(out=ebuf[:, k, mb, 0:n], in_=bk[:, 0:n], func=AF.Exp,
                                                             scale=1.0 / 16.0), r=[rb], w=[rb, Re[k][mb]])

        def att_b(h, c0, n, k):
            bk, rb = self.bank()
            for mb in range(2):
                P.op(PE, lambda e, bk=bk, mb=mb: e.matmul(bk[:, 0:n], lhsT=self.ones_b[:], rhs=ebuf[:, k, mb, 0:n],
                                                         start=(mb == 0), stop=(mb == 1)), r=[Re[k][mb], self.Rconst], w=[rb])
            P.op(ACT, lambda e, bk=bk: e.activation(out=rden[:, k, 0:n], in_=bk[:, 0:n], func=AF.Ln), r=[rb], w=[rb, Rrden[k]])
            P.op(ACT, lambda e: e.activation(out=rden[:, k, 0:n], in_=rden[:, k, 0:n], func=AF.Exp, scale=-1.0),
                 r=[Rrden[k]], w=[Rrden[k]])
            for dc in range(2):
                c = 2 * h + dc
                bk, rb = self.bank()
                for mb in range(2):
                    P.op(PE, lambda e, bk=bk, mb=mb, c=c: e.matmul(
                        bk[:, 0:n], lhsT=Vb[:, mb, c * 128:(c + 1) * 128], rhs=ebuf[:, k, mb, 0:n],
                        start=(mb == 0), stop=(mb == 1)), r=[Re[k][mb], RVb], w=[rb])
                P.op(DVE, lambda e, bk=bk, c=c: e.tensor_tensor(
                    out=oT[:, c, c0:c0 + n], in0=bk[:, 0:n], in1=rden[:, k, 0:n], op=ALU.mult),
                    r=[rb, Rrden[k]], w=[rb, RoT[c]])

        steps = [(h, c0, n) for h in range(4) for (c0, n) in psubs]
        att_a(*steps[0], 0)
        for i, (h, c0, n) in enumerate(steps):
            sample_steps(2)
            if i + 1 < len(steps):
                att_a(*steps[i + 1], (i + 1) % 2)
            att_b(h, c0, n, i % 2)
        sample_steps(10 ** 6)
        self.out_proj_ln(t, wo, oT, RoT, l * 4 + 2)

    def xattn_samples(self, l, qtok, Rqtok, oT, RoT):
        P = self.P
        hi = 74 * 1024
        ksb = self.scr_view(hi + 2048, [128, 2, 2, D], BF16)
        vsb = self.scr_view(hi + 10240, [128, 2, 2, D], BF16)
        prod = self.scr_view(hi + 18432, [128, 2, 512], F32)
        lo = 41472
        mask4 = self.scr_view(lo, [4, D], BF16)
        o4m = self.scr_view(lo + 2048, [4, D], BF16)
        ss = self.scr_view(lo + 4096, [128, 2, 4], F32)
        es = self.scr_view(lo + 4096 + 64, [128, 2, 4], BF16)
        rd = self.scr_view(lo + 4096 + 128, [128, 4], F32)
        pT = self.scr_view(lo + 4096 + 192, [128, 2, 4], BF16)
        sel = self.scr_view(70 * 1024, [NS, NS, 128], BF16)
        junk = self.scr_view(lo + 4608, [128, 256], F32)
        Rjunk = self.PR("junk")
        Rk = [self.PR("ksb0"), self.PR("ksb1")]
        Rv = [self.PR("vsb0"), self.PR("vsb1")]
        Rprod = [self.PR("prod0"), self.PR("prod1")]
        Ro4 = self.PR("o4m")
        ss2 = [ss, self.scr_view(lo + 4096 + 256, [128, 2, 4], F32)]
        es2 = [es, self.scr_view(lo + 4096 + 256 + 64, [128, 2, 4], BF16)]
        rd2 = [rd, self.scr_view(lo + 4096 + 256 + 128, [128, 4], F32)]
        pT2 = [pT, self.scr_view(lo + 4096 + 256 + 192, [128, 2, 4], BF16)]
        Rss2 = [self.PR("ss0"), self.PR("ss1")]
        Res2 = [self.PR("es0"), self.PR("es1")]
        Rrd2 = [self.PR("rd0"), self.PR("rd1")]
        RpT2 = [self.PR("pT0"), self.PR("pT1")]
        Rsel = self.PR("selmask")
        P.op(POOL, lambda e: e.memset(sel[:], 1.0), w=[Rsel])
        P.op(POOL, lambda e: e.affine_select(out=sel[:], in_=sel[:], pattern=[[1, NS], [0, 128]], compare_op=ALU.is_equal,
                                            fill=0.0, base=0, channel_multiplier=-1), r=[Rsel], w=[Rsel])
        P.op(POOL, lambda e: e.memset(mask4[:], 1.0), w=[Rsel])
        P.op(POOL, lambda e: e.affine_select(out=mask4[:], in_=mask4[:], pattern=[[1, D]], compare_op=ALU.is_ge,
                                            fill=0.0, base=0, channel_multiplier=-256), r=[Rsel], w=[Rsel])
        P.op(POOL, lambda e: e.affine_select(out=mask4[:], in_=mask4[:], pattern=[[-1, D]], compare_op=ALU.is_ge,
                                            fill=0.0, base=255, channel_multiplier=256), r=[Rsel], w=[Rsel])
        osb, rosb = self.bank(reserve=True)
        ck, cv = self.d["cache_mem_k"][l], self.d["cache_mem_v"][l]
        den_banks = {}

        def stage_a(s):
            k = s % 2
            P.load(POOL, ksb[:, k], ck[s].rearrange("(mb p) d -> p mb d", p=128), Rk[k])
            qb = []
            for half in range(2):
                bk, rb = self.bank()
                P.op(PE, lambda e, bk=bk, half=half: e.matmul(
                    bk[:, :], lhsT=sel[:, s, :], rhs=qtok[:, half * 512:(half + 1) * 512], start=True, stop=True),
                    r=[Rqtok, Rsel], w=[rb])
                qb.append((bk, rb))
            for mb in range(2):
                for half in range(2):
                    bk, rb = qb[half]
                    kk = (mb * 2 + half) % 2
                    P.op(DVE, lambda e, bk=bk, mb=mb, half=half, kk=kk: e.tensor_tensor(
                        out=prod[:, kk, :], in0=ksb[:, k, mb, half * 512:(half + 1) * 512], in1=bk[:, :], op=ALU.mult),
                        r=[rb, Rk[k]], w=[rb, Rprod[kk]])
                    for hh in range(2):
                        P.op(ACT, lambda e, mb=mb, half=half, kk=kk, hh=hh: e.activation(
                            out=junk[:, :], in_=prod[:, kk, hh * 256:(hh + 1) * 256], func=AF.Copy,
                            accum_out=ss2[k][:, mb, half * 2 + hh:half * 2 + hh + 1]), r=[Rprod[kk]], w=[Rss2[k], Rjunk])
            P.op(ACT, lambda e: e.activation(out=es2[k][:], in_=ss2[k][:], func=AF.Exp, scale=1.0 / 16.0), r=[Rss2[k]], w=[Res2[k]])

        def stage_b(s):
            k = s % 2
            P.load(POOL, vsb[:, k], cv[s].rearrange("(mb p) d -> p mb d", p=128), Rv[k])
            bk, rb = self.bank()
            for mb in range(2):
                P.op(PE, lambda e, bk=bk, mb=mb: e.matmul(bk[:, 0:4], lhsT=self.ones_b[:], rhs=es2[k][:, mb, :],
                                                         start=(mb == 0), stop=(mb == 1)), r=[Res2[k], self.Rconst], w=[rb])
            P.op(DVE, lambda e, bk=bk: e.reciprocal(out=rd2[k][:], in_=bk[:, 0:4]), r=[rb], w=[rb, Rrd2[k]])
            for mb in range(2):
                P.op(DVE, lambda e, mb=mb: e.tensor_tensor(out=pT2[k][:, mb, :], in0=es2[k][:, mb, :], in1=rd2[k][:], op=ALU.mult),
                     r=[Res2[k], Rrd2[k]], w=[RpT2[k]])

        def stage_c(s):
            k = s % 2
            for half in range(2):
                bk, rb = self.bank()
                for mb in range(2):
                    P.op(PE, lambda e, bk=bk, mb=mb, half=half: e.matmul(
                        bk[0:4, :], lhsT=pT2[k][:, mb, :], rhs=vsb[:, k, mb, half * 512:(half + 1) * 512],
                        start=(mb == 0), stop=(mb == 1)), r=[RpT2[k], Rv[k]], w=[rb])
                P.op(DVE, lambda e, bk=bk, half=half: e.tensor_tensor(
                    out=o4m[:, half * 512:(half + 1) * 512], in0=bk[0:4, :], in1=mask4[:, half * 512:(half + 1) * 512],
                    op=ALU.mult), r=[rb, Rsel], w=[rb, Ro4])
            for c in range(8):
                P.op(PE, lambda e, c=c: e.matmul(osb[:, c * NS + s: c * NS + s + 1], lhsT=o4m[:, c * 128:(c + 1) * 128],
                                                rhs=self.ones_b[0:4, 0:1], start=True, stop=True),
                     r=[Ro4, self.Rconst], w=[rosb])

        for step in range(NS + 2):
            if 0 <= step - 2 < NS:
                stage_c(step - 2)
            if 0 <= step - 1 < NS:
                stage_b(step - 1)
            if step < NS:
                stage_a(step)
            yield
        P.op(ACT, lambda e: e.copy(out=oT[:, :, NT:NT + NS], in_=osb[:, 0:8 * NS].rearrange("p (c s) -> p c s", c=8)),
             r=[rosb], w=[rosb] + RoT)
        self.release(osb)

    def rstd_from_var(self, var_ap, out_ap, Rv, rows):
        P = self.P
        P.op(DVE, lambda e: e.tensor_scalar(out=out_ap, in0=var_ap, scalar1=0.0, scalar2=self._eps, op0=ALU.max, op1=ALU.add),
             r=[Rv], w=[Rv])
        P.op(ACT, lambda e: e.activation(out=out_ap, in_=out_ap, func=AF.Ln), r=[Rv], w=[Rv])
        P.op(ACT, lambda e: e.activation(out=out_ap, in_=out_ap, func=AF.Exp, scale=-0.5), r=[Rv], w=[Rv])

    def mixer(self, t, l):
        if l == 0:
            self.mixer_ab(t)
        else:
            self.mixer_gm(t)

    def mixer_gm(self, t):
        P = self.P
        subs = self.subtiles(t)
        w_in, w_out = self.d["gm_w_in"], self.d["gm_w_out"]
        uT = self.scr_view(0, [128, 8, NC], BF16)
        h2T = self.scr_view(16640, [128, 8, NC], BF16)
        vraw = self.scr_view(33280, [128, D], F32)
        vtok = self.scr_view(37376, [128, 2, D], BF16)
        stats = self.scr_view(41472, [128, 2, 6], F32)
        mv = self.scr_view(41472 + 48, [128, 2], F32)
        rstd = self.scr_view(41472 + 56, [128, 1], F32)
        WsT = self.scr_view(41600, [128, 4, 128], BF16)
        bsrow = self.scr_view(42624, [1, 512], BF16)
        w00b = self.scr_view(43648, [NS, 4], F32)
        b0b = self.scr_view(43648 + 16, [NS, 4], F32)
        hi = 74 * 1024
        gmg = self.scr_view(hi, [128, D], F32)
        gmb = self.scr_view(hi + 4096, [128, D], F32)
        wsnat = self.scr_view(hi + 8192, [128, 4, 128], F32)
        fsamp = self.scr_view(hi + 10240, [NS, D], F32)
        vsamp = self.scr_view(hi + 14336, [NS, D], F32)
        RuT = [self.PR(f"uT{c}") for c in range(8)]
        Rh2 = [self.PR(f"h2T{c}") for c in range(8)]
        Rvraw, Rstat = self.PR("vraw"), self.PR("gmstat")
        Rvtok = [self.PR("vtok0"), self.PR("vtok1")]
        Rc = self.PR("gmconst")
        Rfs, Rvs = self.PR("fsamp"), self.PR("vsamp")
        self._eps = LN_EPS
        P.load(SP, gmg[:], self.d["gm_ln_g"].partition_broadcast(128).rearrange("p a d -> p (a d)"), Rc)
        P.load(SP, gmb[:], self.d["gm_ln_b"].partition_broadcast(128).rearrange("p a d -> p (a d)"), Rc)
        P.load(SP, wsnat[:], self.d["gm_w_s"].rearrange("g i j -> i g j"), Rc)
        Rbs = self.PR("bsrow")
        P.load(POOL, bsrow[:], self.d["gm_b_s"].rearrange("g i -> (g i)").rearrange("(a n) -> a n", a=1), Rbs)
        if t == 1:
            P.load(SP, w00b[:], self.d["gm_w_s"][:, 0, 0:1].rearrange("g a -> a g").partition_broadcast(NS).rearrange("p a g -> p (a g)"), Rc,
                   allow_slow_non_contiguous=True)
            P.load(SP, b0b[:], self.d["gm_b_s"][:, 0:1].rearrange("g a -> a g").partition_broadcast(NS).rearrange("p a g -> p (a g)"), Rc,
                   allow_slow_non_contiguous=True)
        bk, rb = self.bank()
        for g in range(4):
            P.op(PE, lambda e, bk=bk, g=g: e.transpose(bk[:, g * 128:(g + 1) * 128], wsnat[:, g, :], self.ident_f[:]),
                 r=[Rc, self.Rconst], w=[rb])
        P.op(DVE, lambda e, bk=bk: e.tensor_tensor(out=WsT[:], in0=bk[:].rearrange("p (g i) -> p g i", g=4),
                                                  in1=self.mask_c[:, :].unsqueeze(1).to_broadcast([128, 4, 128]), op=ALU.mult),
             r=[rb, self.Rconst], w=[rb, Rc])
        for g in range(4):
            slot, rs = self.wloadA(w_in, g * 256, 256)
            for jj in range(2):
                m = 2 * g + jj
                for (c0, n) in subs:
                    bk, rb = self.bank()
                    for kc in range(8):
                        P.op(PE, lambda e, bk=bk, kc=kc, jj=jj, c0=c0, n=n, slot=slot: e.matmul(
                            bk[:, 0:n], lhsT=slot[:, kc, jj * 128:(jj + 1) * 128], rhs=self.xb[:, kc, c0:c0 + n],
                            start=(kc == 0), stop=(kc == 7)), r=[rs, self.xbr(kc, c0)], w=[rb])
                    P.op(ACT, lambda e, bk=bk, m=m, c0=c0, n=n: e.activation(out=uT[:, m, c0:c0 + n], in_=bk[:, 0:n],
                                                                            func=AF.Gelu_apprx_tanh), r=[rb], w=[rb, RuT[m]])
        vslots = [self.wloadA(w_in, D + g * 256, 256) for g in range(4)]
        blocks = [(tb * 128, 128) for tb in range(NT // 128)]
        if t == 1:
            blocks.append((NT, NS))
        vraw2 = [vraw, self.scr_view(hi + 18432, [128, D], F32)]
        Rvraw2 = [Rvraw, self.PR("vraw1")]
        stats2 = [stats, self.scr_view(41472 + 64, [128, 2, 6], F32)]
        mv2 = [mv, self.scr_view(41472 + 64 + 48, [128, 2], F32)]
        rstd2 = [rstd, self.scr_view(41472 + 64 + 56, [128, 1], F32)]
        Rstat2 = [Rstat, self.PR("gmstat1")]

        def proj(bi):
            col0, rows = blocks[bi]
            vr, Rvr = vraw2[bi % 2], Rvraw2[bi % 2]
            b2 = [self.bank(), self.bank()]
            for g in range(4):
                slot, rs = vslots[g]
                bk, rb = b2[g // 2]
                for kc in range(8):
                    P.op(PE, lambda e, bk=bk, kc=kc, g=g, slot=slot: e.matmul(
                        bk[0:rows, (g % 2) * 256:(g % 2 + 1) * 256], lhsT=self.xb[:, kc, col0:col0 + rows], rhs=slot[:, kc, 0:256],
                        start=(kc == 0), stop=(kc == 7)), r=[rs, self.xbr(kc, col0)], w=[rb])
            for hf in range(2):
                bk, rb = b2[hf]
                P.op(ACT, lambda e, bk=bk, hf=hf: e.activation(out=vr[0:rows, hf * 512:(hf + 1) * 512], in_=bk[0:rows, :],
                                                             func=AF.Gelu_apprx_tanh), r=[rb], w=[rb, Rvr])

        def lnmix(bi):
            col0, rows = blocks[bi]
            q = bi % 2
            vr, Rvr, st_, mv_, rs_, Rst = vraw2[q], Rvraw2[q], stats2[q], mv2[q], rstd2[q], Rstat2[q]
            for hf in range(2):
                P.op(DVE, lambda e, hf=hf: e.bn_stats(out=st_[0:rows, hf, :], in_=vr[0:rows, hf * 512:(hf + 1) * 512]),
                     r=[Rvr], w=[Rst])
            P.op(DVE, lambda e: e.bn_aggr(out=mv_[0:rows, :], in_=st_[0:rows].rearrange("p a b -> p (a b)")), r=[Rst], w=[Rst])
            self.rstd_from_var(mv_[0:rows, 1:2], rs_[0:rows, :], Rst, rows)
            P.op(DVE, lambda e: e.tensor_scalar(out=vr[0:rows, :], in0=vr[0:rows, :], scalar1=mv_[0:rows, 0:1],
                                                scalar2=rs_[0:rows, 0:1], op0=ALU.subtract, op1=ALU.mult), r=[Rvr, Rst], w=[Rvr])
            P.op(DVE, lambda e: e.tensor_tensor(out=vr[0:rows, :], in0=vr[0:rows, :], in1=gmg[0:rows, :], op=ALU.mult),
                 r=[Rvr, Rc], w=[Rvr])
            if rows == 128:
                k = bi % 2
                P.op(DVE, lambda e: e.tensor_tensor(out=vtok[:, k, :], in0=vr[:, :], in1=gmb[:, :], op=ALU.add),
                     r=[Rvr, Rc], w=[Rvtok[k]])
                for hf in range(2):
                    bk, rb = self.bank()
                    for cc in range(4):
                        c = hf * 4 + cc
                        g = c // 2
                        P.op(PE, lambda e, bk=bk, cc=cc, c=c, g=g: e.matmul(
                            bk[:, cc * 128:(cc + 1) * 128], lhsT=vtok[:, k, c * 128:(c + 1) * 128], rhs=WsT[:, g, :],
                            start=True, stop=False), r=[Rvtok[k], Rc], w=[rb])
                        P.op(PE, lambda e, bk=bk, cc=cc, g=g: e.matmul(
                            bk[:, cc * 128:(cc + 1) * 128], lhsT=self.ones_b[0:1, :], rhs=bsrow[0:1, g * 128:(g + 1) * 128],
                            start=False, stop=True), r=[Rbs, self.Rconst], w=[rb])
                    P.op(DVE, lambda e, bk=bk, hf=hf: e.tensor_tensor(
                        out=h2T[:, hf * 4:(hf + 1) * 4, col0:col0 + 128], in0=bk[:].rearrange("p (c i) -> p c i", c=4),
                        in1=uT[:, hf * 4:(hf + 1) * 4, col0:col0 + 128], op=ALU.mult),
                        r=[rb] + [RuT[hf * 4 + j] for j in range(4)], w=[rb] + [Rh2[hf * 4 + j] for j in range(4)])
            else:
                P.op(DVE, lambda e: e.tensor_tensor(out=vsamp[:, :], in0=vr[0:NS, :], in1=gmb[0:NS, :], op=ALU.add),
                     r=[Rvr, Rc], w=[Rvs])
                P.store(SP, self.o["gm_v_sample"][:, :], vsamp[:, :], Rvs)
                for g in range(4):
                    P.op(DVE, lambda e, g=g: e.tensor_scalar(out=fsamp[:, g * 256:(g + 1) * 256], in0=vsamp[:, g * 256:(g + 1) * 256],
                                                            scalar1=w00b[:, g:g + 1], scalar2=b0b[:, g:g + 1], op0=ALU.mult, op1=ALU.add),
                         r=[Rvs, Rc], w=[Rfs])
                bk, rb = self.bank()
                for c in range(8):
                    P.op(PE, lambda e, bk=bk, c=c: e.transpose(bk[:, c * NS:(c + 1) * NS], fsamp[:, c * 128:(c + 1) * 128],
                                                              self.ident_f[0:NS, 0:NS]), r=[Rfs, self.Rconst], w=[rb])
                P.op(DVE, lambda e, bk=bk: e.tensor_tensor(out=h2T[:, :, NT:NT + NS], in0=bk[:, 0:8 * NS].rearrange("p (c s) -> p c s", c=8),
                                                          in1=uT[:, :, NT:NT + NS], op=ALU.mult),
                     r=[rb] + RuT, w=[rb] + Rh2)

        proj(0)
        for bi in range(len(blocks)):
            if bi + 1 < len(blocks):
                proj(bi + 1)
            lnmix(bi)
        self.out_proj_ln(t, w_out, h2T, Rh2, 5)

    def keep(self, res_list):
        self.phase_res.extend(res_list)

    def mixer_ab(self, t):
        P = self.P
        subs = self.subtiles(t)
        oc = self.scr_view(0, [128, 8, NC], BF16)
        Roc = [self.PR(f"oc{c}") for c in range(8)]
        self.conformer_prompt(t, oc, Roc)
        if t == 1:
            self.phase_barrier()
            self.keep(Roc)
            self.conformer_samples(oc, Roc)
        lnres = list(self.Rln) + list(self.Rlnt)
        if self.cfg.get("no_gdn"):
            self.phase_barrier()
            self.keep(Roc)
            P.op(DVE, lambda e: e.memset(oc[:, 0:4, :], 0.0), w=Roc[0:4])
        else:
            self.phase_barrier(extra_old=lnres)
            self.keep(Roc)
            self.gdn(t, oc, Roc)
            if t == 1:
                self.phase_barrier(extra_new=lnres)
                self.keep(Roc)
                self.gdn_samples(oc, Roc)
            self.phase_barrier(extra_new=lnres)
            self.keep(Roc)
        self.out_proj_ln(t, self.d["ab_w_out"], oc, Roc, 1)

    def conformer_prompt(self, t, oc, Roc):
        P = self.P
        w_in = self.d["ab_w_in"]
        psubs = [(0, 512), (512, 512)]
        glubuf = self.scr_view(16640, [128, 2, 1056], BF16)
        ccv = self.scr_view(20864, [128, 4, NC], F32)
        ccw = self.scr_view(37504, [128, 4, 31], F32)
        ccb = self.scr_view(38000, [128, 4, 1], F32)
        ccg = self.scr_view(38016, [128, 4, 1], F32)
        ccbeta = self.scr_view(38032, [128, 4, 1], F32)
        halo_f = self.scr_view(38080, [128, 4, 30], F32)
        sgt = self.scr_view(70 * 1024, [128, 2, 512], F32)
        dg = self.scr_view(74 * 1024, [128, 2, 31, 128], BF16)
        Rglu = [self.PR("glu0"), self.PR("glu1")]
        Rccv = [self.PR(f"ccv{i}") for i in range(4)]
        Rcp = self.PR("ccparams")
        Rhf = self.PR("halo_f")
        Rsg = [self.PR("sgt0"), self.PR("sgt1")]
        Rdg = [self.PR("dg0"), self.PR("dg1")]
        self.fm_load(self.d["cc_conv_w"], 31, 512, ccw, Rcp)
        self.fm_load(self.d["cc_conv_b"], 1, 512, ccb, Rcp)
        self.fm_load(self.d["cc_ln_g"], 1, 512, ccg, Rcp)
        self.fm_load(self.d["cc_ln_b"], 1, 512, ccbeta, Rcp)
        isg = 0
        for g2 in range(2):
            sa, ra = self.wloadA(w_in, 2056 + g2 * 256, 256)
            sbb, rbb = self.wloadA(w_in, 2568 + g2 * 256, 256)
            for jj in range(2):
                i = 2 * g2 + jj
                k = i % 2
                P.op(DVE, lambda e, k=k, i=i: e.tensor_copy(out=glubuf[:, k, 0:30], in_=self.cc_halo[:, i, :]),
                     r=[self.Rcch], w=[Rglu[k]])
                for w in range(31):
                    P.op(DVE, lambda e, k=k, w=w, i=i: e.tensor_scalar(out=dg[:, k, w, :], in0=self.ident_b[:], scalar1=ccw[:, i, w:w + 1],
                                                                     scalar2=None, op0=ALU.mult),
                         r=[Rcp, self.Rconst], w=[Rdg[k]])
                for (c0, n) in psubs:
                    ba, rba = self.bank()
                    bb, rbbk = self.bank()
                    for kc in range(8):
                        P.op(PE, lambda e, ba=ba, kc=kc, jj=jj, c0=c0, n=n, sa=sa: e.matmul(
                            ba[:, 0:n], lhsT=sa[:, kc, jj * 128:(jj + 1) * 128], rhs=self.xb[:, kc, c0:c0 + n],
                            start=(kc == 0), stop=(kc == 7)), r=[ra, self.xbr(kc, c0)], w=[rba])
                    for kc in range(8):
                        P.op(PE, lambda e, bb=bb, kc=kc, jj=jj, c0=c0, n=n, sbb=sbb: e.matmul(
                            bb[:, 0:n], lhsT=sbb[:, kc, jj * 128:(jj + 1) * 128], rhs=self.xb[:, kc, c0:c0 + n],
                            start=(kc == 0), stop=(kc == 7)), r=[rbb, self.xbr(kc, c0)], w=[rbbk])
                    q = isg % 2
                    isg += 1
                    P.op(ACT, lambda e, bb=bb, q=q, n=n: e.activation(out=sgt[:, q, 0:n], in_=bb[:, 0:n], func=AF.Sigmoid),
                         r=[rbbk], w=[rbbk, Rsg[q]])
                    P.op(DVE, lambda e, ba=ba, q=q, n=n, k=k, c0=c0: e.tensor_tensor(
                        out=glubuf[:, k, 30 + c0:30 + c0 + n], in0=ba[:, 0:n], in1=sgt[:, q, 0:n], op=ALU.mult),
                        r=[rba, Rsg[q]], w=[rba, Rglu[k]])
                    if t == 1 and c0 == 512:
                        P.op(DVE, lambda e, ba=ba, q=q, i=i: e.tensor_tensor(
                            out=halo_f[:, i, :], in0=ba[:, 482:512], in1=sgt[:, q, 482:512], op=ALU.mult),
                            r=[rba, Rsg[q]], w=[rba, Rhf])
                P.op(DVE, lambda e, k=k, i=i: e.tensor_copy(out=self.cc_halo[:, i, :], in_=glubuf[:, k, 1024:1054]),
                     r=[Rglu[k]], w=[self.Rcch])
                for (c0, n) in psubs:
                    bk, rb = self.bank()
                    for w in range(31):
                        P.op(PE, lambda e, bk=bk, w=w, k=k, c0=c0, n=n: e.matmul(
                            bk[:, 0:n], lhsT=dg[:, k, w, :], rhs=glubuf[:, k, c0 + w:c0 + w + n],
                            start=(w == 0), stop=(w == 30)), r=[Rdg[k], Rglu[k]], w=[rb])
                    P.op(ACT, lambda e, bk=bk, i=i, c0=c0, n=n: e.activation(out=ccv[:, i, c0:c0 + n], in_=bk[:, 0:n], func=AF.Identity,
                                                                            bias=ccb[:, i, :], scale=1.0), r=[rb, Rcp], w=[rb, Rccv[i]])

        def finish(kc, c0, n, tt, Rt):
            P.op(ACT, lambda e: e.activation(out=oc[:, 4 + kc, c0:c0 + n], in_=tt, func=AF.Silu,
                                             scale=ccg[:, kc, :], bias=ccbeta[:, kc, :]), r=[Rt, Rcp], w=[Roc[4 + kc]])

        self.ln_fm(4, lambda kc, c0, n: ccv[:, kc, c0:c0 + n], (lambda kc, c0: Rccv[kc]), self.ones512_b, psubs, finish,
                   src_all=lambda c0, n: ccv[:, :, c0:c0 + n])
        if t == 1:
            i = self.istage % 2
            self.istage += 1
            Rs = self.Rstage[i]
            bk, rb = self.bank()
            for c in range(4):
                P.op(PE, lambda e, bk=bk, c=c: e.transpose(bk[0:30, c * 128:(c + 1) * 128], halo_f[:, c, :], self.ident_f[:]),
                     r=[Rhf, self.Rconst], w=[rb])
            P.op(DVE, lambda e, bk=bk, i=i: e.tensor_copy(out=self.stage[0:30, i, 0:512], in_=bk[0:30, :]), r=[rb], w=[rb, Rs])
            P.store(SP, self.o["cc_conv_prompt"][:, :], self.stage[0:30, i, 0:512], Rs)

    def conformer_samples(self, oc, Roc):
        P = self.P
        w_in = self.d["ab_w_in"]
        hi = 74 * 1024
        st = self.scr_view(hi, [120, 4, 512], F32)
        wrep = self.scr_view(hi + 8192, [120, 512], F32)
        prod = self.scr_view(hi + 10240, [120, 2, 512], F32)
        ind = self.scr_view(hi + 14336, [120, 4, NS], F32)
        lo = 16640
        glus = self.scr_view(lo, [NS, 512], F32)
        w30b = self.scr_view(lo + 2048, [NS, 512], F32)
        biasb = self.scr_view(lo + 4096, [NS, 512], F32)
        lngb = self.scr_view(lo + 6144, [NS, 512], F32)
        lnbb = self.scr_view(lo + 8192, [NS, 512], F32)
        ys = self.scr_view(lo + 10240, [NS, 512], F32)
        sgs = self.scr_view(lo + 12288, [NS, 512], F32)
        stats = self.scr_view(lo + 14336, [NS, 6], F32)
        mv = self.scr_view(lo + 14336 + 32, [NS, 2], F32)
        rstd = self.scr_view(lo + 14336 + 48, [NS, 1], F32)
        Rst, Rw, Rprod, Rglus, Rys, Rsm = self.PR("ccst"), self.PR("ccw"), [self.PR("ccp0"), self.PR("ccp1")], self.PR("glus"), self.PR("ys"), self.PR("ccsm")
        P.load(SP, st[:], self.d["state_cc_conv"].rearrange("(b j) w c -> (j w) b c", j=4), Rst)
        for j in range(4):
            P.load(SP, wrep[j * 30:(j + 1) * 30, :], self.d["cc_conv_w"][0:30, :], Rw)
        P.load(SP, w30b[:], self.d["cc_conv_w"][30:31, :].partition_broadcast(NS).rearrange("p a d -> p (a d)"), Rw)
        P.load(SP, biasb[:], self.d["cc_conv_b"].partition_broadcast(NS).rearrange("p a d -> p (a d)"), Rw)
        P.load(SP, lngb[:], self.d["cc_ln_g"].partition_broadcast(NS).rearrange("p a d -> p (a d)"), Rw)
        P.load(SP, lnbb[:], self.d["cc_ln_b"].partition_broadcast(NS).rearrange("p a d -> p (a d)"), Rw)
        P.op(POOL, lambda e: e.memset(ind[:], 1.0), w=[Rw])
        P.op(POOL, lambda e: e.affine_select(out=ind[:], in_=ind[:], pattern=[[120, 4], [-30, NS]], compare_op=ALU.is_ge,
                                            fill=0.0, base=0, channel_multiplier=1), r=[Rw], w=[Rw])
        P.op(POOL, lambda e: e.affine_select(out=ind[:], in_=ind[:], pattern=[[-120, 4], [30, NS]], compare_op=ALU.is_ge,
                                            fill=0.0, base=29, channel_multiplier=-1), r=[Rw], w=[Rw])
        for g2 in range(2):
            sa, ra = self.wloadA(w_in, 2056 + g2 * 256, 256)
            sbb, rbb = self.wloadA(w_in, 2568 + g2 * 256, 256)
            ba, rba = self.bank()
            bb, rbbk = self.bank()
            for kc in range(8):
                P.op(PE, lambda e, ba=ba, kc=kc, sa=sa: e.matmul(ba[0:NS, 0:256], lhsT=self.xb[:, kc, NT:NT + NS], rhs=sa[:, kc, 0:256],
                                                                start=(kc == 0), stop=(kc == 7)), r=[ra, self.xbr(kc, NT)], w=[rba])
            for kc in range(8):
                P.op(PE, lambda e, bb=bb, kc=kc, sbb=sbb: e.matmul(bb[0:NS, 0:256], lhsT=self.xb[:, kc, NT:NT + NS], rhs=sbb[:, kc, 0:256],
                                                                  start=(kc == 0), stop=(kc == 7)), r=[rbb, self.xbr(kc, NT)], w=[rbbk])
            P.op(ACT, lambda e, bb=bb, g2=g2: e.activation(out=sgs[:, g2 * 256:(g2 + 1) * 256], in_=bb[0:NS, 0:256], func=AF.Sigmoid),
                 r=[rbbk], w=[rbbk, Rsm])
            P.op(DVE, lambda e, ba=ba, g2=g2: e.tensor_tensor(out=glus[:, g2 * 256:(g2 + 1) * 256], in0=ba[0:NS, 0:256],
                                                             in1=sgs[:, g2 * 256:(g2 + 1) * 256], op=ALU.mult),
                 r=[rba, Rsm], w=[rba, Rglus])
        P.d2d(SP, self.o["cc_conv_sample"][:, 0:29, :], self.d["state_cc_conv"][:, 1:30, :], self.Rd2d)
        P.store(SP, self.o["cc_conv_sample"][:, 29, :], glus[:, :], Rglus)
        bk, rb = self.bank()
        for b in range(4):
            q = b % 2
            P.op(DVE, lambda e, b=b, q=q: e.tensor_tensor(out=prod[:, q, :], in0=st[:, b, :], in1=wrep[:, :], op=ALU.mult),
                 r=[Rst, Rw], w=[Rprod[q]])
            P.op(PE, lambda e, bk=bk, b=b, q=q: e.matmul(bk[0:NS, :], lhsT=ind[:, b, :], rhs=prod[:, q, :], start=(b == 0), stop=(b == 3)),
                 r=[Rprod[q], Rw], w=[rb])
        P.op(DVE, lambda e: e.tensor_tensor(out=ys[:], in0=glus[:], in1=w30b[:], op=ALU.mult), r=[Rglus, Rw], w=[Rys])
        P.op(DVE, lambda e, bk=bk: e.tensor_tensor(out=ys[:], in0=bk[0:NS, :], in1=ys[:], op=ALU.add), r=[rb, Rys], w=[rb, Rys])
        P.op(DVE, lambda e: e.tensor_tensor(out=ys[:], in0=ys[:], in1=biasb[:], op=ALU.add), r=[Rys, Rw], w=[Rys])
        P.op(DVE, lambda e: e.bn_stats(out=stats[:], in_=ys[:]), r=[Rys], w=[Rsm])
        P.op(DVE, lambda e: e.bn_aggr(out=mv[:], in_=stats[:]), r=[Rsm], w=[Rsm])
        self._eps = LN_EPS
        self.rstd_from_var(mv[:, 1:2], rstd[:], Rsm, NS)
        P.op(DVE, lambda e: e.tensor_scalar(out=ys[:], in0=ys[:], scalar1=mv[:, 0:1], scalar2=rstd[:, 0:1],
                                            op0=ALU.subtract, op1=ALU.mult), r=[Rys, Rsm], w=[Rys])
        P.op(DVE, lambda e: e.tensor_tensor(out=ys[:], in0=ys[:], in1=lngb[:], op=ALU.mult), r=[Rys, Rw], w=[Rys])
        P.op(DVE, lambda e: e.tensor_tensor(out=ys[:], in0=ys[:], in1=lnbb[:], op=ALU.add), r=[Rys, Rw], w=[Rys])
        P.op(ACT, lambda e: e.activation(out=ys[:], in_=ys[:], func=AF.Silu), r=[Rys], w=[Rys])
        bk, rb = self.bank()
        for c in range(4):
            P.op(PE, lambda e, bk=bk, c=c: e.transpose(bk[:, c * NS:(c + 1) * NS], ys[:, c * 128:(c + 1) * 128],
                                                      self.ident_f[0:NS, 0:NS]), r=[Rys, self.Rconst], w=[rb])
        P.op(ACT, lambda e, bk=bk: e.copy(out=oc[:, 4:8, NT:NT + NS], in_=bk[:, 0:4 * NS].rearrange("p (c s) -> p c s", c=4)),
             r=[rb], w=[rb] + Roc[4:8])

    def gdn(self, t, oc, Roc):
        P = self.P
        w_in = self.d["ab_w_in"]
        psubs = [(0, 512), (512, 512)]
        NCH = NT // 128
        qkv = self.scr_view(16640, [128, 12, NT], BF16)
        rawbuf = self.scr_view(41216, [128, 2, 1028], BF16)
        sm = 45328
        dnw = self.scr_view(sm, [128, 12, 4], F32)
        normg = self.scr_view(sm + 192, [128, 1, 1], F32)
        dtb = self.scr_view(sm + 196, [4, 1], F32)
        negA = self.scr_view(sm + 200, [4, 1], F32)
        dnlast = self.scr_view(sm + 208, [128, 12, 3], F32)
        bg = self.scr_view(sm + 400, [128, NCH, 8], F32)
        gcum = self.scr_view(sm + 656, [128, NCH, 4], F32)
        eg = self.scr_view(sm + 784, [128, NCH, 4], F32)
        etail = self.scr_view(sm + 912, [128, NCH, 4], F32)
        egl = self.scr_view(sm + 1040, [128, NCH, 4], F32)
        gl = self.scr_view(sm + 1168, [128, NCH, 4], F32)
        cw = self.scr_view(sm + 1296, [128, NCH, 4], F32)
        ssq = self.scr_view(sm + 1424, [128, 4], F32)
        rr = self.scr_view(sm + 1440, [128, 4], F32)
        T0 = 47104
        sqt = self.scr_view(T0, [128, 2, 512], BF16)
        rn = self.scr_view(T0 + 2048, [128, 2, 512], F32)
        bgT = self.scr_view(T0 + 6144, [4, 2, 512], F32)
        eaT = self.scr_view(T0 + 10240, [4, 512], F32)
        dgd = self.scr_view(T0 + 12288, [128, 2, 4, 128], BF16)
        Rqkv = [self.PR(f"qkv{c}") for c in range(12)]
        Rraw = [self.PR("raw0"), self.PR("raw1")]
        Rdgd = [self.PR("dgd0"), self.PR("dgd1")]
        Rprm, Rdnl, Rbg, Rgc = self.PR("gprm"), self.PR("dnlast"), self.PR("bg"), self.PR("gcum")
        Rssq = self.PR("ssq")
        Rsqt = [self.PR("sqt0"), self.PR("sqt1")]
        Rrn = [self.PR("rn0"), self.PR("rn1")]
        RbgT, Rea = self.PR("bgT"), self.PR("eaT")
        bc4 = lambda ap2: ap2.unsqueeze(2).to_broadcast([128, 4, 128])
        bcm = lambda m: m[:, :].unsqueeze(1).to_broadcast([128, 4, 128])
        self.fm_load(self.d["dn_conv_w"], 4, 1536, dnw, Rprm)
        self.fm_load(self.d["dn_norm_g"], 1, 128, normg, Rprm)
        P.load(SP, dtb[:], self.d["dn_dt_bias"].rearrange("a h -> h a"), Rprm, allow_slow_non_contiguous=True)
        P.load(SP, negA[:], self.d["dn_A_log"].rearrange("a h -> h a"), Rprm, allow_slow_non_contiguous=True)
        P.op(ACT, lambda e: e.activation(out=negA[:], in_=negA[:], func=AF.Exp), r=[Rprm], w=[Rprm])
        P.op(DVE, lambda e: e.tensor_scalar(out=negA[:], in0=negA[:], scalar1=-1.0, scalar2=None, op0=ALU.mult), r=[Rprm], w=[Rprm])
        def l2_part1(c):
            for q, (c0, n) in enumerate(psubs):
                P.op(ACT, lambda e, c=c, c0=c0, n=n, q=q: e.activation(out=sqt[:, q, 0:n], in_=qkv[:, c, c0:c0 + n], func=AF.Square),
                     r=[Rqkv[c]], w=[Rsqt[q]])

        def l2_part2(c):
            scl = float(128 ** -0.5) if c < 4 else 1.0
            for q, (c0, n) in enumerate(psubs):
                bk, rb = self.bank()
                P.op(PE, lambda e, bk=bk, q=q, n=n: e.matmul(bk[:, 0:n], lhsT=self.ones_b[:], rhs=sqt[:, q, 0:n], start=True, stop=True),
                     r=[Rsqt[q], self.Rconst], w=[rb])
                P.op(DVE, lambda e, bk=bk, q=q, n=n: e.tensor_scalar(out=rn[:, q, 0:n], in0=bk[:, 0:n], scalar1=1e-6, scalar2=None, op0=ALU.add),
                     r=[rb], w=[rb, Rrn[q]])
                P.op(ACT, lambda e, q=q, n=n: e.activation(out=rn[:, q, 0:n], in_=rn[:, q, 0:n], func=AF.Ln), r=[Rrn[q]], w=[Rrn[q]])
                P.op(ACT, lambda e, q=q, n=n: e.activation(out=rn[:, q, 0:n], in_=rn[:, q, 0:n], func=AF.Exp, scale=-0.5), r=[Rrn[q]], w=[Rrn[q]])
                P.op(DVE, lambda e, c=c, c0=c0, n=n, q=q, scl=scl: e.scalar_tensor_tensor(
                    out=qkv[:, c, c0:c0 + n], in0=qkv[:, c, c0:c0 + n], scalar=scl, in1=rn[:, q, 0:n], op0=ALU.mult, op1=ALU.mult),
                    r=[Rqkv[c], Rrn[q]], w=[Rqkv[c]])

        for g in range(6):
            slot, rs = self.wloadA(w_in, g * 256, 256)
            for jj in range(2):
                c = 2 * g + jj
                k = c % 2
                P.op(DVE, lambda e, k=k, c=c: e.tensor_copy(out=rawbuf[:, k, 0:3], in_=self.dn_halo[:, c, :]), r=[self.Rdnh], w=[Rraw[k]])
                for w in range(4):
                    P.op(DVE, lambda e, k=k, w=w, c=c: e.tensor_scalar(out=dgd[:, k, w, :], in0=self.ident_b[:], scalar1=dnw[:, c, w:w + 1],
                                                                     scalar2=None, op0=ALU.mult),
                         r=[Rprm, self.Rconst], w=[Rdgd[k]])
                for (c0, n) in psubs:
                    bk, rb = self.bank()
                    for kc in range(8):
                        P.op(PE, lambda e, bk=bk, kc=kc, jj=jj, c0=c0, n=n, slot=slot: e.matmul(
                            bk[:, 0:n], lhsT=slot[:, kc, jj * 128:(jj + 1) * 128], rhs=self.xb[:, kc, c0:c0 + n],
                            start=(kc == 0), stop=(kc == 7)), r=[rs, self.xbr(kc, c0)], w=[rb])
                    P.op(ACT, lambda e, bk=bk, k=k, c0=c0, n=n: e.copy(out=rawbuf[:, k, 3 + c0:3 + c0 + n], in_=bk[:, 0:n]),
                         r=[rb], w=[rb, Rraw[k]])
                    if t == 1 and c0 == 512:
                        P.op(DVE, lambda e, bk=bk, c=c: e.tensor_copy(out=dnlast[:, c, :], in_=bk[:, 509:512]), r=[rb], w=[rb, Rdnl])
                P.op(DVE, lambda e, k=k, c=c: e.tensor_copy(out=self.dn_halo[:, c, :], in_=rawbuf[:, k, 1024:1027]),
                     r=[Rraw[k]], w=[self.Rdnh])
                if 1 <= c <= 8:
                    l2_part2(c - 1)
                for (c0, n) in psubs:
                    bk, rb = self.bank()
                    for w in range(4):
                        P.op(PE, lambda e, bk=bk, w=w, k=k, c0=c0, n=n: e.matmul(
                            bk[:, 0:n], lhsT=dgd[:, k, w, :], rhs=rawbuf[:, k, c0 + w:c0 + w + n], start=(w == 0), stop=(w == 3)),
                            r=[Rdgd[k], Rraw[k]], w=[rb])
                    P.op(ACT, lambda e, bk=bk, c=c, c0=c0, n=n: e.activation(out=qkv[:, c, c0:c0 + n], in_=bk[:, 0:n], func=AF.Silu),
                         r=[rb], w=[rb, Rqkv[c]])
                if c < 8:
                    l2_part1(c)
        slot, rs = self.wloadA(w_in, 2048, 8)
        bgb, rbgb = self.bank(reserve=True)
        for si, (c0, n) in enumerate(psubs):
            bb_, rbb_ = self.bank()
            ba_, rba_ = self.bank()
            for kc in range(8):
                P.op(PE, lambda e, bb_=bb_, kc=kc, c0=c0, n=n, slot=slot: e.matmul(bb_[0:4, 0:n], lhsT=slot[:, kc, 0:4], rhs=self.xb[:, kc, c0:c0 + n],
                                                                                start=(kc == 0), stop=(kc == 7)), r=[rs, self.xbr(kc, c0)], w=[rbb_])
            for kc in range(8):
                P.op(PE, lambda e, ba_=ba_, kc=kc, c0=c0, n=n, slot=slot: e.matmul(ba_[0:4, 0:n], lhsT=slot[:, kc, 4:8], rhs=self.xb[:, kc, c0:c0 + n],
                                                                                start=(kc == 0), stop=(kc == 7)), r=[rs, self.xbr(kc, c0)], w=[rba_])
            P.op(ACT, lambda e, bb_=bb_, n=n: e.activation(out=bgT[:, 0, 0:n], in_=bb_[0:4, 0:n], func=AF.Sigmoid), r=[rbb_], w=[rbb_, RbgT])
            P.op(ACT, lambda e, ba_=ba_, n=n: e.activation(out=eaT[:, 0:n], in_=ba_[0:4, 0:n], func=AF.Exp, bias=dtb[:, 0:1], scale=1.0),
                 r=[rba_, Rprm], w=[rba_, Rea])
            P.op(DVE, lambda e, n=n: e.tensor_scalar(out=eaT[:, 0:n], in0=eaT[:, 0:n], scalar1=1.0, scalar2=None, op0=ALU.add), r=[Rea], w=[Rea])
            P.op(ACT, lambda e, n=n: e.activation(out=eaT[:, 0:n], in_=eaT[:, 0:n], func=AF.Ln), r=[Rea], w=[Rea])
            P.op(DVE, lambda e, n=n: e.tensor_scalar(out=bgT[:, 1, 0:n], in0=eaT[:, 0:n], scalar1=negA[:, 0:1], scalar2=None, op0=ALU.mult),
                 r=[Rea, Rprm], w=[RbgT])
            for j in range(4):
                nn = si * 4 + j
                for q in range(2):
                    P.op(PE, lambda e, nn=nn, q=q, j=j: e.transpose(bgb[:, nn * 8 + q * 4: nn * 8 + q * 4 + 4], bgT[:, q, j * 128:(j + 1) * 128],
                                                                 self.ident_f[0:4, 0:4]), r=[RbgT, self.Rconst], w=[rbgb])
        P.op(DVE, lambda e: e.tensor_copy(out=bg[:], in_=bgb[:, 0:NCH * 8].rearrange("p (n k) -> p n k", k=8)), r=[rbgb], w=[rbgb, Rbg])
        self.release(bgb)
        bk, rb = self.bank()
        P.op(PE, lambda e, bk=bk: e.matmul(bk[:, 0:NCH * 4], lhsT=self.mask_c[:], rhs=bg[:, :, 4:8], start=True, stop=True),
             r=[Rbg, self.Rconst], w=[rb])
        P.op(DVE, lambda e, bk=bk: e.tensor_copy(out=gcum[:], in_=bk[:, 0:NCH * 4].rearrange("p (n h) -> p n h", h=4)), r=[rb], w=[rb, Rgc])
        bk, rb = self.bank()
        P.op(PE, lambda e, bk=bk: e.matmul(bk[:, 0:NCH * 4], lhsT=self.sel127[:], rhs=gcum[:], start=True, stop=True),
             r=[Rgc, self.Rconst], w=[rb])
        P.op(DVE, lambda e, bk=bk: e.tensor_copy(out=gl[:], in_=bk[:, 0:NCH * 4].rearrange("p (n h) -> p n h", h=4)), r=[rb], w=[rb, Rgc])
        P.op(ACT, lambda e: e.activation(out=eg[:], in_=gcum[:], func=AF.Exp), r=[Rgc], w=[Rgc])
        P.op(ACT, lambda e: e.activation(out=egl[:], in_=gl[:], func=AF.Exp), r=[Rgc], w=[Rgc])
        P.op(DVE, lambda e: e.tensor_tensor(out=etail[:], in0=gl[:], in1=gcum[:], op=ALU.subtract), r=[Rgc], w=[Rgc])
        P.op(ACT, lambda e: e.activation(out=etail[:], in_=etail[:], func=AF.Exp), r=[Rgc], w=[Rgc])
        P.op(DVE, lambda e: e.tensor_tensor(out=cw[:], in0=eg[:], in1=bg[:, :, 0:4], op=ALU.mult), r=[Rgc, Rbg], w=[Rgc])
        if self.cfg.get("dbg") and t == 1:
            dflat = self.o["gm_v_sample"].rearrange("a b -> (a b)")
            P.store(SP, dflat[0:8192].rearrange("(p k) -> p k", p=128), bg[:].rearrange("p n k -> p (n k)"), Rbg)
            P.store(SP, dflat[8192:12288].rearrange("(p k) -> p k", p=128), gcum[:].rearrange("p n k -> p (n k)"), Rgc)
            P.store(SP, dflat[12288:16384].rearrange("(p k) -> p k", p=128), etail[:].rearrange("p n k -> p (n k)"), Rgc)
        zslots = [self.wloadA(w_in, 1536 + g * 256, 256) for g in range(2)]
        self.phase_barrier()
        self.keep(Roc + Rqkv + [Rprm, Rdnl, Rbg, Rgc, Rssq])
        dgm = self.scr_view(T0, [128, 2, 4, 128], F32)
        DT = self.scr_view(T0 + 4096, [128, 8, 128], F32)
        X = self.scr_view(T0 + 8192, [128, 8, 128], F32)
        u = self.scr_view(T0 + 12288, [128, 8, 128], F32)
        sqo = self.scr_view(T0 + 16384, [128, 4, 128], F32)
        bfb = T0 + 18432
        names = ["Xb", "QKT", "qg", "wT", "ktail", "RHSw", "RHSv"]
        bt = {nm: self.scr_view(bfb + 2048 * i, [128, 8, 128], BF16) for i, nm in enumerate(names)}
        bt["vnew"] = self.scr_view(bfb + 14336, [128, 4, 128], BF16)
        bt["on"] = self.scr_view(bfb + 15360, [128, 4, 128], BF16)
        fb = bfb + 16384
        Nf = [self.scr_view(fb + 4096 * i, [128, 8, 128], F32) for i in range(2)]
        Mf = [self.scr_view(fb + 8192 + 4096 * i, [128, 8, 128], F32) for i in range(2)]
        assert fb + 16384 <= self.SCR
        zsil = self.scr_view(41216, [128, 2, 512], BF16)
        RNf = [[self.PR(f"Nf{i}_{c}") for c in range(2)] for i in range(2)]
        RMf = [[self.PR(f"Mf{i}_{c}") for c in range(2)] for i in range(2)]
        Rdgm = [self.PR("dgm0"), self.PR("dgm1")]
        RDT = [self.PR("DTa"), self.PR("DTb")]
        RX = [self.PR("Xa"), self.PR("Xb_")]
        Ru = [self.PR("ua"), self.PR("ub")]
        Rsqo = self.PR("sqo")
        Rb = {nm: [self.PR(nm + "a"), self.PR(nm + "b")] for nm in names}
        Rb["vnew"] = self.PR("vnew")
        Rb["on"] = self.PR("on")
        Rzs = [self.PR("zs0"), self.PR("zs1")]
        v4 = lambda bank: bank[:].rearrange("p (h i) -> p h i", h=4)
        idg = 0
        pending_tail = []

        def flush_tail():
            while pending_tail:
                cs_ = pending_tail.pop(0)
                bk, rb = self.bank()
                bkb = bk[:].bitcast(BF16)
                for h in range(4):
                    P.op(PE, lambda e, bkb=bkb, h=h: e.transpose(bkb[:, h * 128:(h + 1) * 128], bt["on"][:, h, :], self.ident_b[:]),
                         r=[Rb["on"], self.Rconst], w=[rb])
                P.op(DVE, lambda e, bkb=bkb, cs_=cs_: e.tensor_scalar(out=oc[:, 0:4, cs_], in0=bkb[:, 0:512].rearrange("p (h i) -> p h i", h=4),
                                                                  scalar1=normg[:, 0, 0:1], scalar2=None, op0=ALU.mult),
                     r=[rb, Rprm], w=[rb] + Roc[0:4])

        for pr in range(NCH // 2):
            pair = (2 * pr, 2 * pr + 1)
            hs = [slice(0, 4), slice(4, 8)]

            def bcast_rows(src_ap):
                nonlocal idg
                q = idg % 2
                idg += 1
                P.op(POOL, lambda e, q=q: e.tensor_tensor(out=dgm[:, q], in0=bcm(self.ident_f), in1=bc4(src_ap), op=ALU.mult),
                     r=[Rgc, Rbg, self.Rconst], w=[Rdgm[q]])
                bk, rb = self.bank()
                P.op(PE, lambda e, bk=bk, q=q: e.matmul(bk[:, :], lhsT=self.ones_f[:], rhs=dgm[:, q].rearrange("p h i -> p (h i)"),
                                                       start=True, stop=True), r=[Rdgm[q], self.Rconst], w=[rb])
                return bk, rb

            for ci, n_ in enumerate(pair):
                bk, rb = bcast_rows(gcum[:, n_, :])
                P.op(DVE, lambda e, bk=bk, ci=ci, n_=n_: e.tensor_tensor(out=DT[:, hs[ci], :], in0=v4(bk), in1=bc4(gcum[:, n_, :]), op=ALU.subtract),
                     r=[rb, Rgc], w=[rb, RDT[ci]])
                P.op(DVE, lambda e, ci=ci: e.tensor_scalar(out=DT[:, hs[ci], :], in0=DT[:, hs[ci], :], scalar1=0.0, scalar2=None, op0=ALU.min),
                     r=[RDT[ci]], w=[RDT[ci]])
                P.op(ACT, lambda e, ci=ci: e.activation(out=DT[:, hs[ci], :], in_=DT[:, hs[ci], :], func=AF.Exp), r=[RDT[ci]], w=[RDT[ci]])
                P.op(DVE, lambda e, ci=ci: e.tensor_tensor(out=DT[:, hs[ci], :], in0=DT[:, hs[ci], :], in1=bcm(self.mask_c), op=ALU.mult),
                     r=[RDT[ci], self.Rconst], w=[RDT[ci]])
            bq, bkk = [None, None], [None, None]
            for ci, n_ in enumerate(pair):
                cs = slice(n_ * 128, (n_ + 1) * 128)
                bk, rb = bcast_rows(eg[:, n_, :])
                P.op(DVE, lambda e, bk=bk, cs=cs, ci=ci: e.tensor_tensor(out=bt["qg"][:, hs[ci], :], in0=v4(bk), in1=qkv[:, 0:4, cs], op=ALU.mult),
                     r=[rb] + Rqkv[0:4], w=[rb, Rb["qg"][ci]])
                bk, rb = self.bank()
                bkb = bk[:].bitcast(BF16)
                for h in range(4):
                    P.op(PE, lambda e, bkb=bkb, h=h, cs=cs: e.transpose(bkb[:, h * 128:(h + 1) * 128], qkv[:, 4 + h, cs], self.ident_b[:]),
                         r=[Rqkv[4 + h], self.Rconst], w=[rb])
                    P.op(PE, lambda e, bkb=bkb, h=h, cs=cs: e.transpose(bkb[:, (4 + h) * 128:(5 + h) * 128], qkv[:, 8 + h, cs], self.ident_b[:]),
                         r=[Rqkv[8 + h], self.Rconst], w=[rb])
                ktk = bkb[:, 0:512].rearrange("p (h d) -> p h d", h=4)
                vtk = bkb[:, 512:1024].rearrange("p (h d) -> p h d", h=4)
                P.op(DVE, lambda e, ktk=ktk, n_=n_, ci=ci: e.tensor_tensor(out=bt["ktail"][:, hs[ci], :], in0=ktk, in1=bc4(etail[:, n_, :]), op=ALU.mult),
                     r=[rb, Rgc], w=[rb, Rb["ktail"][ci]])
                P.op(DVE, lambda e, ktk=ktk, n_=n_, ci=ci: e.tensor_tensor(out=bt["RHSw"][:, hs[ci], :], in0=ktk, in1=bc4(cw[:, n_, :]), op=ALU.mult),
                     r=[rb, Rgc], w=[rb, Rb["RHSw"][ci]])
                P.op(DVE, lambda e, vtk=vtk, n_=n_, ci=ci: e.tensor_tensor(out=bt["RHSv"][:, hs[ci], :], in0=vtk, in1=bc4(bg[:, n_, 0:4]), op=ALU.mult),
                     r=[rb, Rbg], w=[rb, Rb["RHSv"][ci]])
                bgk, rbgk = self.bank()
                bgq, rbgq = self.bank()
                for h in range(4):
                    P.op(PE, lambda e, bgk=bgk, h=h, cs=cs: e.matmul(bgk[:, h * 128:(h + 1) * 128], lhsT=qkv[:, 4 + h, cs], rhs=qkv[:, 4 + h, cs],
                                                                  start=True, stop=True), r=[Rqkv[4 + h]], w=[rbgk])
                    P.op(PE, lambda e, bgq=bgq, h=h, cs=cs: e.matmul(bgq[:, h * 128:(h + 1) * 128], lhsT=qkv[:, 4 + h, cs], rhs=qkv[:, h, cs],
                                                                  start=True, stop=True), r=[Rqkv[4 + h], Rqkv[h]], w=[rbgq])
                P.op(DVE, lambda e, bgq=bgq, ci=ci: e.tensor_tensor(out=bt["QKT"][:, hs[ci], :], in0=v4(bgq), in1=DT[:, hs[ci], :], op=ALU.mult),
                     r=[rbgq, RDT[ci]], w=[rbgq, Rb["QKT"][ci]])
                bkk[ci] = (bgk, rbgk)
            for ci, n_ in enumerate(pair):
                bk, rb = bcast_rows(bg[:, n_, 0:4])
                P.op(DVE, lambda e, bk=bk, ci=ci: e.tensor_tensor(out=DT[:, hs[ci], :], in0=v4(bk), in1=DT[:, hs[ci], :], op=ALU.mult),
                     r=[rb, RDT[ci]], w=[rb, RDT[ci]])
                P.op(DVE, lambda e, ci=ci: e.tensor_tensor(out=DT[:, hs[ci], :], in0=DT[:, hs[ci], :], in1=bcm(self.mask_s), op=ALU.mult),
                     r=[RDT[ci], self.Rconst], w=[RDT[ci]])
                bgk, rbgk = bkk[ci]
                P.op(DVE, lambda e, bgk=bgk, ci=ci: e.tensor_tensor(out=Nf[0][:, hs[ci], :], in0=v4(bgk), in1=DT[:, hs[ci], :], op=ALU.mult),
                     r=[rbgk, RDT[ci]], w=[rbgk, RNf[0][ci]])
            for ci in range(2):
                bk, rb = self.bank()
                for h in range(4):
                    P.op(PE, lambda e, bk=bk, h=h, ci=ci: e.transpose(bk[:, h * 128:(h + 1) * 128], Nf[0][:, ci * 4 + h, :], self.ident_f[:]),
                         r=[RNf[0][ci], self.Rconst], w=[rb])
                P.op(ACT, lambda e, bk=bk, ci=ci: e.copy(out=Mf[0][:, hs[ci], :], in_=v4(bk)), r=[rb], w=[rb, RMf[0][ci]])
                P.op(DVE, lambda e, ci=ci: e.tensor_tensor(out=X[:, hs[ci], :], in0=bcm(self.ident_f), in1=Nf[0][:, hs[ci], :], op=ALU.subtract),
                     r=[RNf[0][ci], self.Rconst], w=[RX[ci]])
            cur = 0
            for lev in range(6):
                nxt = 1 - cur
                for ci in range(2):
                    bm, rbm = self.bank()
                    for h in range(4):
                        b_ = ci * 4 + h
                        P.op(PE, lambda e, bm=bm, h=h, b_=b_, cur=cur: e.matmul(bm[:, h * 128:(h + 1) * 128], lhsT=Nf[cur][:, b_, :], rhs=Mf[cur][:, b_, :],
                                                                             start=True, stop=True), r=[RNf[cur][ci], RMf[cur][ci]], w=[rbm])
                    P.op(ACT, lambda e, bm=bm, ci=ci, nxt=nxt: e.copy(out=Mf[nxt][:, hs[ci], :], in_=v4(bm)), r=[rbm], w=[rbm, RMf[nxt][ci]])
                if lev < 5:
                    for ci in range(2):
                        bn, rbn = self.bank()
                        for h in range(4):
                            b_ = ci * 4 + h
                            P.op(PE, lambda e, bn=bn, h=h, b_=b_, cur=cur: e.matmul(bn[:, h * 128:(h + 1) * 128], lhsT=Mf[cur][:, b_, :], rhs=Nf[cur][:, b_, :],
                                                                                 start=True, stop=True), r=[RNf[cur][ci], RMf[cur][ci]], w=[rbn])
                        P.op(DVE, lambda e, bn=bn, ci=ci, nxt=nxt: e.tensor_copy(out=Nf[nxt][:, hs[ci], :], in_=v4(bn)), r=[rbn], w=[rbn, RNf[nxt][ci]])
                for ci in range(2):
                    bp, rbp = self.bank()
                    for h in range(4):
                        b_ = ci * 4 + h
                        P.op(PE, lambda e, bp=bp, h=h, b_=b_, nxt=nxt: e.matmul(bp[:, h * 128:(h + 1) * 128], lhsT=Mf[nxt][:, b_, :], rhs=X[:, b_, :],
                                                                             start=True, stop=True), r=[RMf[nxt][ci], RX[ci]], w=[rbp])
                    P.op(DVE, lambda e, bp=bp, ci=ci: e.tensor_tensor(out=X[:, hs[ci], :], in0=v4(bp), in1=X[:, hs[ci], :], op=ALU.add),
                         r=[rbp, RX[ci]], w=[rbp, RX[ci]])
                cur = nxt
            for ci in range(2):
                P.op(ACT, lambda e, ci=ci: e.copy(out=bt["Xb"][:, hs[ci], :], in_=X[:, hs[ci], :]), r=[RX[ci]], w=[Rb["Xb"][ci]])
                bu, rbu = self.bank()
                bw, rbw = self.bank()
                for h in range(4):
                    b_ = ci * 4 + h
                    P.op(PE, lambda e, bu=bu, h=h, b_=b_: e.matmul(bu[:, h * 128:(h + 1) * 128], lhsT=bt["Xb"][:, b_, :], rhs=bt["RHSv"][:, b_, :],
                                                                  start=True, stop=True), r=[Rb["Xb"][ci], Rb["RHSv"][ci]], w=[rbu])
                    P.op(PE, lambda e, bw=bw, h=h, b_=b_: e.matmul(bw[:, h * 128:(h + 1) * 128], lhsT=bt["RHSw"][:, b_, :], rhs=bt["Xb"][:, b_, :],
                                                                  start=True, stop=True), r=[Rb["Xb"][ci], Rb["RHSw"][ci]], w=[rbw])
                P.op(DVE, lambda e, bu=bu, ci=ci: e.tensor_copy(out=u[:, hs[ci], :], in_=v4(bu)), r=[rbu], w=[rbu, Ru[ci]])
                P.op(ACT, lambda e, bw=bw, ci=ci: e.copy(out=bt["wT"][:, hs[ci], :], in_=v4(bw)), r=[rbw], w=[rbw, Rb["wT"][ci]])
            for ci, n_ in enumerate(pair):
                cs = slice(n_ * 128, (n_ + 1) * 128)
                bws, rbws = self.bank()
                for h in range(4):
                    b_ = ci * 4 + h
                    P.op(PE, lambda e, bws=bws, h=h, b_=b_: e.matmul(bws[:, h * 128:(h + 1) * 128], lhsT=bt["wT"][:, b_, :], rhs=self.Sb[:, h, :],
                                                                    start=True, stop=True), r=[Rb["wT"][ci], self.RSb[h]], w=[rbws])
                P.op(DVE, lambda e, bws=bws, ci=ci: e.tensor_tensor(out=bt["vnew"][:], in0=u[:, hs[ci], :], in1=v4(bws), op=ALU.subtract),
                     r=[rbws, Ru[ci]], w=[rbws, Rb["vnew"]])
                bo, rbo = self.bank()
                bs_, rbs_ = self.bank()
                for h in range(4):
                    b_ = ci * 4 + h
                    P.op(PE, lambda e, bo=bo, h=h, b_=b_: e.matmul(bo[:, h * 128:(h + 1) * 128], lhsT=bt["qg"][:, b_, :], rhs=self.Sb[:, h, :],
                                                                  start=True, stop=False), r=[Rb["qg"][ci], self.RSb[h]], w=[rbo])
                    P.op(PE, lambda e, bo=bo, h=h, b_=b_: e.matmul(bo[:, h * 128:(h + 1) * 128], lhsT=bt["QKT"][:, b_, :], rhs=bt["vnew"][:, h, :],
                                                                  start=False, stop=True), r=[Rb["QKT"][ci], Rb["vnew"]], w=[rbo])
                    P.op(PE, lambda e, bs_=bs_, h=h, b_=b_: e.matmul(bs_[:, h * 128:(h + 1) * 128], lhsT=bt["ktail"][:, b_, :], rhs=bt["vnew"][:, h, :],
                                                                    start=True, stop=True), r=[Rb["ktail"][ci], Rb["vnew"]], w=[rbs_])
                for h in range(4):
                    P.op(DVE, lambda e, bs_=bs_, h=h, n_=n_: e.scalar_tensor_tensor(
                        out=self.Sf[:, h, :], in0=self.Sf[:, h, :], scalar=egl[:, n_, h:h + 1], in1=bs_[:, h * 128:(h + 1) * 128],
                        op0=ALU.mult, op1=ALU.add), r=[rbs_, Rgc, self.RS[h]], w=[rbs_, self.RS[h]])
                P.op(ACT, lambda e: e.copy(out=self.Sb[:], in_=self.Sf[:]), r=self.RS, w=self.RSb)
                flush_tail()
                for h in range(4):
                    P.op(ACT, lambda e, bo=bo, h=h: e.activation(out=sqo[:, h, :], in_=bo[:, h * 128:(h + 1) * 128], func=AF.Square,
                                                              accum_out=ssq[:, h:h + 1]), r=[rbo], w=[rbo, Rsqo, Rssq])
                P.op(DVE, lambda e: e.tensor_scalar(out=rr[:], in0=ssq[:], scalar1=1.0 / 128.0, scalar2=1e-6, op0=ALU.mult, op1=ALU.add), r=[Rssq], w=[Rssq])
                P.op(ACT, lambda e: e.activation(out=rr[:], in_=rr[:], func=AF.Ln), r=[Rssq], w=[Rssq])
                P.op(ACT, lambda e: e.activation(out=rr[:], in_=rr[:], func=AF.Exp, scale=-0.5), r=[Rssq], w=[Rssq])
                P.op(DVE, lambda e, bo=bo: e.tensor_tensor(out=bt["on"][:], in0=v4(bo), in1=bc4(rr[:, :]), op=ALU.mult),
                     r=[rbo, Rssq], w=[rbo, Rb["on"]])
                pending_tail.append(cs)
        flush_tail()
        iz = 0
        for g in range(2):
            slot, rs = zslots[g]
            for jj in range(2):
                h = 2 * g + jj
                for (c0, n) in psubs:
                    bk, rb = self.bank()
                    for kc in range(8):
                        P.op(PE, lambda e, bk=bk, kc=kc, jj=jj, c0=c0, n=n, slot=slot: e.matmul(
                            bk[:, 0:n], lhsT=slot[:, kc, jj * 128:(jj + 1) * 128], rhs=self.xb[:, kc, c0:c0 + n],
                            start=(kc == 0), stop=(kc == 7)), r=[rs, self.xbr(kc, c0)], w=[rb])
                    q = iz % 2
                    iz += 1
                    P.op(ACT, lambda e, bk=bk, q=q, n=n: e.activation(out=zsil[:, q, 0:n], in_=bk[:, 0:n], func=AF.Silu), r=[rb], w=[rb, Rzs[q]])
                    P.op(DVE, lambda e, h=h, c0=c0, n=n, q=q: e.tensor_tensor(out=oc[:, h, c0:c0 + n], in0=oc[:, h, c0:c0 + n], in1=zsil[:, q, 0:n],
                                                                          op=ALU.mult), r=[Roc[h], Rzs[q]], w=[Roc[h]])
        if t == 1:
            P.store(SP, self.o["dn_S_prompt"].rearrange("h d v -> d h v"), self.Sf[:], self.RS[3])
            for g in range(3):
                i = self.istage % 2
                self.istage += 1
                Rs = self.Rstage[i]
                bk, rb = self.bank()
                for j in range(4):
                    c = g * 4 + j
                    P.op(PE, lambda e, bk=bk, j=j, c=c: e.transpose(bk[0:3, j * 128:(j + 1) * 128], dnlast[:, c, :], self.ident_f[:]),
                         r=[Rdnl, self.Rconst], w=[rb])
                P.op(DVE, lambda e, bk=bk, i=i: e.tensor_copy(out=self.stage[0:3, i, 0:512], in_=bk[0:3, :]), r=[rb], w=[rb, Rs])
                P.store(SP, self.o["dn_conv_prompt"][:, g * 512:(g + 1) * 512], self.stage[0:3, i, 0:512], Rs)

    def gdn_samples(self, oc, Roc):
        P = self.P
        w_in = self.d["ab_w_in"]
        o = 16640
        S0 = self.scr_view(o, [128, NS, 4, 128], F32); o += 32768
        stcb = self.scr_view(o, [NS, 3, 512], F32); o += 6144
        wb = self.scr_view(o, [NS, 4, 512], F32); o += 8192
        qraw = self.scr_view(o, [NS, 1536], F32); o += 6144
        qkvs = self.scr_view(o, [NS, 1536], F32); o += 6144
        tmp = self.scr_view(o, [NS, 2, 512], F32); zs = self.scr_view(o, [NS, 512], F32); o += 4096
        qkn = self.scr_view(o, [NS, 8, 128], F32); o += 4096
        sqs = self.scr_view(o, [NS, 8, 128], F32); o += 4096
        ba = self.scr_view(o, [NS, 8], F32); o += 32
        bsg = self.scr_view(o, [NS, 3, 4], F32); o += 48
        gt = self.scr_view(o, [NS, 4], F32); o += 16
        dtbb = self.scr_view(o, [NS, 4], F32); o += 16
        negAb = self.scr_view(o, [NS, 4], F32); o += 16
        ss = self.scr_view(o, [NS, 8], F32); o += 32
        id16 = self.scr_view(o, [NS, NS], F32); o += 64
        rhsb = self.scr_view(o, [NS, 12, NS], F32); o += 768
        fm = self.scr_view(o, [128, 12, NS], F32); o += 768
        zfm = self.scr_view(o, [128, 4, NS], F32); o += 256
        bcs = self.scr_view(o, [128, 12, NS], F32); o += 768
        t1 = self.scr_view(o, [128, 4, NS], F32); o += 256
        t2 = self.scr_view(o, [128, 4, NS], F32); o += 256
        vnw = self.scr_view(o, [128, 4, NS], F32); o += 256
        oT = self.scr_view(o, [128, 4, NS], F32); o += 256
        rrb = self.scr_view(o, [128, 4, NS], F32); o += 256
        normg = self.scr_view(o, [128, 1, 1], F32); o += 16
        vdg = self.scr_view(o, [128, 2, 128], F32); o += 1024
        tS = self.scr_view(o, [128, 2, 128], F32); o += 1024
        assert o <= self.SCR
        RS0 = [self.PR(f"S0_{s}") for s in range(NS)]
        Rstc, Rwb, Rqraw, Rqkvs, Rtmp, Rqkn, Rsq, Rsm, Rfm, Rbc, Rt, Rprm = (self.PR(n) for n in (
            "stcb", "wbs", "qraws", "qkvss", "tmps", "qkns", "sqss", "smalls", "fms", "bcs", "ts", "gsprm"))
        Rvdg = [self.PR("vdg0"), self.PR("vdg1")]
        RtS = [self.PR("tS0"), self.PR("tS1")]
        for s_ in range(NS):
            P.load(SP, S0[:, s_], self.d["state_dn_S"][s_].rearrange("h d v -> d h v"), RS0[s_])
        P.load(SP, dtbb[:], self.d["dn_dt_bias"].partition_broadcast(NS).rearrange("p a h -> p (a h)"), Rprm)
        P.load(SP, negAb[:], self.d["dn_A_log"].partition_broadcast(NS).rearrange("p a h -> p (a h)"), Rprm)
        P.op(ACT, lambda e: e.activation(out=negAb[:], in_=negAb[:], func=AF.Exp), r=[Rprm], w=[Rprm])
        P.op(DVE, lambda e: e.tensor_scalar(out=negAb[:], in0=negAb[:], scalar1=-1.0, scalar2=None, op0=ALU.mult), r=[Rprm], w=[Rprm])
        self.fm_load(self.d["dn_norm_g"], 1, 128, normg, Rprm)
        P.op(DVE, lambda e: e.tensor_copy(out=id16[:], in_=self.ident_f[0:NS, 0:NS]), r=[self.Rconst], w=[Rprm])
        for g in range(6):
            slot, rs = self.wloadA(w_in, g * 256, 256)
            bk, rb = self.bank()
            for kc in range(8):
                P.op(PE, lambda e, bk=bk, kc=kc, slot=slot: e.matmul(bk[0:NS, 0:256], lhsT=self.xb[:, kc, NT:NT + NS], rhs=slot[:, kc, 0:256],
                                                                    start=(kc == 0), stop=(kc == 7)), r=[rs, self.xbr(kc, NT)], w=[rb])
            P.op(ACT, lambda e, bk=bk, g=g: e.copy(out=qraw[:, g * 256:(g + 1) * 256], in_=bk[0:NS, 0:256]), r=[rb], w=[rb, Rqraw])
        slot, rs = self.wloadA(w_in, 2048, 8)
        bk, rb = self.bank()
        for kc in range(8):
            P.op(PE, lambda e, bk=bk, kc=kc, slot=slot: e.matmul(bk[0:NS, 0:8], lhsT=self.xb[:, kc, NT:NT + NS], rhs=slot[:, kc, 0:8],
                                                                start=(kc == 0), stop=(kc == 7)), r=[rs, self.xbr(kc, NT)], w=[rb])
        P.op(DVE, lambda e, bk=bk: e.tensor_copy(out=ba[:], in_=bk[0:NS, 0:8]), r=[rb], w=[rb, Rsm])
        P.d2d(SP, self.o["dn_conv_sample"][:, 0:2, :], self.d["state_dn_conv"][:, 1:3, :], self.Rd2d)
        P.store(SP, self.o["dn_conv_sample"][:, 2, :], qraw[:, :], Rqraw)
        P.op(ACT, lambda e: e.activation(out=bsg[:, 0, :], in_=ba[:, 0:4], func=AF.Sigmoid), r=[Rsm], w=[Rsm])
        P.op(DVE, lambda e: e.tensor_tensor(out=gt[:], in0=ba[:, 4:8], in1=dtbb[:], op=ALU.add), r=[Rsm, Rprm], w=[Rsm])
        P.op(ACT, lambda e: e.activation(out=gt[:], in_=gt[:], func=AF.Exp), r=[Rsm], w=[Rsm])
        P.op(DVE, lambda e: e.tensor_scalar(out=gt[:], in0=gt[:], scalar1=1.0, scalar2=None, op0=ALU.add), r=[Rsm], w=[Rsm])
        P.op(ACT, lambda e: e.activation(out=gt[:], in_=gt[:], func=AF.Ln), r=[Rsm], w=[Rsm])
        P.op(DVE, lambda e: e.tensor_tensor(out=gt[:], in0=gt[:], in1=negAb[:], op=ALU.mult), r=[Rsm, Rprm], w=[Rsm])
        P.op(ACT, lambda e: e.activation(out=bsg[:, 1, :], in_=gt[:], func=AF.Exp), r=[Rsm], w=[Rsm])
        for cb in range(3):
            csl = slice(cb * 512, (cb + 1) * 512)
            P.load(SP, stcb[:], self.d["state_dn_conv"][:, :, csl], Rstc)
            P.load(SP, wb[:], self.d["dn_conv_w"][:, csl].partition_broadcast(NS), Rwb)
            P.op(DVE, lambda e: e.tensor_tensor(out=tmp[:, 0, :], in0=stcb[:, 0, :], in1=wb[:, 0, :], op=ALU.mult), r=[Rstc, Rwb], w=[Rtmp])
            for w in (1, 2):
                P.op(DVE, lambda e, w=w: e.tensor_tensor(out=tmp[:, 1, :], in0=stcb[:, w, :], in1=wb[:, w, :], op=ALU.mult), r=[Rstc, Rwb], w=[Rtmp])
                P.op(DVE, lambda e: e.tensor_tensor(out=tmp[:, 0, :], in0=tmp[:, 0, :], in1=tmp[:, 1, :], op=ALU.add), r=[Rtmp], w=[Rtmp])
            P.op(DVE, lambda e, csl=csl: e.tensor_tensor(out=tmp[:, 1, :], in0=qraw[:, csl], in1=wb[:, 3, :], op=ALU.mult), r=[Rqraw, Rwb], w=[Rtmp])
            P.op(DVE, lambda e: e.tensor_tensor(out=tmp[:, 0, :], in0=tmp[:, 0, :], in1=tmp[:, 1, :], op=ALU.add), r=[Rtmp], w=[Rtmp])
            P.op(ACT, lambda e, csl=csl: e.activation(out=qkvs[:, csl], in_=tmp[:, 0, :], func=AF.Silu), r=[Rtmp], w=[Rqkvs])
        P.op(ACT, lambda e: e.activation(out=sqs[:], in_=qkvs[:, 0:1024].rearrange("p (c d) -> p c d", c=8), func=AF.Square), r=[Rqkvs], w=[Rsq])
        P.op(DVE, lambda e: e.tensor_reduce(out=ss[:], in_=sqs[:], axis=AX.X, op=ALU.add), r=[Rsq], w=[Rsm])
        P.op(DVE, lambda e: e.tensor_scalar(out=ss[:], in0=ss[:], scalar1=1e-6, scalar2=None, op0=ALU.add), r=[Rsm], w=[Rsm])
        P.op(ACT, lambda e: e.activation(out=ss[:], in_=ss[:], func=AF.Ln), r=[Rsm], w=[Rsm])
        P.op(ACT, lambda e: e.activation(out=ss[:], in_=ss[:], func=AF.Exp, scale=-0.5), r=[Rsm], w=[Rsm])
        P.op(DVE, lambda e: e.tensor_scalar(out=ss[:, 0:4], in0=ss[:, 0:4], scalar1=float(128 ** -0.5), scalar2=None, op0=ALU.mult), r=[Rsm], w=[Rsm])
        P.op(DVE, lambda e: e.tensor_tensor(out=qkn[:], in0=qkvs[:, 0:1024].rearrange("p (c d) -> p c d", c=8),
                                            in1=ss[:, :].unsqueeze(2).to_broadcast([NS, 8, 128]), op=ALU.mult), r=[Rqkvs, Rsm], w=[Rqkn])
        P.op(DVE, lambda e: e.tensor_tensor(out=sqs[:, 0:4, :], in0=qkn[:, 0:4, :], in1=qkn[:, 4:8, :], op=ALU.mult), r=[Rqkn], w=[Rsq])
        P.op(DVE, lambda e: e.tensor_reduce(out=bsg[:, 2, :], in_=sqs[:, 0:4, :], axis=AX.X, op=ALU.add), r=[Rsq], w=[Rsm])
        P.op(DVE, lambda e: e.tensor_tensor(out=rhsb[:], in0=id16[:, :].unsqueeze(1).to_broadcast([NS, 12, NS]),
                                            in1=bsg[:].rearrange("p a h -> p (a h)").unsqueeze(2).to_broadcast([NS, 12, NS]), op=ALU.mult),
             r=[Rsm, Rprm], w=[Rsm])
        bk, rb = self.bank()
        P.op(PE, lambda e, bk=bk: e.matmul(bk[:, 0:12 * NS], lhsT=self.ones_f[0:NS, :], rhs=rhsb[:].rearrange("p a s -> p (a s)"), start=True, stop=True),
             r=[Rsm, self.Rconst], w=[rb])
        P.op(DVE, lambda e, bk=bk: e.tensor_copy(out=bcs[:], in_=bk[:, 0:12 * NS].rearrange("p (a s) -> p a s", a=12)), r=[rb], w=[rb, Rbc])
        bk, rb = self.bank()
        for c in range(12):
            src = qkn[:, c, :] if c < 8 else qkvs[:, c * 128:(c + 1) * 128]
            P.op(PE, lambda e, bk=bk, c=c, src=src: e.transpose(bk[:, c * NS:(c + 1) * NS], src, self.ident_f[0:NS, 0:NS]),
                 r=[Rqkn, Rqkvs, self.Rconst], w=[rb])
        P.op(DVE, lambda e, bk=bk: e.tensor_copy(out=fm[:], in_=bk[:, 0:12 * NS].rearrange("p (c s) -> p c s", c=12)), r=[rb], w=[rb, Rfm])
        for g in range(2):
            slot, rs = self.wloadA(w_in, 1536 + g * 256, 256)
            bk, rb = self.bank()
            for kc in range(8):
                P.op(PE, lambda e, bk=bk, kc=kc, slot=slot: e.matmul(bk[0:NS, 0:256], lhsT=self.xb[:, kc, NT:NT + NS], rhs=slot[:, kc, 0:256],
                                                                    start=(kc == 0), stop=(kc == 7)), r=[rs, self.xbr(kc, NT)], w=[rb])
            P.op(ACT, lambda e, bk=bk, g=g: e.activation(out=zs[:, g * 256:(g + 1) * 256], in_=bk[0:NS, 0:256], func=AF.Silu), r=[rb], w=[rb, Rtmp])
        bk, rb = self.bank()
        for h in range(4):
            P.op(PE, lambda e, bk=bk, h=h: e.transpose(bk[:, h * NS:(h + 1) * NS], zs[:, h * 128:(h + 1) * 128], self.ident_f[0:NS, 0:NS]),
                 r=[Rtmp, self.Rconst], w=[rb])
        P.op(DVE, lambda e, bk=bk: e.tensor_copy(out=zfm[:], in_=bk[:, 0:4 * NS].rearrange("p (c s) -> p c s", c=4)), r=[rb], w=[rb, Rfm])
        kq, rkq = self.bank(reserve=True)
        for s_ in range(NS):
            for h in range(4):
                for a, c in ((0, 4 + h), (1, h)):
                    col = (a * 4 + h) * NS + s_
                    P.op(PE, lambda e, s_=s_, h=h, c=c, col=col: e.matmul(kq[:, col:col + 1], lhsT=S0[:, s_, h, :], rhs=fm[:, c, s_:s_ + 1],
                                                                        start=True, stop=True), r=[RS0[s_], Rfm], w=[rkq])
        kS = kq[:, 0:4 * NS].rearrange("p (h s) -> p h s", h=4)
        qS = kq[:, 4 * NS:8 * NS].rearrange("p (h s) -> p h s", h=4)
        bb, egb, qkb = bcs[:, 0:4, :], bcs[:, 4:8, :], bcs[:, 8:12, :]
        P.op(DVE, lambda e: e.tensor_tensor(out=t1[:], in0=kS, in1=egb, op=ALU.mult), r=[rkq, Rbc], w=[rkq, Rt])
        P.op(DVE, lambda e: e.tensor_tensor(out=t1[:], in0=fm[:, 8:12, :], in1=t1[:], op=ALU.subtract), r=[Rfm, Rt], w=[Rt])
        P.op(DVE, lambda e: e.tensor_tensor(out=vnw[:], in0=t1[:], in1=bb, op=ALU.mult), r=[Rt, Rbc], w=[Rt])
        P.op(DVE, lambda e: e.tensor_tensor(out=t2[:], in0=qS, in1=egb, op=ALU.mult), r=[rkq, Rbc], w=[rkq, Rt])
        P.op(DVE, lambda e: e.tensor_tensor(out=t1[:], in0=vnw[:], in1=qkb, op=ALU.mult), r=[Rt, Rbc], w=[Rt])
        P.op(DVE, lambda e: e.tensor_tensor(out=oT[:], in0=t1[:], in1=t2[:], op=ALU.add), r=[Rt], w=[Rt])
        self.release(kq)
        P.op(ACT, lambda e: e.activation(out=t1[:], in_=oT[:], func=AF.Square), r=[Rt], w=[Rt])
        bk, rb = self.bank()
        P.op(PE, lambda e, bk=bk: e.matmul(bk[:, 0:4 * NS], lhsT=self.ones_f[:], rhs=t1[:].rearrange("p h s -> p (h s)"), start=True, stop=True),
             r=[Rt, self.Rconst], w=[rb])
        P.op(DVE, lambda e, bk=bk: e.tensor_scalar(out=rrb[:], in0=bk[:, 0:4 * NS].rearrange("p (h s) -> p h s", h=4), scalar1=1.0 / 128.0, scalar2=1e-6,
                                                  op0=ALU.mult, op1=ALU.add), r=[rb], w=[rb, Rt])
        P.op(ACT, lambda e: e.activation(out=rrb[:], in_=rrb[:], func=AF.Ln), r=[Rt], w=[Rt])
        P.op(ACT, lambda e: e.activation(out=rrb[:], in_=rrb[:], func=AF.Exp, scale=-0.5), r=[Rt], w=[Rt])
        P.op(DVE, lambda e: e.tensor_tensor(out=t2[:], in0=oT[:], in1=rrb[:], op=ALU.mult), r=[Rt], w=[Rt])
        P.op(DVE, lambda e: e.tensor_tensor(out=t2[:], in0=t2[:], in1=zfm[:], op=ALU.mult), r=[Rt, Rfm], w=[Rt])
        P.op(DVE, lambda e: e.tensor_scalar(out=oc[:, 0:4, NT:NT + NS], in0=t2[:], scalar1=normg[:, 0, 0:1], scalar2=None, op0=ALU.mult),
             r=[Rt, Rprm], w=Roc[0:4])
        ip = 0
        for s_ in range(NS):
            for h in range(4):
                q = ip % 2
                ip += 1
                P.op(DVE, lambda e, q=q, h=h, s_=s_: e.tensor_scalar(out=vdg[:, q, :], in0=self.ident_f[:], scalar1=vnw[:, h, s_:s_ + 1], scalar2=None,
                                                                  op0=ALU.mult), r=[Rt, self.Rconst], w=[Rvdg[q]])
                bk, rb = self.bank()
                P.op(PE, lambda e, bk=bk, q=q: e.matmul(bk[:, 0:128], lhsT=self.ones_f[:], rhs=vdg[:, q, :], start=True, stop=True),
                     r=[Rvdg[q], self.Rconst], w=[rb])
                P.op(DVE, lambda e, q=q, h=h, s_=s_: e.tensor_scalar(out=tS[:, q, :], in0=S0[:, s_, h, :], scalar1=egb[:, h, s_:s_ + 1], scalar2=None,
                                                                  op0=ALU.mult), r=[RS0[s_], Rbc], w=[RtS[q]])
                P.op(DVE, lambda e, bk=bk, q=q, h=h, s_=s_: e.scalar_tensor_tensor(out=S0[:, s_, h, :], in0=bk[:, 0:128], scalar=fm[:, 4 + h, s_:s_ + 1],
                                                                                in1=tS[:, q, :], op0=ALU.mult, op1=ALU.add),
                     r=[rb, Rfm, RtS[q]], w=[rb, RS0[s_]])
            P.store(SP, self.o["dn_S_sample"][s_].rearrange("h d v -> d h v"), S0[:, s_], RS0[s_])

    def build(self):
        cfg = self.cfg
        self.declare_io()
        self.alloc_common()
        self.ln_off = 46 * 1024
        self.sg_off = 70 * 1024
        self.Rln = [Res(n) for n in ("zb", "zsq", "mean", "rstd")]
        self.Rlnt = [Res("lnt0"), Res("lnt1")]
        self.isg = 0
        self.build_consts()
        stages = cfg.get("stages", "all")
        on = lambda nm: stages == "all" or nm in stages
        if on("memkv") or on("xattn0") or on("xattn1"):
            self.mem_kv()
            self.phase_barrier()
        for t in cfg.get("tiles", (0, 1)):
            self.load_x(t)
            for l in range(2):
                if on(f"ffn{l}0"):
                    self.ffn(t, l, 0)
                    self.phase_barrier()
                if on(f"mix{l}"):
                    self.mixer(t, l)
                    self.phase_barrier()
                if on(f"xattn{l}"):
                    self.xattn(t, l)
                    self.phase_barrier()
                if on(f"ffn{l}1"):
                    self.ffn(t, l, 1)
                    self.phase_barrier()
            self.store_y(t)
        stats = self.P.emit()
        self.es.close()
        return stats


_OUT_NAMES = ["y_prompt", "y_sample", "mem_k", "mem_v", "dn_S_prompt", "dn_conv_prompt", "cc_conv_prompt",
              "dn_S_sample", "dn_conv_sample", "cc_conv_sample", "gm_v_sample"]


def make_in_maps(inputs, cores):
    f = lambda a: np.ascontiguousarray(np.asarray(a, dtype=np.float32))
    shared = {}
    shared["ln_g"] = f(inputs["ln_g"]).reshape(8, D)
    shared["ln_b"] = f(inputs["ln_b"]).reshape(8, D)
    shared["ffn_w_gate"] = f(inputs["ffn_w_gate"]).reshape(4, D, DFF)
    shared["ffn_w_up"] = f(inputs["ffn_w_up"]).reshape(4, D, DFF)
    shared["ffn_w_down"] = f(inputs["ffn_w_down"]).reshape(4, DFF, D)
    for n in ("xa_wq", "xa_wk", "xa_wv", "xa_wo", "ab_w_in", "dn_conv_w", "cc_conv_w", "ab_w_out", "gm_w_in", "gm_w_s",
              "gm_b_s", "gm_w_out"):
        shared[n] = f(inputs[n])
    for n in ("dn_A_log", "dn_dt_bias", "dn_norm_g", "cc_conv_b", "cc_ln_g", "cc_ln_b", "gm_ln_g", "gm_ln_b"):
        shared[n] = f(inputs[n]).reshape(1, -1)
    maps = []
    for c in cores:
        m = dict(shared)
        m["x_prompt"] = f(inputs["x_prompt"][c])
        m["x_sample"] = f(inputs["x_sample"][NS * c:NS * (c + 1), 0])
        m["mem_prompt"] = f(inputs["mem_prompt"][c])
        m["cache_mem_k"] = f(inputs["cache_mem_k"][:, NS * c:NS * (c + 1)]).reshape(2, NS, NMEM, D)
        m["cache_mem_v"] = f(inputs["cache_mem_v"][:, NS * c:NS * (c + 1)]).reshape(2, NS, NMEM, D)
        m["state_dn_S"] = f(inputs["state_dn_S"][NS * c:NS * (c + 1)])
        m["state_dn_conv"] = f(inputs["state_dn_conv"][NS * c:NS * (c + 1)])
        m["state_cc_conv"] = f(inputs["state_cc_conv"][NS * c:NS * (c + 1)])
        maps.append(m)
    return maps


def assemble(results):
    n = len(results)
    y_prompt = np.stack([r["y_prompt"] for r in results])
    y_sample = np.concatenate([r["y_sample"] for r in results])[:, None, :]
    mem_k = np.stack([r["mem_k"] for r in results], axis=1).reshape(2, n, NMEM, 4, 256)
    mem_v = np.stack([r["mem_v"] for r in results], axis=1).reshape(2, n, NMEM, 4, 256)
    dn_S_p = np.stack([r["dn_S_prompt"] for r in results])
    dn_c_p = np.stack([r["dn_conv_prompt"] for r in results])
    cc_c_p = np.stack([r["cc_conv_prompt"] for r in results])
    dn_S_s = np.concatenate([r["dn_S_sample"] for r in results])
    dn_c_s = np.concatenate([r["dn_conv_sample"] for r in results])
    cc_c_s = np.concatenate([r["cc_conv_sample"] for r in results])
    gm_v = np.concatenate([r["gm_v_sample"] for r in results])[:, None, :]
    outs = (y_prompt, y_sample, mem_k, mem_v, dn_S_p, dn_c_p, cc_c_p, dn_S_s, dn_c_s, cc_c_s, gm_v)
    return tuple(np.ascontiguousarray(o, dtype=np.float32) for o in outs)


def kernel(**inputs):
    kb = KB({})
    kb.build()
    cores = list(range(8))
    res = run_bass_kernel_spmd(kb.nc, make_in_maps(inputs, cores), core_ids=cores)
    return assemble(res.results)
```

```python
import numpy as np
from contextlib import ExitStack
import concourse.bass as bass
import concourse.mybir as mybir
from concourse.bass_utils import run_bass_kernel_spmd

F32 = mybir.dt.float32
BF16 = mybir.dt.bfloat16
AF = mybir.ActivationFunctionType
ALU = mybir.AluOpType
AX = mybir.AxisListType

PE, ACT, DVE, POOL, SP = "pe", "act", "dve", "pool", "sp"
ENGS = (PE, ACT, DVE, POOL, SP)

D = 1024
SEQ = 2048
NT = 1024
NS = 16
NC = NT + NS
DFF = 2816
NJ = 22
NMEM = 256
ALPHA = float(4 ** 0.25)
LN_EPS = 1e-5


class Res:
    __slots__ = ("name", "lw", "rd", "dsem", "dcnt", "drd", "pre")

    def __init__(self, name):
        self.pre = None
        self.name = name
        self.lw = None
        self.rd = {}
        self.dsem = None
        self.dcnt = 0
        self.drd = 0


class Op:
    __slots__ = ("eng", "fn", "waits", "signal", "dres")

    def __init__(self, eng, fn, waits, dres=None):
        self.eng = eng
        self.fn = fn
        self.waits = waits
        self.signal = False
        self.dres = dres


class Prog:
    def __init__(self, nc):
        self.nc = nc
        self.streams = {e: [] for e in ENGS}
        self.dma_res = []

    def _deps(self, eng, r, w):
        waits = []
        for res in r:
            if res.pre:
                waits.extend(res.pre)
            lw = res.lw
            if lw is not None:
                if lw[0] == "e":
                    if not (lw[1] == eng and eng == PE):
                        waits.append(lw)
                else:
                    waits.append(lw)
        for res in w:
            if res.pre:
                waits.extend(res.pre)
                res.pre = None
            lw = res.lw
            if lw is not None:
                if lw[0] == "e":
                    if lw[1] != eng:
                        waits.append(lw)
                else:
                    waits.append(lw)
            for e2, i2 in res.rd.items():
                if e2 != eng:
                    waits.append(("e", e2, i2))
            if res.drd:
                waits.append(("d", res, res.drd))
        return waits

    def op(self, eng, fn, r=(), w=()):
        idx = len(self.streams[eng])
        waits = self._deps(eng, r, w)
        for res in r:
            res.rd[eng] = idx
        for res in w:
            res.lw = ("e", eng, idx)
            res.rd = {}
            res.drd = 0
        self.streams[eng].append(Op(eng, fn, waits))

    def _dsem(self, res):
        if res.dsem is None:
            res.dsem = self.nc.alloc_semaphore("d_" + res.name)
            self.dma_res.append(res)
        return res.dsem

    def load(self, eng, out, in_, res, **kw):
        self._dsem(res)
        waits = self._deps(eng, (), (res,))
        res.dcnt += 16
        res.lw = ("d", res, res.dcnt)
        res.rd = {}
        res.drd = 0
        self.streams[eng].append(
            Op(eng, lambda e: e.dma_start(out=out, in_=in_, **kw), waits, dres=res))

    def store(self, eng, out, in_, res, **kw):
        self._dsem(res)
        waits = self._deps(eng, (res,), ())
        res.dcnt += 16
        res.drd = res.dcnt
        self.streams[eng].append(
            Op(eng, lambda e: e.dma_start(out=out, in_=in_, **kw), waits, dres=res))

    def d2d(self, eng, out, in_, res, **kw):
        self._dsem(res)
        res.dcnt += 16
        res.drd = res.dcnt
        self.streams[eng].append(
            Op(eng, lambda e: e.dma_start(out=out, in_=in_, **kw), [], dres=res))

    def emit(self, final_eng=SP):
        nc = self.nc
        fin = [("d", res, res.dcnt) for res in self.dma_res]
        self.streams[final_eng].append(Op(final_eng, None, fin))
        for e in ENGS:
            for op in self.streams[e]:
                for wt in op.waits:
                    if wt[0] == "e":
                        self.streams[wt[1]][wt[2]].signal = True
        cnt = {}
        for e in ENGS:
            c = 0
            arr = []
            for op in self.streams[e]:
                if op.signal:
                    c += 1
                arr.append(c)
            cnt[e] = arr
        sems = {e: nc.alloc_semaphore("s_" + e) for e in ENGS}
        stats = {e: [0, 0, (cnt[e][-1] if cnt[e] else 0)] for e in ENGS}
        stats['maxdma'] = max([r.dcnt for r in self.dma_res] + [0])

        def run(eng_name, engine):
            waited_e = {}
            waited_d = {}
            for op in self.streams[eng_name]:
                need_e = {}
                need_d = {}
                for wt in op.waits:
                    if wt[0] == "e":
                        c = cnt[wt[1]][wt[2]]
                        if c > need_e.get(wt[1], 0):
                            need_e[wt[1]] = c
                    else:
                        if wt[2] > need_d.get(wt[1], 0):
                            need_d[wt[1]] = wt[2]
                for e2, c in need_e.items():
                    if waited_e.get(e2, 0) < c:
                        engine.wait_ge(sems[e2], c)
                        waited_e[e2] = c
                        stats[eng_name][1] += 1
                for res, c in need_d.items():
                    if waited_d.get(res, 0) < c:
                        engine.wait_ge(res.dsem, c)
                        waited_d[res] = c
                        stats[eng_name][1] += 1
                if op.fn is None:
                    continue
                ins = op.fn(engine)
                stats[eng_name][0] += 1
                if op.signal:
                    ins.then_inc(sems[eng_name], 1)
                if op.dres is not None:
                    ins.then_inc(op.dres.dsem, 16)

        with nc.Block() as blk:
            @blk.tensor
            def _(e):
                run(PE, e)

            @blk.scalar
            def _(e):
                run(ACT, e)

            @blk.vector
            def _(e):
                run(DVE, e)

            @blk.gpsimd
            def _(e):
                run(POOL, e)

            @blk.sync
            def _(e):
                run(SP, e)
        return stats


class KB:
    def __init__(self, cfg):
        self.cfg = cfg
        self.nc = bass.Bass("TRN2", target_bir_lowering=False)
        self.P = Prog(self.nc)
        self.es = ExitStack()
        self.nbank = 0
        self.reserved = set()
        self.uid = 0

    def sb(self, name, shape, dt):
        return self.es.enter_context(self.nc.sbuf_tensor(name, list(shape), dt))

    def din(self, name, shape):
        return self.nc.dram_tensor(name, list(shape), F32, kind="ExternalInput").ap()

    def dout(self, name, shape):
        return self.nc.dram_tensor(name, list(shape), F32, kind="ExternalOutput").ap()

    def R(self, name):
        self.uid += 1
        return Res(f"{name}{self.uid}")

    def bank(self, reserve=False):
        while True:
            i = self.nbank % 8
            self.nbank += 1
            if i not in self.reserved:
                break
        if reserve:
            self.reserved.add(i)
        return self.banks[i], self.bres[i]

    def xfr(self, kc, c0):
        return self.Rxf[kc][min(c0 // 512, 2)]

    def xbr(self, kc, c0):
        return self.Rxb[kc][min(c0 // 512, 2)]

    def release(self, bk):
        self.reserved.discard(self.banks.index(bk))

    def declare_io(self):
        d = {}
        d["x_prompt"] = self.din("x_prompt", [SEQ, D])
        d["x_sample"] = self.din("x_sample", [NS, D])
        d["mem_prompt"] = self.din("mem_prompt", [NMEM, D])
        d["cache_mem_k"] = self.din("cache_mem_k", [2, NS, NMEM, D])
        d["cache_mem_v"] = self.din("cache_mem_v", [2, NS, NMEM, D])
        d["state_dn_S"] = self.din("state_dn_S", [NS, 4, 128, 128])
        d["state_dn_conv"] = self.din("state_dn_conv", [NS, 3, 1536])
        d["state_cc_conv"] = self.din("state_cc_conv", [NS, 30, 512])
        d["ln_g"] = self.din("ln_g", [8, D])
        d["ln_b"] = self.din("ln_b", [8, D])
        d["ffn_w_gate"] = self.din("ffn_w_gate", [4, D, DFF])
        d["ffn_w_up"] = self.din("ffn_w_up", [4, D, DFF])
        d["ffn_w_down"] = self.din("ffn_w_down", [4, DFF, D])
        for n in ("xa_wq", "xa_wk", "xa_wv", "xa_wo"):
            d[n] = self.din(n, [2, D, D])
        d["ab_w_in"] = self.din("ab_w_in", [D, 3080])
        d["dn_conv_w"] = self.din("dn_conv_w", [4, 1536])
        d["dn_A_log"] = self.din("dn_A_log", [1, 4])
        d["dn_dt_bias"] = self.din("dn_dt_bias", [1, 4])
        d["dn_norm_g"] = self.din("dn_norm_g", [1, 128])
        d["cc_conv_w"] = self.din("cc_conv_w", [31, 512])
        d["cc_conv_b"] = self.din("cc_conv_b", [1, 512])
        d["cc_ln_g"] = self.din("cc_ln_g", [1, 512])
        d["cc_ln_b"] = self.din("cc_ln_b", [1, 512])
        d["ab_w_out"] = self.din("ab_w_out", [D, D])
        d["gm_w_in"] = self.din("gm_w_in", [D, 2048])
        d["gm_ln_g"] = self.din("gm_ln_g", [1, D])
        d["gm_ln_b"] = self.din("gm_ln_b", [1, D])
        d["gm_w_s"] = self.din("gm_w_s", [4, 128, 128])
        d["gm_b_s"] = self.din("gm_b_s", [4, 128])
        d["gm_w_out"] = self.din("gm_w_out", [D, D])
        o = {}
        o["y_prompt"] = self.dout("y_prompt", [SEQ, D])
        o["y_sample"] = self.dout("y_sample", [NS, D])
        o["mem_k"] = self.dout("mem_k", [2, NMEM, D])
        o["mem_v"] = self.dout("mem_v", [2, NMEM, D])
        o["dn_S_prompt"] = self.dout("dn_S_prompt", [4, 128, 128])
        o["dn_conv_prompt"] = self.dout("dn_conv_prompt", [3, 1536])
        o["cc_conv_prompt"] = self.dout("cc_conv_prompt", [30, 512])
        o["dn_S_sample"] = self.dout("dn_S_sample", [NS, 4, 128, 128])
        o["dn_conv_sample"] = self.dout("dn_conv_sample", [NS, 3, 1536])
        o["cc_conv_sample"] = self.dout("cc_conv_sample", [NS, 30, 512])
        o["gm_v_sample"] = self.dout("gm_v_sample", [NS, D])
        self.d = d
        self.o = o

    def alloc_common(self):
        nc = self.nc
        self.banks = [self.es.enter_context(nc.psum_tensor(f"bank{i}", [128, 512], F32)) for i in range(8)]
        self.bres = [Res(f"bank{i}") for i in range(8)]
        self.xf = self.sb("xf", [128, 8, NC], F32)
        self.xb = self.sb("xb", [128, 8, NC], BF16)
        self.Rxf = [[Res(f"xf{c}_{si}") for si in range(3)] for c in range(8)]
        self.Rxb = [[Res(f"xb{c}_{si}") for si in range(3)] for c in range(8)]
        self.NA = 5
        self.ringA = [self.sb(f"wa{i}", [128, 8, 256], BF16) for i in range(self.NA)]
        self.RA = [Res(f"wa{i}") for i in range(self.NA)]
        self.iA = 0
        self.NB = 2
        self.ringB = [self.sb(f"wb{i}", [128, NJ, 128], BF16) for i in range(self.NB)]
        self.RB = [Res(f"wb{i}") for i in range(self.NB)]
        self.iB = 0
        self.SCR = 96 * 1024
        self.scr = self.sb("scr", [128, self.SCR // 4], F32)
        self.ident_f = self.sb("ident_f", [128, 128], F32)
        self.ident_b = self.sb("ident_b", [128, 128], BF16)
        self.onesD_b = self.sb("onesD_b", [128, 128], BF16)
        self.ones_b = self.sb("ones_b", [128, 128], BF16)
        self.ones_f = self.sb("ones_f", [128, 128], F32)
        self.Rconst = Res("const")
        self.lng = self.sb("lng", [128, 8, 8], F32)
        self.lnb = self.sb("lnb", [128, 8, 8], F32)
        self.KT = [self.sb(f"KT{l}", [128, 8, NMEM], BF16) for l in range(2)]
        self.Vb = [self.sb(f"Vb{l}", [128, 2, D], BF16) for l in range(2)]
        self.RKT = [Res("KT0"), Res("KT1")]
        self.RVb = [Res("Vb0"), Res("Vb1")]
        self.phase_res = []
        self.pending_pre = []
        self.ones512_b = self.sb("ones512_b", [128, 128], BF16)
        self.cc_halo = self.sb("cc_halo", [128, 4, 30], BF16)
        self.Rcch = Res("cc_halo")
        self.Rd2d = Res("d2d")
        self.Rbar = Res("barrier")
        self.Sf = self.sb("Sf", [128, 4, 128], F32)
        self.Sb = self.sb("Sb", [128, 4, 128], BF16)
        self.RS = [Res(f"S{h}") for h in range(4)]
        self.RSb = [Res(f"Sb{h}") for h in range(4)]
        self.dn_halo = self.sb("dn_halo", [128, 12, 3], BF16)
        self.Rdnh = Res("dn_halo")
        self.mask_c = self.sb("mask_c", [128, 128], F32)
        self.mask_s = self.sb("mask_s", [128, 128], F32)
        self.sel127 = self.sb("sel127", [128, 128], F32)
        self.stage = self.sb("stage", [128, 2, D], F32)
        self.Rstage = [Res("stage0"), Res("stage1")]
        self.istage = 0

    def scr_view(self, off_bytes, shape, dt):
        esz = 4 if dt == F32 else 2
        n = 1
        for s in shape[1:]:
            n *= s
        assert off_bytes % 4 == 0 and off_bytes + n * esz <= self.SCR, (off_bytes, shape)
        flat = self.scr[0:shape[0], off_bytes // 4: off_bytes // 4 + (n * esz + 3) // 4]
        if dt != F32:
            flat = flat.bitcast(dt)[:, 0:n]
        if len(shape) == 2:
            return flat
        names = " ".join(f"a{i}" for i in range(len(shape) - 1))
        kw = {f"a{i}": shape[i + 1] for i in range(len(shape) - 1)}
        return flat.rearrange(f"p ({names}) -> p {names}", **kw)

    def build_consts(self):
        P = self.P
        Rc = self.Rconst
        idf, idb = self.ident_f, self.ident_b
        P.op(POOL, lambda e: e.memset(idf[:], 1.0), w=[Rc])
        P.op(POOL, lambda e: e.affine_select(out=idf[:], in_=idf[:], pattern=[[1, 128]], compare_op=ALU.is_equal,
                                            fill=0.0, base=0, channel_multiplier=-1), r=[Rc], w=[Rc])
        P.op(POOL, lambda e: e.tensor_copy(out=idb[:], in_=idf[:]), r=[Rc], w=[Rc])
        P.op(POOL, lambda e: e.memset(self.onesD_b[:], 1.0 / D), w=[Rc])
        P.op(POOL, lambda e: e.memset(self.ones_b[:], 1.0), w=[Rc])
        P.op(POOL, lambda e: e.memset(self.ones_f[:], 1.0), w=[Rc])
        P.op(POOL, lambda e: e.memset(self.ones512_b[:], 1.0 / 512), w=[Rc])
        P.op(POOL, lambda e: e.memset(self.cc_halo[:], 0.0), w=[self.Rcch])
        P.op(POOL, lambda e: e.memset(self.dn_halo[:], 0.0), w=[self.Rdnh])
        P.op(POOL, lambda e: e.memset(self.Sf[:], 0.0), w=self.RS)
        P.op(POOL, lambda e: e.memset(self.Sb[:], 0.0), w=self.RSb)
        for tile_, base_ in ((self.mask_c, 0), (self.mask_s, -1)):
            P.op(POOL, lambda e, tile_=tile_: e.memset(tile_[:], 1.0), w=[Rc])
            P.op(POOL, lambda e, tile_=tile_, base_=base_: e.affine_select(
                out=tile_[:], in_=tile_[:], pattern=[[1, 128]], compare_op=ALU.is_ge, fill=0.0, base=base_,
                channel_multiplier=-1), r=[Rc], w=[Rc])
        P.op(POOL, lambda e: e.memset(self.sel127[:], 1.0), w=[Rc])
        P.op(POOL, lambda e: e.affine_select(out=self.sel127[:], in_=self.sel127[:], pattern=[[0, 128]], compare_op=ALU.is_ge,
                                            fill=0.0, base=-127, channel_multiplier=1), r=[Rc], w=[Rc])
        self.fm_load(self.d["ln_g"], 8, D, self.lng, Rc)
        self.fm_load(self.d["ln_b"], 8, D, self.lnb, Rc)

    def fm_load(self, rows_ap, r, n, dest, rdest):
        P = self.P
        st = self.stage
        i = self.istage % 2
        self.istage += 1
        Rs = self.Rstage[i]
        nchunk = n // 128
        done = 0
        while done < n:
            w = min(D, n - done)
            P.load(SP, st[0:r, i, 0:w], rows_ap[:, done:done + w], Rs)
            for c0 in range(0, w // 128, 4):
                nb = min(4, w // 128 - c0)
                bk, rb = self.bank()
                for j in range(nb):
                    P.op(PE, lambda e, bk=bk, j=j, c0=c0, i=i: e.transpose(
                        bk[:, j * 128: j * 128 + r], st[0:r, i, (c0 + j) * 128:(c0 + j + 1) * 128], self.ident_f[0:r, 0:r]),
                        r=[Rs, self.Rconst], w=[rb])
                gc = done // 128 + c0
                P.op(DVE, lambda e, bk=bk, nb=nb, gc=gc: e.tensor_copy(
                    out=dest[:, gc:gc + nb, 0:r],
                    in_=bk[:, 0:nb * 128].rearrange("p (j t) -> p j t", j=nb)[:, :, 0:r]), r=[rb], w=[rb, rdest])
            done += w

    def wloadA(self, w2d, col0, ncols):
        i = self.iA % self.NA
        self.iA += 1
        slot, res = self.ringA[i], self.RA[i]
        self.P.load(POOL, slot[:, :, 0:ncols], w2d.rearrange("(kc p) n -> p kc n", p=128)[:, :, col0:col0 + ncols], res)
        return slot, res

    def wloadB(self, w2d, col0):
        i = self.iB % self.NB
        self.iB += 1
        slot, res = self.ringB[i], self.RB[i]
        self.P.load(POOL, slot[:], w2d.rearrange("(jc p) n -> p jc n", p=128)[:, :, col0:col0 + 128], res)
        return slot, res

    def subtiles(self, t):
        s = [(0, 512), (512, 512)]
        if t == 1:
            s.append((NT, NS))
        return s

    def load_x(self, t):
        P = self.P
        st = self.stage
        for tb in range(NT // 128):
            i = self.istage % 2
            self.istage += 1
            Rs = self.Rstage[i]
            r0 = t * NT + tb * 128
            P.load(SP, st[:, i, :], self.d["x_prompt"][r0:r0 + 128, :], Rs)
            self._tr_in(st[:, i, :], 128, tb * 128, Rs)
        if t == 1:
            i = self.istage % 2
            self.istage += 1
            Rs = self.Rstage[i]
            P.load(SP, st[0:NS, i, :], self.d["x_sample"][:, :], Rs)
            self._tr_in(st[0:NS, i, :], NS, NT, Rs)

    def _tr_in(self, src, r, col0, Rs):
        P = self.P
        for g in range(2):
            bk, rb = self.bank()
            for j in range(4):
                kc = g * 4 + j
                P.op(PE, lambda e, bk=bk, j=j, kc=kc: e.transpose(
                    bk[:, j * 128: j * 128 + r], src[:, kc * 128:(kc + 1) * 128], self.ident_f[0:r, 0:r]),
                    r=[Rs, self.Rconst], w=[rb])
            pv = bk[:].rearrange("p (j t) -> p j t", j=4)[:, :, 0:r]
            wr = [self.xfr(g * 4 + j, col0) for j in range(4)]
            wb = [self.xbr(g * 4 + j, col0) for j in range(4)]
            P.op(DVE, lambda e, pv=pv, g=g: e.tensor_copy(out=self.xf[:, g * 4:(g + 1) * 4, col0:col0 + r], in_=pv),
                 r=[rb], w=[rb] + wr)
            P.op(ACT, lambda e, pv=pv, g=g: e.copy(out=self.xb[:, g * 4:(g + 1) * 4, col0:col0 + r], in_=pv),
                 r=[rb], w=[rb] + wb)

    def store_y(self, t):
        P = self.P
        st = self.stage
        blocks = [(tb * 128, 128, self.o["y_prompt"][t * NT + tb * 128: t * NT + (tb + 1) * 128, :]) for tb in range(NT // 128)]
        if t == 1:
            blocks.append((NT, NS, self.o["y_sample"][:, :]))
        for col0, r, dst in blocks:
            i = self.istage % 2
            self.istage += 1
            Rs = self.Rstage[i]
            for g in range(2):
                bk, rb = self.bank()
                for j in range(4):
                    kc = g * 4 + j
                    P.op(PE, lambda e, bk=bk, j=j, kc=kc, col0=col0, r=r: e.transpose(
                        bk[0:r, j * 128:(j + 1) * 128], self.xf[:, kc, col0:col0 + r], self.ident_f[:, :]),
                        r=[self.xfr(kc, col0), self.Rconst], w=[rb])
                P.op(DVE if g == 0 else ACT, (lambda e, bk=bk, g=g, i=i, r=r: e.tensor_copy(out=st[0:r, i, g * 512:(g + 1) * 512], in_=bk[0:r, :]))
                     if g == 0 else (lambda e, bk=bk, g=g, i=i, r=r: e.copy(out=st[0:r, i, g * 512:(g + 1) * 512], in_=bk[0:r, :])),
                     r=[rb], w=[rb, Rs])
            P.store(SP, dst, st[0:r, i, :], Rs)

    def layer_norm(self, t, li, subs=None, parts=(1, 2)):
        P = self.P
        if subs is None:
            subs = self.subtiles(t)

        def finish(kc, c0, n, tt, Rt):
            P.op(ACT, lambda e: e.activation(out=self.xf[:, kc, c0:c0 + n], in_=tt, func=AF.Identity,
                                             scale=self.lng[:, kc, li:li + 1], bias=self.lnb[:, kc, li:li + 1]),
                 r=[Rt, self.Rconst], w=[self.xfr(kc, c0)])
            P.op(ACT, lambda e: e.activation(out=self.xb[:, kc, c0:c0 + n], in_=tt, func=AF.Identity,
                                             scale=self.lng[:, kc, li:li + 1], bias=self.lnb[:, kc, li:li + 1]),
                 r=[Rt, self.Rconst], w=[self.xbr(kc, c0)])

        def finish_small(c0, n, tall, Rts):
            gb = self.lng[:, :, li:li + 1].to_broadcast([128, 8, n])
            bb = self.lnb[:, :, li:li + 1].to_broadcast([128, 8, n])
            wx = [self.xfr(kc, c0) for kc in range(8)]
            wb = [self.xbr(kc, c0) for kc in range(8)]
            P.op(DVE, lambda e: e.tensor_tensor(out=tall, in0=tall, in1=gb, op=ALU.mult), r=Rts + [self.Rconst], w=Rts)
            P.op(DVE, lambda e: e.tensor_tensor(out=self.xf[:, :, c0:c0 + n], in0=tall, in1=bb, op=ALU.add), r=Rts + [self.Rconst], w=wx)
            P.op(ACT, lambda e: e.copy(out=self.xb[:, :, c0:c0 + n], in_=self.xf[:, :, c0:c0 + n]), r=wx, w=wb)

        self.ln_fm(8, lambda kc, c0, n: self.xf[:, kc, c0:c0 + n], self.xfr, self.onesD_b, subs, finish,
                   src_all=lambda c0, n: self.xf[:, :, c0:c0 + n], parts=parts, finish_small=finish_small)

    def ln_fm(self, nch, src, Rsrc, ones_mat, subs, finish, src_all=None, parts=(1, 2), finish_small=None):
        P = self.P
        zb = self.scr_view(self.ln_off, [128, 8, 512], BF16)
        zsq = self.scr_view(self.ln_off + 8192, [128, 8, 512], BF16)
        mean = self.scr_view(self.ln_off + 16384, [128, 512], F32)
        rstd = self.scr_view(self.ln_off + 18432, [128, 512], F32)
        tmp = self.scr_view(self.ln_off + 20480, [128, 2, 512], F32)
        Rzb, Rzsq, Rmean, Rrstd = self.Rln
        held = {}

        def p1(c0, n):
            allsrc = [Rsrc(kc, c0) for kc in range(nch)]
            if src_all is not None:
                P.op(DVE, lambda e: e.tensor_copy(out=zb[:, 0:nch, 0:n], in_=src_all(c0, n)), r=allsrc, w=[Rzb])
                P.op(ACT, lambda e: e.activation(out=zsq[:, 0:nch, 0:n], in_=src_all(c0, n), func=AF.Square), r=allsrc, w=[Rzsq])
            else:
                for kc in range(nch):
                    P.op(DVE, lambda e, kc=kc: e.tensor_copy(out=zb[:, kc, 0:n], in_=src(kc, c0, n)), r=[Rsrc(kc, c0)], w=[Rzb])
                    P.op(ACT, lambda e, kc=kc: e.activation(out=zsq[:, kc, 0:n], in_=src(kc, c0, n), func=AF.Square),
                         r=[Rsrc(kc, c0)], w=[Rzsq])

        def p2a(c0, n):
            bm, rbm = self.bank()
            bq, rbq = self.bank()
            for kc in range(nch):
                P.op(PE, lambda e, kc=kc: e.matmul(bm[:, 0:n], lhsT=ones_mat[:], rhs=zb[:, kc, 0:n],
                                                 start=(kc == 0), stop=(kc == nch - 1)), r=[Rzb, self.Rconst], w=[rbm])
            for kc in range(nch):
                P.op(PE, lambda e, kc=kc: e.matmul(bq[:, 0:n], lhsT=ones_mat[:], rhs=zsq[:, kc, 0:n],
                                                 start=(kc == 0), stop=(kc == nch - 1)), r=[Rzsq, self.Rconst], w=[rbq])
            held[c0] = (bm, rbm, bq, rbq)

        def p2b(c0, n):
            bm, rbm, bq, rbq = held.pop(c0)
            P.op(DVE, lambda e: e.tensor_copy(out=mean[:, 0:n], in_=bm[:, 0:n]), r=[rbm], w=[rbm, Rmean])
            P.op(DVE, lambda e: e.tensor_tensor(out=rstd[:, 0:n], in0=mean[:, 0:n], in1=mean[:, 0:n], op=ALU.mult), r=[Rmean], w=[Rrstd])
            P.op(DVE, lambda e: e.tensor_tensor(out=rstd[:, 0:n], in0=bq[:, 0:n], in1=rstd[:, 0:n], op=ALU.subtract),
                 r=[rbq, Rrstd], w=[rbq, Rrstd])
            P.op(DVE, lambda e: e.tensor_scalar(out=rstd[:, 0:n], in0=rstd[:, 0:n], scalar1=0.0, scalar2=LN_EPS,
                                                op0=ALU.max, op1=ALU.add), r=[Rrstd], w=[Rrstd])
            P.op(ACT, lambda e: e.activation(out=rstd[:, 0:n], in_=rstd[:, 0:n], func=AF.Ln), r=[Rrstd], w=[Rrstd])
            P.op(ACT, lambda e: e.activation(out=rstd[:, 0:n], in_=rstd[:, 0:n], func=AF.Exp, scale=-0.5), r=[Rrstd], w=[Rrstd])
            if finish_small is not None and n <= 64:
                allsrc = [Rsrc(kc, c0) for kc in range(nch)]
                tall = tmp[:].rearrange("p a b -> p (a b)")[:, 0:nch * n].rearrange("p (c k) -> p c k", c=nch)
                bc = lambda ap2: ap2.unsqueeze(1).to_broadcast([128, nch, n])
                P.op(DVE, lambda e: e.tensor_tensor(out=tall, in0=src_all(c0, n), in1=bc(mean[:, 0:n]), op=ALU.subtract),
                     r=allsrc + [Rmean], w=list(self.Rlnt))
                P.op(DVE, lambda e: e.tensor_tensor(out=tall, in0=tall, in1=bc(rstd[:, 0:n]), op=ALU.mult), r=list(self.Rlnt) + [Rrstd], w=list(self.Rlnt))
                finish_small(c0, n, tall, list(self.Rlnt))
                return
            for kc in range(nch):
                tt = tmp[:, kc % 2, 0:n]
                Rt = self.Rlnt[kc % 2]
                P.op(DVE, lambda e, kc=kc, tt=tt: e.tensor_tensor(out=tt, in0=src(kc, c0, n), in1=mean[:, 0:n], op=ALU.subtract),
                     r=[Rsrc(kc, c0), Rmean], w=[Rt])
                P.op(DVE, lambda e, tt=tt: e.tensor_tensor(out=tt, in0=tt, in1=rstd[:, 0:n], op=ALU.mult), r=[Rt, Rrstd], w=[Rt])
                finish(kc, c0, n, tt, Rt)

        if 1 in parts and 2 in parts:
            for (c0, n) in subs:
                p1(c0, n)
                p2a(c0, n)
            for (c0, n) in subs:
                p2b(c0, n)
        else:
            for (c0, n) in subs:
                if 1 in parts:
                    p1(c0, n)
                if 2 in parts:
                    p2a(c0, n)
                    p2b(c0, n)

    def residual_from_psum(self, kc, c0, n, bk, rb, coef):
        P = self.P
        if coef == 1.0:
            P.op(DVE, lambda e: e.scalar_tensor_tensor(out=self.xf[:, kc, c0:c0 + n], in0=self.xf[:, kc, c0:c0 + n], scalar=ALPHA,
                                                       in1=bk[:, 0:n], op0=ALU.mult, op1=ALU.add),
                 r=[rb, self.xfr(kc, c0)], w=[rb, self.xfr(kc, c0)])
        else:
            P.op(ACT, lambda e: e.mul(out=self.xf[:, kc, c0:c0 + n], in_=self.xf[:, kc, c0:c0 + n], mul=ALPHA),
                 r=[self.xfr(kc, c0)], w=[self.xfr(kc, c0)])
            P.op(DVE, lambda e: e.scalar_tensor_tensor(out=self.xf[:, kc, c0:c0 + n], in0=bk[:, 0:n], scalar=coef,
                                                       in1=self.xf[:, kc, c0:c0 + n], op0=ALU.mult, op1=ALU.add),
                 r=[rb, self.xfr(kc, c0)], w=[rb, self.xfr(kc, c0)])

    def out_proj_ln(self, t, w2d, srcT, Rsrc, li):
        P = self.P
        subs = self.subtiles(t)
        slots = [self.wloadA(w2d, g * 256, 256) for g in range(4)]
        prev = None
        for (c0, n) in subs:
            for g in range(4):
                slot, rs = slots[g]
                for jj in range(2):
                    m = 2 * g + jj
                    bk, rb = self.bank()
                    for kc in range(8):
                        P.op(PE, lambda e, bk=bk, kc=kc, jj=jj, c0=c0, n=n, slot=slot: e.matmul(
                            bk[:, 0:n], lhsT=slot[:, kc, jj * 128:(jj + 1) * 128], rhs=srcT[:, kc, c0:c0 + n],
                            start=(kc == 0), stop=(kc == 7)), r=[rs, Rsrc[kc]], w=[rb])
                    self.residual_from_psum(m, c0, n, bk, rb, 1.0)
            if prev is not None:
                self.layer_norm(t, li, subs=[prev], parts=(2,))
            self.layer_norm(t, li, subs=[(c0, n)], parts=(1,))
            prev = (c0, n)
        self.layer_norm(t, li, subs=[prev], parts=(2,))

    def ffn(self, t, l, i):
        P = self.P
        f = l * 2 + i
        wg, wu, wd = self.d["ffn_w_gate"][f], self.d["ffn_w_up"][f], self.d["ffn_w_down"][f]
        hid = self.scr_view(0, [128, NJ, NC], BF16)
        sg = self.scr_view(self.sg_off, [128, 2, 512], F32)
        Rhid = [self.PR(f"hid{j}") for j in range(NJ)]
        self.Rsg = [self.PR("sg0"), self.PR("sg1")]
        subs = self.subtiles(t)
        for g in range(NJ // 2):
            sg_slot, rg = self.wloadA(wg, g * 256, 256)
            su_slot, ru = self.wloadA(wu, g * 256, 256)
            for jj in range(2):
                j = 2 * g + jj
                for (c0, n) in subs:
                    bg, rbg = self.bank()
                    bu, rbu = self.bank()
                    for kc in range(8):
                        P.op(PE, lambda e, kc=kc, jj=jj, c0=c0, n=n, bg=bg, s=sg_slot: e.matmul(
                            bg[:, 0:n], lhsT=s[:, kc, jj * 128:(jj + 1) * 128], rhs=self.xb[:, kc, c0:c0 + n],
                            start=(kc == 0), stop=(kc == 7)), r=[rg, self.xbr(kc, c0)], w=[rbg])
                    for kc in range(8):
                        P.op(PE, lambda e, kc=kc, jj=jj, c0=c0, n=n, bu=bu, s=su_slot: e.matmul(
                            bu[:, 0:n], lhsT=s[:, kc, jj * 128:(jj + 1) * 128], rhs=self.xb[:, kc, c0:c0 + n],
                            start=(kc == 0), stop=(kc == 7)), r=[ru, self.xbr(kc, c0)], w=[rbu])
                    k = self.isg % 2
                    self.isg += 1
                    Rs = self.Rsg[k]
                    P.op(ACT, lambda e, bg=bg, n=n, k=k: e.activation(out=sg[:, k, 0:n], in_=bg[:, 0:n], func=AF.Silu),
                         r=[rbg], w=[rbg, Rs])
                    P.op(DVE, lambda e, bu=bu, n=n, k=k, j=j, c0=c0: e.scalar_tensor_tensor(
                        out=hid[:, j, c0:c0 + n], in0=bu[:, 0:n], scalar=0.5, in1=sg[:, k, 0:n], op0=ALU.mult, op1=ALU.mult),
                        r=[rbu, Rs], w=[rbu, Rhid[j]])
        li = l * 4 + (0 if i == 0 else 3)
        for m in range(8):
            sd, rd = self.wloadB(wd, m * 128)
            for (c0, n) in subs:
                bk, rb = self.bank()
                for j in range(NJ):
                    P.op(PE, lambda e, j=j, c0=c0, n=n, bk=bk, sd=sd: e.matmul(
                        bk[:, 0:n], lhsT=sd[:, j, :], rhs=hid[:, j, c0:c0 + n], start=(j == 0), stop=(j == NJ - 1)),
                        r=[rd, Rhid[j]], w=[rb])
                self.residual_from_psum(m, c0, n, bk, rb, 1.0)
        self.layer_norm(t, li)


    def phase_barrier(self, extra_old=(), extra_new=()):
        best_e, best_d = {}, {}

        def add(wt):
            if wt[0] == "e":
                if wt[2] > best_e.get(wt[1], -1):
                    best_e[wt[1]] = wt[2]
            else:
                if wt[2] > best_d.get(wt[1], 0):
                    best_d[wt[1]] = wt[2]

        for res in list(self.phase_res) + list(extra_old):
            for wt in (res.pre or ()):
                add(wt)
            if res.lw is not None:
                add(res.lw)
            for e2, i2 in res.rd.items():
                add(("e", e2, i2))
            if res.drd:
                add(("d", res, res.drd))
        H = [("e", e, i) for e, i in best_e.items()] + [("d", r, c) for r, c in best_d.items()]
        self.pending_pre = H
        for res in extra_new:
            res.pre = list(res.pre or []) + list(H)
        self.phase_res = []

    def PR(self, name):
        r = self.R(name)
        r.pre = list(self.pending_pre)
        self.phase_res.append(r)
        return r

    def mem_kv(self):
        P = self.P
        st = self.stage
        memT = self.scr_view(16384, [128, 8, NMEM], BF16)
        RmemT = self.PR("memT")
        kvst = [self.scr_view(0, [128, 2, D], F32), self.scr_view(8192, [128, 2, D], F32)]
        Rkvst = [self.PR("kst"), self.PR("vst")]
        for mb in range(2):
            i = self.istage % 2
            self.istage += 1
            Rs = self.Rstage[i]
            P.load(SP, st[:, i, :], self.d["mem_prompt"][mb * 128:(mb + 1) * 128, :], Rs)
            for g in range(2):
                bk, rb = self.bank()
                for j in range(4):
                    kc = g * 4 + j
                    P.op(PE, lambda e, bk=bk, j=j, kc=kc, i=i: e.transpose(
                        bk[:, j * 128:(j + 1) * 128], st[:, i, kc * 128:(kc + 1) * 128], self.ident_f[:]),
                        r=[Rs, self.Rconst], w=[rb])
                P.op(ACT, lambda e, bk=bk, g=g, mb=mb: e.copy(
                    out=memT[:, g * 4:(g + 1) * 4, mb * 128:(mb + 1) * 128], in_=bk[:].rearrange("p (j t) -> p j t", j=4)),
                    r=[rb], w=[rb, RmemT])
        for l in range(2):
            for which, wname in ((0, "xa_wk"), (1, "xa_wv")):
                w2d = self.d[wname][l]
                for g in range(4):
                    slot, rs = self.wloadA(w2d, g * 256, 256)
                    for mb in range(2):
                        bk, rb = self.bank()
                        for kc in range(8):
                            P.op(PE, lambda e, bk=bk, kc=kc, mb=mb, slot=slot: e.matmul(
                                bk[:, 0:256], lhsT=memT[:, kc, mb * 128:(mb + 1) * 128], rhs=slot[:, kc, 0:256],
                                start=(kc == 0), stop=(kc == 7)), r=[RmemT, rs], w=[rb])
                        P.op(DVE, lambda e, bk=bk, mb=mb, g=g, which=which: e.tensor_copy(
                            out=kvst[which][:, mb, g * 256:(g + 1) * 256], in_=bk[:, 0:256]), r=[rb], w=[rb, Rkvst[which]])
                        if which == 1:
                            P.op(ACT, lambda e, bk=bk, mb=mb, g=g, l=l: e.copy(
                                out=self.Vb[l][:, mb, g * 256:(g + 1) * 256], in_=bk[:, 0:256]), r=[rb], w=[rb, self.RVb[l]])
                    if which == 0:
                        for jj in range(2):
                            m = 2 * g + jj
                            bk, rb = self.bank()
                            for kc in range(8):
                                P.op(PE, lambda e, bk=bk, kc=kc, jj=jj, slot=slot: e.matmul(
                                    bk[:, 0:256], lhsT=slot[:, kc, jj * 128:(jj + 1) * 128], rhs=memT[:, kc, :],
                                    start=(kc == 0), stop=(kc == 7)), r=[RmemT, rs], w=[rb])
                            P.op(ACT, lambda e, bk=bk, m=m, l=l: e.copy(out=self.KT[l][:, m, :], in_=bk[:, 0:256]),
                                 r=[rb], w=[rb, self.RKT[l]])
                dst = self.o["mem_k" if which == 0 else "mem_v"][l].rearrange("(mb p) d -> p mb d", p=128)
                P.store(SP, dst, kvst[which][:], Rkvst[which])

    def xattn(self, t, l):
        P = self.P
        subs = self.subtiles(t)
        psubs = [(0, 512), (512, 512)]
        qT = self.scr_view(0, [128, 8, NC], BF16)
        oT = self.scr_view(16640, [128, 8, NC], BF16)
        ebuf = self.scr_view(33280, [128, 2, 2, 512], BF16)
        rden = self.scr_view(37376, [128, 2, 512], F32)
        RqT = [self.PR(f"qT{c}") for c in range(8)]
        RoT = [self.PR(f"oT{c}") for c in range(8)]
        Re = [[self.PR(f"e{k}{mb}") for mb in range(2)] for k in range(2)]
        Rrden = [self.PR("rden0"), self.PR("rden1")]
        wq, wo = self.d["xa_wq"][l], self.d["xa_wo"][l]
        if t == 1:
            qtok = self.scr_view(74 * 1024, [NS, D], BF16)
            Rqtok = self.PR("qtok")
        for g in range(4):
            slot, rs = self.wloadA(wq, g * 256, 256)
            for jj in range(2):
                m = 2 * g + jj
                for (c0, n) in subs:
                    bk, rb = self.bank()
                    for kc in range(8):
                        P.op(PE, lambda e, bk=bk, kc=kc, jj=jj, c0=c0, n=n, slot=slot: e.matmul(
                            bk[:, 0:n], lhsT=slot[:, kc, jj * 128:(jj + 1) * 128], rhs=self.xb[:, kc, c0:c0 + n],
                            start=(kc == 0), stop=(kc == 7)), r=[rs, self.xbr(kc, c0)], w=[rb])
                    P.op(ACT, lambda e, bk=bk, m=m, c0=c0, n=n: e.copy(out=qT[:, m, c0:c0 + n], in_=bk[:, 0:n]),
                         r=[rb], w=[rb, RqT[m]])
            if t == 1:
                bk, rb = self.bank()
                for kc in range(8):
                    P.op(PE, lambda e, bk=bk, kc=kc, slot=slot: e.matmul(
                        bk[0:NS, 0:256], lhsT=self.xb[:, kc, NT:NT + NS], rhs=slot[:, kc, 0:256],
                        start=(kc == 0), stop=(kc == 7)), r=[rs, self.xbr(kc, NT)], w=[rb])
                P.op(ACT, lambda e, bk=bk, g=g: e.copy(out=qtok[:, g * 256:(g + 1) * 256], in_=bk[0:NS, 0:256]),
                     r=[rb], w=[rb, Rqtok])
        KT, Vb = self.KT[l], self.Vb[l]
        RKT, RVb = self.RKT[l], self.RVb[l]
        it = 0
        sgen = self.xattn_samples(l, qtok, Rqtok, oT, RoT) if t == 1 else None

        def sample_steps(nsteps):
            if sgen is None:
                return
            for _ in range(nsteps):
                try:
                    next(sgen)
                except StopIteration:
                    return

        def att_a(h, c0, n, k):
            for mb in range(2):
                bk, rb = self.bank()
                for cc in range(2):
                    c = 2 * h + cc
                    P.op(PE, lambda e, bk=bk, c=c, cc=cc, mb=mb: e.matmul(
                        bk[:, 0:n], lhsT=KT[:, c, mb * 128:(mb + 1) * 128], rhs=qT[:, c, c0:c0 + n],
                        start=(cc == 0), stop=(cc == 1)), r=[RKT, RqT[c]], w=[rb])
                P.op(ACT, lambda e, bk=bk, mb=mb: e.activation(out=ebuf[:, k, mb, 0:n], in_=bk[:, 0:n], func=AF.Exp,
                                                             scale=1.0 / 16.0), r=[rb], w=[rb, Re[k][mb]])

        def att_b(h, c0, n, k):
            bk, rb = self.bank()
            for mb in range(2):
                P.op(PE, lambda e, bk=bk, mb=mb: e.matmul(bk[:, 0:n], lhsT=self.ones_b[:], rhs=ebuf[:, k, mb, 0:n],
                                                         start=(mb == 0), stop=(mb == 1)), r=[Re[k][mb], self.Rconst], w=[rb])
            P.op(ACT, lambda e, bk=bk: e.activation(out=rden[:, k, 0:n], in_=bk[:, 0:n], func=AF.Ln), r=[rb], w=[rb, Rrden[k]])
            P.op(ACT, lambda e: e.activation(out=rden[:, k, 0:n], in_=rden[:, k, 0:n], func=AF.Exp, scale=-1.0),
                 r=[Rrden[k]], w=[Rrden[k]])
            for dc in range(2):
                c = 2 * h + dc
                bk, rb = self.bank()
                for mb in range(2):
                    P.op(PE, lambda e, bk=bk, mb=mb, c=c: e.matmul(
                        bk[:, 0:n], lhsT=Vb[:, mb, c * 128:(c + 1) * 128], rhs=ebuf[:, k, mb, 0:n],
                        start=(mb == 0), stop=(mb == 1)), r=[Re[k][mb], RVb], w=[rb])
                P.op(DVE, lambda e, bk=bk, c=c: e.tensor_tensor(
                    out=oT[:, c, c0:c0 + n], in0=bk[:, 0:n], in1=rden[:, k, 0:n], op=ALU.mult),
                    r=[rb, Rrden[k]], w=[rb, RoT[c]])

        steps = [(h, c0, n) for h in range(4) for (c0, n) in psubs]
        att_a(*steps[0], 0)
        for i, (h, c0, n) in enumerate(steps):
            sample_steps(2)
            if i + 1 < len(steps):
                att_a(*steps[i + 1], (i + 1) % 2)
            att_b(h, c0, n, i % 2)
        sample_steps(10 ** 6)
        self.out_proj_ln(t, wo, oT, RoT, l * 4 + 2)

    def xattn_samples(self, l, qtok, Rqtok, oT, RoT):
        P = self.P
        hi = 74 * 1024
        ksb = self.scr_view(hi + 2048, [128, 2, 2, D], BF16)
        vsb = self.scr_view(hi + 10240, [128, 2, 2, D], BF16)
        prod = self.scr_view(hi + 18432, [128, 2, 512], F32)
        lo = 41472
        mask4 = self.scr_view(lo, [4, D], BF16)
        o4m = self.scr_view(lo + 2048, [4, D], BF16)
        ss = self.scr_view(lo + 4096, [128, 2, 4], F32)
        es = self.scr_view(lo + 4096 + 64, [128, 2, 4], BF16)
        rd = self.scr_view(lo + 4096 + 128, [128, 4], F32)
        pT = self.scr_view(lo + 4096 + 192, [128, 2, 4], BF16)
        sel = self.scr_view(70 * 1024, [NS, NS, 128], BF16)
        junk = self.scr_view(lo + 4608, [128, 256], F32)
        Rjunk = self.PR("junk")
        Rk = [self.PR("ksb0"), self.PR("ksb1")]
        Rv = [self.PR("vsb0"), self.PR("vsb1")]
        Rprod = [self.PR("prod0"), self.PR("prod1")]
        Ro4 = self.PR("o4m")
        ss2 = [ss, self.scr_view(lo + 4096 + 256, [128, 2, 4], F32)]
        es2 = [es, self.scr_view(lo + 4096 + 256 + 64, [128, 2, 4], BF16)]
        rd2 = [rd, self.scr_view(lo + 4096 + 256 + 128, [128, 4], F32)]
        pT2 = [pT, self.scr_view(lo + 4096 + 256 + 192, [128, 2, 4], BF16)]
        Rss2 = [self.PR("ss0"), self.PR("ss1")]
        Res2 = [self.PR("es0"), self.PR("es1")]
        Rrd2 = [self.PR("rd0"), self.PR("rd1")]
        RpT2 = [self.PR("pT0"), self.PR("pT1")]
        Rsel = self.PR("selmask")
        P.op(POOL, lambda e: e.memset(sel[:], 1.0), w=[Rsel])
        P.op(POOL, lambda e: e.affine_select(out=sel[:], in_=sel[:], pattern=[[1, NS], [0, 128]], compare_op=ALU.is_equal,
                                            fill=0.0, base=0, channel_multiplier=-1), r=[Rsel], w=[Rsel])
        P.op(POOL, lambda e: e.memset(mask4[:], 1.0), w=[Rsel])
        P.op(POOL, lambda e: e.affine_select(out=mask4[:], in_=mask4[:], pattern=[[1, D]], compare_op=ALU.is_ge,
                                            fill=0.0, base=0, channel_multiplier=-256), r=[Rsel], w=[Rsel])
        P.op(POOL, lambda e: e.affine_select(out=mask4[:], in_=mask4[:], pattern=[[-1, D]], compare_op=ALU.is_ge,
                                            fill=0.0, base=255, channel_multiplier=256), r=[Rsel], w=[Rsel])
        osb, rosb = self.bank(reserve=True)
        ck, cv = self.d["cache_mem_k"][l], self.d["cache_mem_v"][l]
        den_banks = {}

        def stage_a(s):
            k = s % 2
            P.load(POOL, ksb[:, k], ck[s].rearrange("(mb p) d -> p mb d", p=128), Rk[k])
            qb = []
            for half in range(2):
                bk, rb = self.bank()
                P.op(PE, lambda e, bk=bk, half=half: e.matmul(
                    bk[:, :], lhsT=sel[:, s, :], rhs=qtok[:, half * 512:(half + 1) * 512], start=True, stop=True),
                    r=[Rqtok, Rsel], w=[rb])
                qb.append((bk, rb))
            for mb in range(2):
                for half in range(2):
                    bk, rb = qb[half]
                    kk = (mb * 2 + half) % 2
                    P.op(DVE, lambda e, bk=bk, mb=mb, half=half, kk=kk: e.tensor_tensor(
                        out=prod[:, kk, :], in0=ksb[:, k, mb, half * 512:(half + 1) * 512], in1=bk[:, :], op=ALU.mult),
                        r=[rb, Rk[k]], w=[rb, Rprod[kk]])
                    for hh in range(2):
                        P.op(ACT, lambda e, mb=mb, half=half, kk=kk, hh=hh: e.activation(
                            out=junk[:, :], in_=prod[:, kk, hh * 256:(hh + 1) * 256], func=AF.Copy,
                            accum_out=ss2[k][:, mb, half * 2 + hh:half * 2 + hh + 1]), r=[Rprod[kk]], w=[Rss2[k], Rjunk])
            P.op(ACT, lambda e: e.activation(out=es2[k][:], in_=ss2[k][:], func=AF.Exp, scale=1.0 / 16.0), r=[Rss2[k]], w=[Res2[k]])

        def stage_b(s):
            k = s % 2
            P.load(POOL, vsb[:, k], cv[s].rearrange("(mb p) d -> p mb d", p=128), Rv[k])
            bk, rb = self.bank()
            for mb in range(2):
                P.op(PE, lambda e, bk=bk, mb=mb: e.matmul(bk[:, 0:4], lhsT=self.ones_b[:], rhs=es2[k][:, mb, :],
                                                         start=(mb == 0), stop=(mb == 1)), r=[Res2[k], self.Rconst], w=[rb])
            P.op(DVE, lambda e, bk=bk: e.reciprocal(out=rd2[k][:], in_=bk[:, 0:4]), r=[rb], w=[rb, Rrd2[k]])
            for mb in range(2):
                P.op(DVE, lambda e, mb=mb: e.tensor_tensor(out=pT2[k][:, mb, :], in0=es2[k][:, mb, :], in1=rd2[k][:], op=ALU.mult),
                     r=[Res2[k], Rrd2[k]], w=[RpT2[k]])

        def stage_c(s):
            k = s % 2
            for half in range(2):
                bk, rb = self.bank()
                for mb in range(2):
                    P.op(PE, lambda e, bk=bk, mb=mb, half=half: e.matmul(
                        bk[0:4, :], lhsT=pT2[k][:, mb, :], rhs=vsb[:, k, mb, half * 512:(half + 1) * 512],
                        start=(mb == 0), stop=(mb == 1)), r=[RpT2[k], Rv[k]], w=[rb])
                P.op(DVE, lambda e, bk=bk, half=half: e.tensor_tensor(
                    out=o4m[:, half * 512:(half + 1) * 512], in0=bk[0:4, :], in1=mask4[:, half * 512:(half + 1) * 512],
                    op=ALU.mult), r=[rb, Rsel], w=[rb, Ro4])
            for c in range(8):
                P.op(PE, lambda e, c=c: e.matmul(osb[:, c * NS + s: c * NS + s + 1], lhsT=o4m[:, c * 128:(c + 1) * 128],
                                                rhs=self.ones_b[0:4, 0:1], start=True, stop=True),
                     r=[Ro4, self.Rconst], w=[rosb])

        for step in range(NS + 2):
            if 0 <= step - 2 < NS:
                stage_c(step - 2)
            if 0 <= step - 1 < NS:
                stage_b(step - 1)
            if step < NS:
                stage_a(step)
            yield
        P.op(ACT, lambda e: e.copy(out=oT[:, :, NT:NT + NS], in_=osb[:, 0:8 * NS].rearrange("p (c s) -> p c s", c=8)),
             r=[rosb], w=[rosb] + RoT)
        self.release(osb)

    def rstd_from_var(self, var_ap, out_ap, Rv, rows):
        P = self.P
        P.op(DVE, lambda e: e.tensor_scalar(out=out_ap, in0=var_ap, scalar1=0.0, scalar2=self._eps, op0=ALU.max, op1=ALU.add),
             r=[Rv], w=[Rv])
        P.op(ACT, lambda e: e.activation(out=out_ap, in_=out_ap, func=AF.Ln), r=[Rv], w=[Rv])
        P.op(ACT, lambda e: e.activation(out=out_ap, in_=out_ap, func=AF.Exp, scale=-0.5), r=[Rv], w=[Rv])

    def mixer(self, t, l):
        if l == 0:
            self.mixer_ab(t)
        else:
            self.mixer_gm(t)

    def mixer_gm(self, t):
        P = self.P
        subs = self.subtiles(t)
        w_in, w_out = self.d["gm_w_in"], self.d["gm_w_out"]
        uT = self.scr_view(0, [128, 8, NC], BF16)
        h2T = self.scr_view(16640, [128, 8, NC], BF16)
        vraw = self.scr_view(33280, [128, D], F32)
        vtok = self.scr_view(37376, [128, 2, D], BF16)
        stats = self.scr_view(41472, [128, 2, 6], F32)
        mv = self.scr_view(41472 + 48, [128, 2], F32)
        rstd = self.scr_view(41472 + 56, [128, 1], F32)
        WsT = self.scr_view(41600, [128, 4, 128], BF16)
        bsrow = self.scr_view(42624, [1, 512], BF16)
        w00b = self.scr_view(43648, [NS, 4], F32)
        b0b = self.scr_view(43648 + 16, [NS, 4], F32)
        hi = 74 * 1024
        gmg = self.scr_view(hi, [128, D], F32)
        gmb = self.scr_view(hi + 4096, [128, D], F32)
        wsnat = self.scr_view(hi + 8192, [128, 4, 128], F32)
        fsamp = self.scr_view(hi + 10240, [NS, D], F32)
        vsamp = self.scr_view(hi + 14336, [NS, D], F32)
        RuT = [self.PR(f"uT{c}") for c in range(8)]
        Rh2 = [self.PR(f"h2T{c}") for c in range(8)]
        Rvraw, Rstat = self.PR("vraw"), self.PR("gmstat")
        Rvtok = [self.PR("vtok0"), self.PR("vtok1")]
        Rc = self.PR("gmconst")
        Rfs, Rvs = self.PR("fsamp"), self.PR("vsamp")
        self._eps = LN_EPS
        P.load(SP, gmg[:], self.d["gm_ln_g"].partition_broadcast(128).rearrange("p a d -> p (a d)"), Rc)
        P.load(SP, gmb[:], self.d["gm_ln_b"].partition_broadcast(128).rearrange("p a d -> p (a d)"), Rc)
        P.load(SP, wsnat[:], self.d["gm_w_s"].rearrange("g i j -> i g j"), Rc)
        Rbs = self.PR("bsrow")
        P.load(POOL, bsrow[:], self.d["gm_b_s"].rearrange("g i -> (g i)").rearrange("(a n) -> a n", a=1), Rbs)
        if t == 1:
            P.load(SP, w00b[:], self.d["gm_w_s"][:, 0, 0:1].rearrange("g a -> a g").partition_broadcast(NS).rearrange("p a g -> p (a g)"), Rc,
                   allow_slow_non_contiguous=True)
            P.load(SP, b0b[:], self.d["gm_b_s"][:, 0:1].rearrange("g a -> a g").partition_broadcast(NS).rearrange("p a g -> p (a g)"), Rc,
                   allow_slow_non_contiguous=True)
        bk, rb = self.bank()
        for g in range(4):
            P.op(PE, lambda e, bk=bk, g=g: e.transpose(bk[:, g * 128:(g + 1) * 128], wsnat[:, g, :], self.ident_f[:]),
                 r=[Rc, self.Rconst], w=[rb])
        P.op(DVE, lambda e, bk=bk: e.tensor_tensor(out=WsT[:], in0=bk[:].rearrange("p (g i) -> p g i", g=4),
                                                  in1=self.mask_c[:, :].unsqueeze(1).to_broadcast([128, 4, 128]), op=ALU.mult),
             r=[rb, self.Rconst], w=[rb, Rc])
        for g in range(4):
            slot, rs = self.wloadA(w_in, g * 256, 256)
            for jj in range(2):
                m = 2 * g + jj
                for (c0, n) in subs:
                    bk, rb = self.bank()
                    for kc in range(8):
                        P.op(PE, lambda e, bk=bk, kc=kc, jj=jj, c0=c0, n=n, slot=slot: e.matmul(
                            bk[:, 0:n], lhsT=slot[:, kc, jj * 128:(jj + 1) * 128], rhs=self.xb[:, kc, c0:c0 + n],
                            start=(kc == 0), stop=(kc == 7)), r=[rs, self.xbr(kc, c0)], w=[rb])
                    P.op(ACT, lambda e, bk=bk, m=m, c0=c0, n=n: e.activation(out=uT[:, m, c0:c0 + n], in_=bk[:, 0:n],
                                                                            func=AF.Gelu_apprx_tanh), r=[rb], w=[rb, RuT[m]])
        vslots = [self.wloadA(w_in, D + g * 256, 256) for g in range(4)]
        blocks = [(tb * 128, 128) for tb in range(NT // 128)]
        if t == 1:
            blocks.append((NT, NS))
        vraw2 = [vraw, self.scr_view(hi + 18432, [128, D], F32)]
        Rvraw2 = [Rvraw, self.PR("vraw1")]
        stats2 = [stats, self.scr_view(41472 + 64, [128, 2, 6], F32)]
        mv2 = [mv, self.scr_view(41472 + 64 + 48, [128, 2], F32)]
        rstd2 = [rstd, self.scr_view(41472 + 64 + 56, [128, 1], F32)]
        Rstat2 = [Rstat, self.PR("gmstat1")]

        def proj(bi):
            col0, rows = blocks[bi]
            vr, Rvr = vraw2[bi % 2], Rvraw2[bi % 2]
            b2 = [self.bank(), self.bank()]
            for g in range(4):
                slot, rs = vslots[g]
                bk, rb = b2[g // 2]
                for kc in range(8):
                    P.op(PE, lambda e, bk=bk, kc=kc, g=g, slot=slot: e.matmul(
                        bk[0:rows, (g % 2) * 256:(g % 2 + 1) * 256], lhsT=self.xb[:, kc, col0:col0 + rows], rhs=slot[:, kc, 0:256],
                        start=(kc == 0), stop=(kc == 7)), r=[rs, self.xbr(kc, col0)], w=[rb])
            for hf in range(2):
                bk, rb = b2[hf]
                P.op(ACT, lambda e, bk=bk, hf=hf: e.activation(out=vr[0:rows, hf * 512:(hf + 1) * 512], in_=bk[0:rows, :],
                                                             func=AF.Gelu_apprx_tanh), r=[rb], w=[rb, Rvr])

        def lnmix(bi):
            col0, rows = blocks[bi]
            q = bi % 2
            vr, Rvr, st_, mv_, rs_, Rst = vraw2[q], Rvraw2[q], stats2[q], mv2[q], rstd2[q], Rstat2[q]
            for hf in range(2):
                P.op(DVE, lambda e, hf=hf: e.bn_stats(out=st_[0:rows, hf, :], in_=vr[0:rows, hf * 512:(hf + 1) * 512]),
                     r=[Rvr], w=[Rst])
            P.op(DVE, lambda e: e.bn_aggr(out=mv_[0:rows, :], in_=st_[0:rows].rearrange("p a b -> p (a b)")), r=[Rst], w=[Rst])
            self.rstd_from_var(mv_[0:rows, 1:2], rs_[0:rows, :], Rst, rows)
            P.op(DVE, lambda e: e.tensor_scalar(out=vr[0:rows, :], in0=vr[0:rows, :], scalar1=mv_[0:rows, 0:1],
                                                scalar2=rs_[0:rows, 0:1], op0=ALU.subtract, op1=ALU.mult), r=[Rvr, Rst], w=[Rvr])
            P.op(DVE, lambda e: e.tensor_tensor(out=vr[0:rows, :], in0=vr[0:rows, :], in1=gmg[0:rows, :], op=ALU.mult),
                 r=[Rvr, Rc], w=[Rvr])
            if rows == 128:
                k = bi % 2
                P.op(DVE, lambda e: e.tensor_tensor(out=vtok[:, k, :], in0=vr[:, :], in1=gmb[:, :], op=ALU.add),
                     r=[Rvr, Rc], w=[Rvtok[k]])
                for hf in range(2):
                    bk, rb = self.bank()
                    for cc in range(4):
                        c = hf * 4 + cc
                        g = c // 2
                        P.op(PE, lambda e, bk=bk, cc=cc, c=c, g=g: e.matmul(
                            bk[:, cc * 128:(cc + 1) * 128], lhsT=vtok[:, k, c * 128:(c + 1) * 128], rhs=WsT[:, g, :],
                            start=True, stop=False), r=[Rvtok[k], Rc], w=[rb])
                        P.op(PE, lambda e, bk=bk, cc=cc, g=g: e.matmul(
                            bk[:, cc * 128:(cc + 1) * 128], lhsT=self.ones_b[0:1, :], rhs=bsrow[0:1, g * 128:(g + 1) * 128],
                            start=False, stop=True), r=[Rbs, self.Rconst], w=[rb])
                    P.op(DVE, lambda e, bk=bk, hf=hf: e.tensor_tensor(
                        out=h2T[:, hf * 4:(hf + 1) * 4, col0:col0 + 128], in0=bk[:].rearrange("p (c i) -> p c i", c=4),
                        in1=uT[:, hf * 4:(hf + 1) * 4, col0:col0 + 128], op=ALU.mult),
                        r=[rb] + [RuT[hf * 4 + j] for j in range(4)], w=[rb] + [Rh2[hf * 4 + j] for j in range(4)])
            else:
                P.op(DVE, lambda e: e.tensor_tensor(out=vsamp[:, :], in0=vr[0:NS, :], in1=gmb[0:NS, :], op=ALU.add),
                     r=[Rvr, Rc], w=[Rvs])
                P.store(SP, self.o["gm_v_sample"][:, :], vsamp[:, :], Rvs)
                for g in range(4):
                    P.op(DVE, lambda e, g=g: e.tensor_scalar(out=fsamp[:, g * 256:(g + 1) * 256], in0=vsamp[:, g * 256:(g + 1) * 256],
                                                            scalar1=w00b[:, g:g + 1], scalar2=b0b[:, g:g + 1], op0=ALU.mult, op1=ALU.add),
                         r=[Rvs, Rc], w=[Rfs])
                bk, rb = self.bank()
                for c in range(8):
                    P.op(PE, lambda e, bk=bk, c=c: e.transpose(bk[:, c * NS:(c + 1) * NS], fsamp[:, c * 128:(c + 1) * 128],
                                                              self.ident_f[0:NS, 0:NS]), r=[Rfs, self.Rconst], w=[rb])
                P.op(DVE, lambda e, bk=bk: e.tensor_tensor(out=h2T[:, :, NT:NT + NS], in0=bk[:, 0:8 * NS].rearrange("p (c s) -> p c s", c=8),
                                                          in1=uT[:, :, NT:NT + NS], op=ALU.mult),
                     r=[rb] + RuT, w=[rb] + Rh2)

        proj(0)
        for bi in range(len(blocks)):
            if bi + 1 < len(blocks):
                proj(bi + 1)
            lnmix(bi)
        self.out_proj_ln(t, w_out, h2T, Rh2, 5)

    def keep(self, res_list):
        self.phase_res.extend(res_list)

    def mixer_ab(self, t):
        P = self.P
        subs = self.subtiles(t)
        oc = self.scr_view(0, [128, 8, NC], BF16)
        Roc = [self.PR(f"oc{c}") for c in range(8)]
        self.conformer_prompt(t, oc, Roc)
        if t == 1:
            self.phase_barrier()
            self.keep(Roc)
            self.conformer_samples(oc, Roc)
        lnres = list(self.Rln) + list(self.Rlnt)
        if self.cfg.get("no_gdn"):
            self.phase_barrier()
            self.keep(Roc)
            P.op(DVE, lambda e: e.memset(oc[:, 0:4, :], 0.0), w=Roc[0:4])
        else:
            self.phase_barrier(extra_old=lnres)
            self.keep(Roc)
            self.gdn(t, oc, Roc)
            if t == 1:
                self.phase_barrier(extra_new=lnres)
                self.keep(Roc)
                self.gdn_samples(oc, Roc)
            self.phase_barrier(extra_new=lnres)
            self.keep(Roc)
        self.out_proj_ln(t, self.d["ab_w_out"], oc, Roc, 1)

    def conformer_prompt(self, t, oc, Roc):
        P = self.P
        w_in = self.d["ab_w_in"]
        psubs = [(0, 512), (512, 512)]
        glubuf = self.scr_view(16640, [128, 2, 1056], BF16)
        ccv = self.scr_view(20864, [128, 4, NC], F32)
        ccw = self.scr_view(37504, [128, 4, 31], F32)
        ccb = self.scr_view(38000, [128, 4, 1], F32)
        ccg = self.scr_view(38016, [128, 4, 1], F32)
        ccbeta = self.scr_view(38032, [128, 4, 1], F32)
        halo_f = self.scr_view(38080, [128, 4, 30], F32)
        sgt = self.scr_view(70 * 1024, [128, 2, 512], F32)
        dg = self.scr_view(74 * 1024, [128, 2, 31, 128], BF16)
        Rglu = [self.PR("glu0"), self.PR("glu1")]
        Rccv = [self.PR(f"ccv{i}") for i in range(4)]
        Rcp = self.PR("ccparams")
        Rhf = self.PR("halo_f")
        Rsg = [self.PR("sgt0"), self.PR("sgt1")]
        Rdg = [self.PR("dg0"), self.PR("dg1")]
        self.fm_load(self.d["cc_conv_w"], 31, 512, ccw, Rcp)
        self.fm_load(self.d["cc_conv_b"], 1, 512, ccb, Rcp)
        self.fm_load(self.d["cc_ln_g"], 1, 512, ccg, Rcp)
        self.fm_load(self.d["cc_ln_b"], 1, 512, ccbeta, Rcp)
        isg = 0
        for g2 in range(2):
            sa, ra = self.wloadA(w_in, 2056 + g2 * 256, 256)
            sbb, rbb = self.wloadA(w_in, 2568 + g2 * 256, 256)
            for jj in range(2):
                i = 2 * g2 + jj
                k = i % 2
                P.op(DVE, lambda e, k=k, i=i: e.tensor_copy(out=glubuf[:, k, 0:30], in_=self.cc_halo[:, i, :]),
                     r=[self.Rcch], w=[Rglu[k]])
                for w in range(31):
                    P.op(DVE, lambda e, k=k, w=w, i=i: e.tensor_scalar(out=dg[:, k, w, :], in0=self.ident_b[:], scalar1=ccw[:, i, w:w + 1],
                                                                     scalar2=None, op0=ALU.mult),
                         r=[Rcp, self.Rconst], w=[Rdg[k]])
                for (c0, n) in psubs:
                    ba, rba = self.bank()
                    bb, rbbk = self.bank()
                    for kc in range(8):
                        P.op(PE, lambda e, ba=ba, kc=kc, jj=jj, c0=c0, n=n, sa=sa: e.matmul(
                            ba[:, 0:n], lhsT=sa[:, kc, jj * 128:(jj + 1) * 128], rhs=self.xb[:, kc, c0:c0 + n],
                            start=(kc == 0), stop=(kc == 7)), r=[ra, self.xbr(kc, c0)], w=[rba])
                    for kc in range(8):
                        P.op(PE, lambda e, bb=bb, kc=kc, jj=jj, c0=c0, n=n, sbb=sbb: e.matmul(
                            bb[:, 0:n], lhsT=sbb[:, kc, jj * 128:(jj + 1) * 128], rhs=self.xb[:, kc, c0:c0 + n],
                            start=(kc == 0), stop=(kc == 7)), r=[rbb, self.xbr(kc, c0)], w=[rbbk])
                    q = isg % 2
                    isg += 1
                    P.op(ACT, lambda e, bb=bb, q=q, n=n: e.activation(out=sgt[:, q, 0:n], in_=bb[:, 0:n], func=AF.Sigmoid),
                         r=[rbbk], w=[rbbk, Rsg[q]])
                    P.op(DVE, lambda e, ba=ba, q=q, n=n, k=k, c0=c0: e.tensor_tensor(
                        out=glubuf[:, k, 30 + c0:30 + c0 + n], in0=ba[:, 0:n], in1=sgt[:, q, 0:n], op=ALU.mult),
                        r=[rba, Rsg[q]], w=[rba, Rglu[k]])
                    if t == 1 and c0 == 512:
                        P.op(DVE, lambda e, ba=ba, q=q, i=i: e.tensor_tensor(
                            out=halo_f[:, i, :], in0=ba[:, 482:512], in1=sgt[:, q, 482:512], op=ALU.mult),
                            r=[rba, Rsg[q]], w=[rba, Rhf])
                P.op(DVE, lambda e, k=k, i=i: e.tensor_copy(out=self.cc_halo[:, i, :], in_=glubuf[:, k, 1024:1054]),
                     r=[Rglu[k]], w=[self.Rcch])
                for (c0, n) in psubs:
                    bk, rb = self.bank()
                    for w in range(31):
                        P.op(PE, lambda e, bk=bk, w=w, k=k, c0=c0, n=n: e.matmul(
                            bk[:, 0:n], lhsT=dg[:, k, w, :], rhs=glubuf[:, k, c0 + w:c0 + w + n],
                            start=(w == 0), stop=(w == 30)), r=[Rdg[k], Rglu[k]], w=[rb])
                    P.op(ACT, lambda e, bk=bk, i=i, c0=c0, n=n: e.activation(out=ccv[:, i, c0:c0 + n], in_=bk[:, 0:n], func=AF.Identity,
                                                                            bias=ccb[:, i, :], scale=1.0), r=[rb, Rcp], w=[rb, Rccv[i]])

        def finish(kc, c0, n, tt, Rt):
            P.op(ACT, lambda e: e.activation(out=oc[:, 4 + kc, c0:c0 + n], in_=tt, func=AF.Silu,
                                             scale=ccg[:, kc, :], bias=ccbeta[:, kc, :]), r=[Rt, Rcp], w=[Roc[4 + kc]])

        self.ln_fm(4, lambda kc, c0, n: ccv[:, kc, c0:c0 + n], (lambda kc, c0: Rccv[kc]), self.ones512_b, psubs, finish,
                   src_all=lambda c0, n: ccv[:, :, c0:c0 + n])
        if t == 1:
            i = self.istage % 2
            self.istage += 1
            Rs = self.Rstage[i]
            bk, rb = self.bank()
            for c in range(4):
                P.op(PE, lambda e, bk=bk, c=c: e.transpose(bk[0:30, c * 128:(c + 1) * 128], halo_f[:, c, :], self.ident_f[:]),
                     r=[Rhf, self.Rconst], w=[rb])
            P.op(DVE, lambda e, bk=bk, i=i: e.tensor_copy(out=self.stage[0:30, i, 0:512], in_=bk[0:30, :]), r=[rb], w=[rb, Rs])
            P.store(SP, self.o["cc_conv_prompt"][:, :], self.stage[0:30, i, 0:512], Rs)

    def conformer_samples(self, oc, Roc):
        P = self.P
        w_in = self.d["ab_w_in"]
        hi = 74 * 1024
        st = self.scr_view(hi, [120, 4, 512], F32)
        wrep = self.scr_view(hi + 8192, [120, 512], F32)
        prod = self.scr_view(hi + 10240, [120, 2, 512], F32)
        ind = self.scr_view(hi + 14336, [120, 4, NS], F32)
        lo = 16640
        glus = self.scr_view(lo, [NS, 512], F32)
        w30b = self.scr_view(lo + 2048, [NS, 512], F32)
        biasb = self.scr_view(lo + 4096, [NS, 512], F32)
        lngb = self.scr_view(lo + 6144, [NS, 512], F32)
        lnbb = self.scr_view(lo + 8192, [NS, 512], F32)
        ys = self.scr_view(lo + 10240, [NS, 512], F32)
        sgs = self.scr_view(lo + 12288, [NS, 512], F32)
        stats = self.scr_view(lo + 14336, [NS, 6], F32)
        mv = self.scr_view(lo + 14336 + 32, [NS, 2], F32)
        rstd = self.scr_view(lo + 14336 + 48, [NS, 1], F32)
        Rst, Rw, Rprod, Rglus, Rys, Rsm = self.PR("ccst"), self.PR("ccw"), [self.PR("ccp0"), self.PR("ccp1")], self.PR("glus"), self.PR("ys"), self.PR("ccsm")
        P.load(SP, st[:], self.d["state_cc_conv"].rearrange("(b j) w c -> (j w) b c", j=4), Rst)
        for j in range(4):
            P.load(SP, wrep[j * 30:(j + 1) * 30, :], self.d["cc_conv_w"][0:30, :], Rw)
        P.load(SP, w30b[:], self.d["cc_conv_w"][30:31, :].partition_broadcast(NS).rearrange("p a d -> p (a d)"), Rw)
        P.load(SP, biasb[:], self.d["cc_conv_b"].partition_broadcast(NS).rearrange("p a d -> p (a d)"), Rw)
        P.load(SP, lngb[:], self.d["cc_ln_g"].partition_broadcast(NS).rearrange("p a d -> p (a d)"), Rw)
        P.load(SP, lnbb[:], self.d["cc_ln_b"].partition_broadcast(NS).rearrange("p a d -> p (a d)"), Rw)
        gslots = [(self.wloadA(w_in, 2056 + g2 * 256, 256), self.wloadA(w_in, 2568 + g2 * 256, 256)) for g2 in range(2)]
        P.op(POOL, lambda e: e.memset(ind[:], 1.0), w=[Rw])
        P.op(POOL, lambda e: e.affine_select(out=ind[:], in_=ind[:], pattern=[[120, 4], [-30, NS]], compare_op=ALU.is_ge,
                                            fill=0.0, base=0, channel_multiplier=1), r=[Rw], w=[Rw])
        P.op(POOL, lambda e: e.affine_select(out=ind[:], in_=ind[:], pattern=[[-120, 4], [30, NS]], compare_op=ALU.is_ge,
                                            fill=0.0, base=29, channel_multiplier=-1), r=[Rw], w=[Rw])
        for g2 in range(2):
            (sa, ra), (sbb, rbb) = gslots[g2]
            ba, rba = self.bank()
            bb, rbbk = self.bank()
            for kc in range(8):
                P.op(PE, lambda e, ba=ba, kc=kc, sa=sa: e.matmul(ba[0:NS, 0:256], lhsT=self.xb[:, kc, NT:NT + NS], rhs=sa[:, kc, 0:256],
                                                                start=(kc == 0), stop=(kc == 7)), r=[ra, self.xbr(kc, NT)], w=[rba])
            for kc in range(8):
                P.op(PE, lambda e, bb=bb, kc=kc, sbb=sbb: e.matmul(bb[0:NS, 0:256], lhsT=self.xb[:, kc, NT:NT + NS], rhs=sbb[:, kc, 0:256],
                                                                  start=(kc == 0), stop=(kc == 7)), r=[rbb, self.xbr(kc, NT)], w=[rbbk])
            P.op(ACT, lambda e, bb=bb, g2=g2: e.activation(out=sgs[:, g2 * 256:(g2 + 1) * 256], in_=bb[0:NS, 0:256], func=AF.Sigmoid),
                 r=[rbbk], w=[rbbk, Rsm])
            P.op(DVE, lambda e, ba=ba, g2=g2: e.tensor_tensor(out=glus[:, g2 * 256:(g2 + 1) * 256], in0=ba[0:NS, 0:256],
                                                             in1=sgs[:, g2 * 256:(g2 + 1) * 256], op=ALU.mult),
                 r=[rba, Rsm], w=[rba, Rglus])
        P.d2d(SP, self.o["cc_conv_sample"][:, 0:29, :], self.d["state_cc_conv"][:, 1:30, :], self.Rd2d)
        P.store(SP, self.o["cc_conv_sample"][:, 29, :], glus[:, :], Rglus)
        bk, rb = self.bank()
        for b in range(4):
            q = b % 2
            P.op(DVE, lambda e, b=b, q=q: e.tensor_tensor(out=prod[:, q, :], in0=st[:, b, :], in1=wrep[:, :], op=ALU.mult),
                 r=[Rst, Rw], w=[Rprod[q]])
            P.op(PE, lambda e, bk=bk, b=b, q=q: e.matmul(bk[0:NS, :], lhsT=ind[:, b, :], rhs=prod[:, q, :], start=(b == 0), stop=(b == 3)),
                 r=[Rprod[q], Rw], w=[rb])
        P.op(DVE, lambda e: e.tensor_tensor(out=ys[:], in0=glus[:], in1=w30b[:], op=ALU.mult), r=[Rglus, Rw], w=[Rys])
        P.op(DVE, lambda e, bk=bk: e.tensor_tensor(out=ys[:], in0=bk[0:NS, :], in1=ys[:], op=ALU.add), r=[rb, Rys], w=[rb, Rys])
        P.op(DVE, lambda e: e.tensor_tensor(out=ys[:], in0=ys[:], in1=biasb[:], op=ALU.add), r=[Rys, Rw], w=[Rys])
        P.op(DVE, lambda e: e.bn_stats(out=stats[:], in_=ys[:]), r=[Rys], w=[Rsm])
        P.op(DVE, lambda e: e.bn_aggr(out=mv[:], in_=stats[:]), r=[Rsm], w=[Rsm])
        self._eps = LN_EPS
        self.rstd_from_var(mv[:, 1:2], rstd[:], Rsm, NS)
        P.op(DVE, lambda e: e.tensor_scalar(out=ys[:], in0=ys[:], scalar1=mv[:, 0:1], scalar2=rstd[:, 0:1],
                                            op0=ALU.subtract, op1=ALU.mult), r=[Rys, Rsm], w=[Rys])
        P.op(DVE, lambda e: e.tensor_tensor(out=ys[:], in0=ys[:], in1=lngb[:], op=ALU.mult), r=[Rys, Rw], w=[Rys])
        P.op(DVE, lambda e: e.tensor_tensor(out=ys[:], in0=ys[:], in1=lnbb[:], op=ALU.add), r=[Rys, Rw], w=[Rys])
        P.op(ACT, lambda e: e.activation(out=ys[:], in_=ys[:], func=AF.Silu), r=[Rys], w=[Rys])
        bk, rb = self.bank()
        for c in range(4):
            P.op(PE, lambda e, bk=bk, c=c: e.transpose(bk[:, c * NS:(c + 1) * NS], ys[:, c * 128:(c + 1) * 128],
                                                      self.ident_f[0:NS, 0:NS]), r=[Rys, self.Rconst], w=[rb])
        P.op(ACT, lambda e, bk=bk: e.copy(out=oc[:, 4:8, NT:NT + NS], in_=bk[:, 0:4 * NS].rearrange("p (c s) -> p c s", c=4)),
             r=[rb], w=[rb] + Roc[4:8])

    def gdn(self, t, oc, Roc):
        P = self.P
        w_in = self.d["ab_w_in"]
        psubs = [(0, 512), (512, 512)]
        NCH = NT // 128
        qkv = self.scr_view(16640, [128, 12, NT], BF16)
        rawbuf = self.scr_view(41216, [128, 2, 1028], BF16)
        sm = 45328
        dnw = self.scr_view(sm, [128, 12, 4], F32)
        normg = self.scr_view(sm + 192, [128, 1, 1], F32)
        dtb = self.scr_view(sm + 196, [4, 1], F32)
        negA = self.scr_view(sm + 200, [4, 1], F32)
        dnlast = self.scr_view(sm + 208, [128, 12, 3], F32)
        bg = self.scr_view(sm + 400, [128, NCH, 8], F32)
        gcum = self.scr_view(sm + 656, [128, NCH, 4], F32)
        eg = self.scr_view(sm + 784, [128, NCH, 4], F32)
        etail = self.scr_view(sm + 912, [128, NCH, 4], F32)
        egl = self.scr_view(sm + 1040, [128, NCH, 4], F32)
        gl = self.scr_view(sm + 1168, [128, NCH, 4], F32)
        cw = self.scr_view(sm + 1296, [128, NCH, 4], F32)
        ssq = self.scr_view(sm + 1424, [128, 4], F32)
        rr = self.scr_view(sm + 1440, [128, 4], F32)
        T0 = 47104
        sqt = self.scr_view(T0, [128, 2, 512], BF16)
        rn = self.scr_view(T0 + 2048, [128, 2, 512], F32)
        bgT = self.scr_view(T0 + 6144, [4, 2, 512], F32)
        eaT = self.scr_view(T0 + 10240, [4, 512], F32)
        dgd = self.scr_view(T0 + 12288, [128, 2, 4, 128], BF16)
        Rqkv = [self.PR(f"qkv{c}") for c in range(12)]
        Rraw = [self.PR("raw0"), self.PR("raw1")]
        Rdgd = [self.PR("dgd0"), self.PR("dgd1")]
        Rprm, Rdnl, Rbg, Rgc = self.PR("gprm"), self.PR("dnlast"), self.PR("bg"), self.PR("gcum")
        Rssq = self.PR("ssq")
        Rsqt = [self.PR("sqt0"), self.PR("sqt1")]
        Rrn = [self.PR("rn0"), self.PR("rn1")]
        RbgT, Rea = self.PR("bgT"), self.PR("eaT")
        bc4 = lambda ap2: ap2.unsqueeze(2).to_broadcast([128, 4, 128])
        bcm = lambda m: m[:, :].unsqueeze(1).to_broadcast([128, 4, 128])
        self.fm_load(self.d["dn_conv_w"], 4, 1536, dnw, Rprm)
        self.fm_load(self.d["dn_norm_g"], 1, 128, normg, Rprm)
        P.load(SP, dtb[:], self.d["dn_dt_bias"].rearrange("a h -> h a"), Rprm, allow_slow_non_contiguous=True)
        P.load(SP, negA[:], self.d["dn_A_log"].rearrange("a h -> h a"), Rprm, allow_slow_non_contiguous=True)
        P.op(ACT, lambda e: e.activation(out=negA[:], in_=negA[:], func=AF.Exp), r=[Rprm], w=[Rprm])
        P.op(DVE, lambda e: e.tensor_scalar(out=negA[:], in0=negA[:], scalar1=-1.0, scalar2=None, op0=ALU.mult), r=[Rprm], w=[Rprm])
        def l2_part1(c):
            for q, (c0, n) in enumerate(psubs):
                P.op(ACT, lambda e, c=c, c0=c0, n=n, q=q: e.activation(out=sqt[:, q, 0:n], in_=qkv[:, c, c0:c0 + n], func=AF.Square),
                     r=[Rqkv[c]], w=[Rsqt[q]])

        def l2_part2(c):
            scl = float(128 ** -0.5) if c < 4 else 1.0
            for q, (c0, n) in enumerate(psubs):
                bk, rb = self.bank()
                P.op(PE, lambda e, bk=bk, q=q, n=n: e.matmul(bk[:, 0:n], lhsT=self.ones_b[:], rhs=sqt[:, q, 0:n], start=True, stop=True),
                     r=[Rsqt[q], self.Rconst], w=[rb])
                P.op(DVE, lambda e, bk=bk, q=q, n=n: e.tensor_scalar(out=rn[:, q, 0:n], in0=bk[:, 0:n], scalar1=1e-6, scalar2=None, op0=ALU.add),
                     r=[rb], w=[rb, Rrn[q]])
                P.op(ACT, lambda e, q=q, n=n: e.activation(out=rn[:, q, 0:n], in_=rn[:, q, 0:n], func=AF.Ln), r=[Rrn[q]], w=[Rrn[q]])
                P.op(ACT, lambda e, q=q, n=n: e.activation(out=rn[:, q, 0:n], in_=rn[:, q, 0:n], func=AF.Exp, scale=-0.5), r=[Rrn[q]], w=[Rrn[q]])
                P.op(DVE, lambda e, c=c, c0=c0, n=n, q=q, scl=scl: e.scalar_tensor_tensor(
                    out=qkv[:, c, c0:c0 + n], in0=qkv[:, c, c0:c0 + n], scalar=scl, in1=rn[:, q, 0:n], op0=ALU.mult, op1=ALU.mult),
                    r=[Rqkv[c], Rrn[q]], w=[Rqkv[c]])

        for g in range(6):
            slot, rs = self.wloadA(w_in, g * 256, 256)
            for jj in range(2):
                c = 2 * g + jj
                k = c % 2
                P.op(DVE, lambda e, k=k, c=c: e.tensor_copy(out=rawbuf[:, k, 0:3], in_=self.dn_halo[:, c, :]), r=[self.Rdnh], w=[Rraw[k]])
                for w in range(4):
                    P.op(DVE, lambda e, k=k, w=w, c=c: e.tensor_scalar(out=dgd[:, k, w, :], in0=self.ident_b[:], scalar1=dnw[:, c, w:w + 1],
                                                                     scalar2=None, op0=ALU.mult),
                         r=[Rprm, self.Rconst], w=[Rdgd[k]])
                for (c0, n) in psubs:
                    bk, rb = self.bank()
                    for kc in range(8):
                        P.op(PE, lambda e, bk=bk, kc=kc, jj=jj, c0=c0, n=n, slot=slot: e.matmul(
                            bk[:, 0:n], lhsT=slot[:, kc, jj * 128:(jj + 1) * 128], rhs=self.xb[:, kc, c0:c0 + n],
                            start=(kc == 0), stop=(kc == 7)), r=[rs, self.xbr(kc, c0)], w=[rb])
                    P.op(ACT, lambda e, bk=bk, k=k, c0=c0, n=n: e.copy(out=rawbuf[:, k, 3 + c0:3 + c0 + n], in_=bk[:, 0:n]),
                         r=[rb], w=[rb, Rraw[k]])
                    if t == 1 and c0 == 512:
                        P.op(DVE, lambda e, bk=bk, c=c: e.tensor_copy(out=dnlast[:, c, :], in_=bk[:, 509:512]), r=[rb], w=[rb, Rdnl])
                P.op(DVE, lambda e, k=k, c=c: e.tensor_copy(out=self.dn_halo[:, c, :], in_=rawbuf[:, k, 1024:1027]),
                     r=[Rraw[k]], w=[self.Rdnh])
                if 1 <= c <= 8:
                    l2_part2(c - 1)
                for (c0, n) in psubs:
                    bk, rb = self.bank()
                    for w in range(4):
                        P.op(PE, lambda e, bk=bk, w=w, k=k, c0=c0, n=n: e.matmul(
                            bk[:, 0:n], lhsT=dgd[:, k, w, :], rhs=rawbuf[:, k, c0 + w:c0 + w + n], start=(w == 0), stop=(w == 3)),
                            r=[Rdgd[k], Rraw[k]], w=[rb])
                    P.op(ACT, lambda e, bk=bk, c=c, c0=c0, n=n: e.activation(out=qkv[:, c, c0:c0 + n], in_=bk[:, 0:n], func=AF.Silu),
                         r=[rb], w=[rb, Rqkv[c]])
                if c < 8:
                    l2_part1(c)
        slot, rs = self.wloadA(w_in, 2048, 8)
        bgb, rbgb = self.bank(reserve=True)
        for si, (c0, n) in enumerate(psubs):
            bb_, rbb_ = self.bank()
            ba_, rba_ = self.bank()
            for kc in range(8):
                P.op(PE, lambda e, bb_=bb_, kc=kc, c0=c0, n=n, slot=slot: e.matmul(bb_[0:4, 0:n], lhsT=slot[:, kc, 0:4], rhs=self.xb[:, kc, c0:c0 + n],
                                                                                start=(kc == 0), stop=(kc == 7)), r=[rs, self.xbr(kc, c0)], w=[rbb_])
            for kc in range(8):
                P.op(PE, lambda e, ba_=ba_, kc=kc, c0=c0, n=n, slot=slot: e.matmul(ba_[0:4, 0:n], lhsT=slot[:, kc, 4:8], rhs=self.xb[:, kc, c0:c0 + n],
                                                                                start=(kc == 0), stop=(kc == 7)), r=[rs, self.xbr(kc, c0)], w=[rba_])
            P.op(ACT, lambda e, bb_=bb_, n=n: e.activation(out=bgT[:, 0, 0:n], in_=bb_[0:4, 0:n], func=AF.Sigmoid), r=[rbb_], w=[rbb_, RbgT])
            P.op(ACT, lambda e, ba_=ba_, n=n: e.activation(out=eaT[:, 0:n], in_=ba_[0:4, 0:n], func=AF.Exp, bias=dtb[:, 0:1], scale=1.0),
                 r=[rba_, Rprm], w=[rba_, Rea])
            P.op(DVE, lambda e, n=n: e.tensor_scalar(out=eaT[:, 0:n], in0=eaT[:, 0:n], scalar1=1.0, scalar2=None, op0=ALU.add), r=[Rea], w=[Rea])
            P.op(ACT, lambda e, n=n: e.activation(out=eaT[:, 0:n], in_=eaT[:, 0:n], func=AF.Ln), r=[Rea], w=[Rea])
            P.op(DVE, lambda e, n=n: e.tensor_scalar(out=bgT[:, 1, 0:n], in0=eaT[:, 0:n], scalar1=negA[:, 0:1], scalar2=None, op0=ALU.mult),
                 r=[Rea, Rprm], w=[RbgT])
            for j in range(4):
                nn = si * 4 + j
                for q in range(2):
                    P.op(PE, lambda e, nn=nn, q=q, j=j: e.transpose(bgb[:, nn * 8 + q * 4: nn * 8 + q * 4 + 4], bgT[:, q, j * 128:(j + 1) * 128],
                                                                 self.ident_f[0:4, 0:4]), r=[RbgT, self.Rconst], w=[rbgb])
        P.op(DVE, lambda e: e.tensor_copy(out=bg[:], in_=bgb[:, 0:NCH * 8].rearrange("p (n k) -> p n k", k=8)), r=[rbgb], w=[rbgb, Rbg])
        self.release(bgb)
        bk, rb = self.bank()
        P.op(PE, lambda e, bk=bk: e.matmul(bk[:, 0:NCH * 4], lhsT=self.mask_c[:], rhs=bg[:, :, 4:8], start=True, stop=True),
             r=[Rbg, self.Rconst], w=[rb])
        P.op(DVE, lambda e, bk=bk: e.tensor_copy(out=gcum[:], in_=bk[:, 0:NCH * 4].rearrange("p (n h) -> p n h", h=4)), r=[rb], w=[rb, Rgc])
        bk, rb = self.bank()
        P.op(PE, lambda e, bk=bk: e.matmul(bk[:, 0:NCH * 4], lhsT=self.sel127[:], rhs=gcum[:], start=True, stop=True),
             r=[Rgc, self.Rconst], w=[rb])
        P.op(DVE, lambda e, bk=bk: e.tensor_copy(out=gl[:], in_=bk[:, 0:NCH * 4].rearrange("p (n h) -> p n h", h=4)), r=[rb], w=[rb, Rgc])
        P.op(ACT, lambda e: e.activation(out=eg[:], in_=gcum[:], func=AF.Exp), r=[Rgc], w=[Rgc])
        P.op(ACT, lambda e: e.activation(out=egl[:], in_=gl[:], func=AF.Exp), r=[Rgc], w=[Rgc])
        P.op(DVE, lambda e: e.tensor_tensor(out=etail[:], in0=gl[:], in1=gcum[:], op=ALU.subtract), r=[Rgc], w=[Rgc])
        P.op(ACT, lambda e: e.activation(out=etail[:], in_=etail[:], func=AF.Exp), r=[Rgc], w=[Rgc])
        P.op(DVE, lambda e: e.tensor_tensor(out=cw[:], in0=eg[:], in1=bg[:, :, 0:4], op=ALU.mult), r=[Rgc, Rbg], w=[Rgc])
        if self.cfg.get("dbg") and t == 1:
            dflat = self.o["gm_v_sample"].rearrange("a b -> (a b)")
            P.store(SP, dflat[0:8192].rearrange("(p k) -> p k", p=128), bg[:].rearrange("p n k -> p (n k)"), Rbg)
            P.store(SP, dflat[8192:12288].rearrange("(p k) -> p k", p=128), gcum[:].rearrange("p n k -> p (n k)"), Rgc)
            P.store(SP, dflat[12288:16384].rearrange("(p k) -> p k", p=128), etail[:].rearrange("p n k -> p (n k)"), Rgc)
        zslots = [self.wloadA(w_in, 1536 + g * 256, 256) for g in range(2)]
        self.phase_barrier()
        self.keep(Roc + Rqkv + [Rprm, Rdnl, Rbg, Rgc, Rssq])
        dgm = self.scr_view(T0, [128, 2, 4, 128], F32)
        DT = self.scr_view(T0 + 4096, [128, 8, 128], F32)
        X = self.scr_view(T0 + 8192, [128, 8, 128], F32)
        u = self.scr_view(T0 + 12288, [128, 8, 128], F32)
        sqo = self.scr_view(T0 + 16384, [128, 4, 128], F32)
        bfb = T0 + 18432
        names = ["Xb", "QKT", "qg", "wT", "ktail", "RHSw", "RHSv"]
        bt = {nm: self.scr_view(bfb + 2048 * i, [128, 8, 128], BF16) for i, nm in enumerate(names)}
        bt["vnew"] = self.scr_view(bfb + 14336, [128, 4, 128], BF16)
        bt["on"] = self.scr_view(bfb + 15360, [128, 4, 128], BF16)
        fb = bfb + 16384
        Nf = [self.scr_view(fb + 4096 * i, [128, 8, 128], F32) for i in range(2)]
        Mf = [self.scr_view(fb + 8192 + 4096 * i, [128, 8, 128], F32) for i in range(2)]
        assert fb + 16384 <= self.SCR
        zsil = self.scr_view(41216, [128, 2, 512], BF16)
        RNf = [[self.PR(f"Nf{i}_{c}") for c in range(2)] for i in range(2)]
        RMf = [[self.PR(f"Mf{i}_{c}") for c in range(2)] for i in range(2)]
        Rdgm = [self.PR("dgm0"), self.PR("dgm1")]
        RDT = [self.PR("DTa"), self.PR("DTb")]
        RX = [self.PR("Xa"), self.PR("Xb_")]
        Ru = [self.PR("ua"), self.PR("ub")]
        Rsqo = self.PR("sqo")
        Rb = {nm: [self.PR(nm + "a"), self.PR(nm + "b")] for nm in names}
        Rb["vnew"] = self.PR("vnew")
        Rb["on"] = self.PR("on")
        Rzs = [self.PR("zs0"), self.PR("zs1")]
        v4 = lambda bank: bank[:].rearrange("p (h i) -> p h i", h=4)
        idg = 0
        pending_tail = []

        def flush_tail():
            while pending_tail:
                cs_ = pending_tail.pop(0)
                bk, rb = self.bank()
                bkb = bk[:].bitcast(BF16)
                for h in range(4):
                    P.op(PE, lambda e, bkb=bkb, h=h: e.transpose(bkb[:, h * 128:(h + 1) * 128], bt["on"][:, h, :], self.ident_b[:]),
                         r=[Rb["on"], self.Rconst], w=[rb])
                P.op(DVE, lambda e, bkb=bkb, cs_=cs_: e.tensor_scalar(out=oc[:, 0:4, cs_], in0=bkb[:, 0:512].rearrange("p (h i) -> p h i", h=4),
                                                                  scalar1=normg[:, 0, 0:1], scalar2=None, op0=ALU.mult),
                     r=[rb, Rprm], w=[rb] + Roc[0:4])

        for pr in range(NCH // 2):
            pair = (2 * pr, 2 * pr + 1)
            hs = [slice(0, 4), slice(4, 8)]

            def bcast_rows(src_ap):
                nonlocal idg
                q = idg % 2
                idg += 1
                P.op(POOL, lambda e, q=q: e.tensor_tensor(out=dgm[:, q], in0=bcm(self.ident_f), in1=bc4(src_ap), op=ALU.mult),
                     r=[Rgc, Rbg, self.Rconst], w=[Rdgm[q]])
                bk, rb = self.bank()
                P.op(PE, lambda e, bk=bk, q=q: e.matmul(bk[:, :], lhsT=self.ones_f[:], rhs=dgm[:, q].rearrange("p h i -> p (h i)"),
                                                       start=True, stop=True), r=[Rdgm[q], self.Rconst], w=[rb])
                return bk, rb

            for ci, n_ in enumerate(pair):
                bk, rb = bcast_rows(gcum[:, n_, :])
                P.op(DVE, lambda e, bk=bk, ci=ci, n_=n_: e.tensor_tensor(out=DT[:, hs[ci], :], in0=v4(bk), in1=bc4(gcum[:, n_, :]), op=ALU.subtract),
                     r=[rb, Rgc], w=[rb, RDT[ci]])
                P.op(DVE, lambda e, ci=ci: e.tensor_scalar(out=DT[:, hs[ci], :], in0=DT[:, hs[ci], :], scalar1=0.0, scalar2=None, op0=ALU.min),
                     r=[RDT[ci]], w=[RDT[ci]])
                P.op(ACT, lambda e, ci=ci: e.activation(out=DT[:, hs[ci], :], in_=DT[:, hs[ci], :], func=AF.Exp), r=[RDT[ci]], w=[RDT[ci]])
                P.op(DVE, lambda e, ci=ci: e.tensor_tensor(out=DT[:, hs[ci], :], in0=DT[:, hs[ci], :], in1=bcm(self.mask_c), op=ALU.mult),
                     r=[RDT[ci], self.Rconst], w=[RDT[ci]])
            bq, bkk = [None, None], [None, None]
            for ci, n_ in enumerate(pair):
                cs = slice(n_ * 128, (n_ + 1) * 128)
                bk, rb = bcast_rows(eg[:, n_, :])
                P.op(DVE, lambda e, bk=bk, cs=cs, ci=ci: e.tensor_tensor(out=bt["qg"][:, hs[ci], :], in0=v4(bk), in1=qkv[:, 0:4, cs], op=ALU.mult),
                     r=[rb] + Rqkv[0:4], w=[rb, Rb["qg"][ci]])
                bk, rb = self.bank()
                bkb = bk[:].bitcast(BF16)
                for h in range(4):
                    P.op(PE, lambda e, bkb=bkb, h=h, cs=cs: e.transpose(bkb[:, h * 128:(h + 1) * 128], qkv[:, 4 + h, cs], self.ident_b[:]),
                         r=[Rqkv[4 + h], self.Rconst], w=[rb])
                    P.op(PE, lambda e, bkb=bkb, h=h, cs=cs: e.transpose(bkb[:, (4 + h) * 128:(5 + h) * 128], qkv[:, 8 + h, cs], self.ident_b[:]),
                         r=[Rqkv[8 + h], self.Rconst], w=[rb])
                ktk = bkb[:, 0:512].rearrange("p (h d) -> p h d", h=4)
                vtk = bkb[:, 512:1024].rearrange("p (h d) -> p h d", h=4)
                P.op(DVE, lambda e, ktk=ktk, n_=n_, ci=ci: e.tensor_tensor(out=bt["ktail"][:, hs[ci], :], in0=ktk, in1=bc4(etail[:, n_, :]), op=ALU.mult),
                     r=[rb, Rgc], w=[rb, Rb["ktail"][ci]])
                P.op(DVE, lambda e, ktk=ktk, n_=n_, ci=ci: e.tensor_tensor(out=bt["RHSw"][:, hs[ci], :], in0=ktk, in1=bc4(cw[:, n_, :]), op=ALU.mult),
                     r=[rb, Rgc], w=[rb, Rb["RHSw"][ci]])
                P.op(DVE, lambda e, vtk=vtk, n_=n_, ci=ci: e.tensor_tensor(out=bt["RHSv"][:, hs[ci], :], in0=vtk, in1=bc4(bg[:, n_, 0:4]), op=ALU.mult),
                     r=[rb, Rbg], w=[rb, Rb["RHSv"][ci]])
                bgk, rbgk = self.bank()
                bgq, rbgq = self.bank()
                for h in range(4):
                    P.op(PE, lambda e, bgk=bgk, h=h, cs=cs: e.matmul(bgk[:, h * 128:(h + 1) * 128], lhsT=qkv[:, 4 + h, cs], rhs=qkv[:, 4 + h, cs],
                                                                  start=True, stop=True), r=[Rqkv[4 + h]], w=[rbgk])
                    P.op(PE, lambda e, bgq=bgq, h=h, cs=cs: e.matmul(bgq[:, h * 128:(h + 1) * 128], lhsT=qkv[:, 4 + h, cs], rhs=qkv[:, h, cs],
                                                                  start=True, stop=True), r=[Rqkv[4 + h], Rqkv[h]], w=[rbgq])
                P.op(DVE, lambda e, bgq=bgq, ci=ci: e.tensor_tensor(out=bt["QKT"][:, hs[ci], :], in0=v4(bgq), in1=DT[:, hs[ci], :], op=ALU.mult),
                     r=[rbgq, RDT[ci]], w=[rbgq, Rb["QKT"][ci]])
                bkk[ci] = (bgk, rbgk)
            for ci, n_ in enumerate(pair):
                bk, rb = bcast_rows(bg[:, n_, 0:4])
                P.op(DVE, lambda e, bk=bk, ci=ci: e.tensor_tensor(out=DT[:, hs[ci], :], in0=v4(bk), in1=DT[:, hs[ci], :], op=ALU.mult),
                     r=[rb, RDT[ci]], w=[rb, RDT[ci]])
                P.op(DVE, lambda e, ci=ci: e.tensor_tensor(out=DT[:, hs[ci], :], in0=DT[:, hs[ci], :], in1=bcm(self.mask_s), op=ALU.mult),
                     r=[RDT[ci], self.Rconst], w=[RDT[ci]])
                bgk, rbgk = bkk[ci]
                P.op(DVE, lambda e, bgk=bgk, ci=ci: e.tensor_tensor(out=Nf[0][:, hs[ci], :], in0=v4(bgk), in1=DT[:, hs[ci], :], op=ALU.mult),
                     r=[rbgk, RDT[ci]], w=[rbgk, RNf[0][ci]])
            for ci in range(2):
                bk, rb = self.bank()
                for h in range(4):
                    P.op(PE, lambda e, bk=bk, h=h, ci=ci: e.transpose(bk[:, h * 128:(h + 1) * 128], Nf[0][:, ci * 4 + h, :], self.ident_f[:]),
                         r=[RNf[0][ci], self.Rconst], w=[rb])
                P.op(ACT, lambda e, bk=bk, ci=ci: e.copy(out=Mf[0][:, hs[ci], :], in_=v4(bk)), r=[rb], w=[rb, RMf[0][ci]])
                P.op(DVE, lambda e, ci=ci: e.tensor_tensor(out=X[:, hs[ci], :], in0=bcm(self.ident_f), in1=Nf[0][:, hs[ci], :], op=ALU.subtract),
                     r=[RNf[0][ci], self.Rconst], w=[RX[ci]])
            cur = 0
            for lev in range(6):
                nxt = 1 - cur
                for ci in range(2):
                    bm, rbm = self.bank()
                    for h in range(4):
                        b_ = ci * 4 + h
                        P.op(PE, lambda e, bm=bm, h=h, b_=b_, cur=cur: e.matmul(bm[:, h * 128:(h + 1) * 128], lhsT=Nf[cur][:, b_, :], rhs=Mf[cur][:, b_, :],
                                                                             start=True, stop=True), r=[RNf[cur][ci], RMf[cur][ci]], w=[rbm])
                    P.op(ACT, lambda e, bm=bm, ci=ci, nxt=nxt: e.copy(out=Mf[nxt][:, hs[ci], :], in_=v4(bm)), r=[rbm], w=[rbm, RMf[nxt][ci]])
                if lev < 5:
                    for ci in range(2):
                        bn, rbn = self.bank()
                        for h in range(4):
                            b_ = ci * 4 + h
                            P.op(PE, lambda e, bn=bn, h=h, b_=b_, cur=cur: e.matmul(bn[:, h * 128:(h + 1) * 128], lhsT=Mf[cur][:, b_, :], rhs=Nf[cur][:, b_, :],
                                                                                 start=True, stop=True), r=[RNf[cur][ci], RMf[cur][ci]], w=[rbn])
                        P.op(DVE, lambda e, bn=bn, ci=ci, nxt=nxt: e.tensor_copy(out=Nf[nxt][:, hs[ci], :], in_=v4(bn)), r=[rbn], w=[rbn, RNf[nxt][ci]])
                for ci in range(2):
                    bp, rbp = self.bank()
                    for h in range(4):
                        b_ = ci * 4 + h
                        P.op(PE, lambda e, bp=bp, h=h, b_=b_, nxt=nxt: e.matmul(bp[:, h * 128:(h + 1) * 128], lhsT=Mf[nxt][:, b_, :], rhs=X[:, b_, :],
                                                                             start=True, stop=True), r=[RMf[nxt][ci], RX[ci]], w=[rbp])
                    P.op(DVE, lambda e, bp=bp, ci=ci: e.tensor_tensor(out=X[:, hs[ci], :], in0=v4(bp), in1=X[:, hs[ci], :], op=ALU.add),
                         r=[rbp, RX[ci]], w=[rbp, RX[ci]])
                cur = nxt
            for ci in range(2):
                P.op(ACT, lambda e, ci=ci: e.copy(out=bt["Xb"][:, hs[ci], :], in_=X[:, hs[ci], :]), r=[RX[ci]], w=[Rb["Xb"][ci]])
                bu, rbu = self.bank()
                bw, rbw = self.bank()
                for h in range(4):
                    b_ = ci * 4 + h
                    P.op(PE, lambda e, bu=bu, h=h, b_=b_: e.matmul(bu[:, h * 128:(h + 1) * 128], lhsT=bt["Xb"][:, b_, :], rhs=bt["RHSv"][:, b_, :],
                                                                  start=True, stop=True), r=[Rb["Xb"][ci], Rb["RHSv"][ci]], w=[rbu])
                    P.op(PE, lambda e, bw=bw, h=h, b_=b_: e.matmul(bw[:, h * 128:(h + 1) * 128], lhsT=bt["RHSw"][:, b_, :], rhs=bt["Xb"][:, b_, :],
                                                                  start=True, stop=True), r=[Rb["Xb"][ci], Rb["RHSw"][ci]], w=[rbw])
                P.op(DVE, lambda e, bu=bu, ci=ci: e.tensor_copy(out=u[:, hs[ci], :], in_=v4(bu)), r=[rbu], w=[rbu, Ru[ci]])
                P.op(ACT, lambda e, bw=bw, ci=ci: e.copy(out=bt["wT"][:, hs[ci], :], in_=v4(bw)), r=[rbw], w=[rbw, Rb["wT"][ci]])
            for ci, n_ in enumerate(pair):
                cs = slice(n_ * 128, (n_ + 1) * 128)
                bws, rbws = self.bank()
                for h in range(4):
                    b_ = ci * 4 + h
                    P.op(PE, lambda e, bws=bws, h=h, b_=b_: e.matmul(bws[:, h * 128:(h + 1) * 128], lhsT=bt["wT"][:, b_, :], rhs=self.Sb[:, h, :],
                                                                    start=True, stop=True), r=[Rb["wT"][ci], self.RSb[h]], w=[rbws])
                P.op(DVE, lambda e, bws=bws, ci=ci: e.tensor_tensor(out=bt["vnew"][:], in0=u[:, hs[ci], :], in1=v4(bws), op=ALU.subtract),
                     r=[rbws, Ru[ci]], w=[rbws, Rb["vnew"]])
                bo, rbo = self.bank()
                bs_, rbs_ = self.bank()
                for h in range(4):
                    b_ = ci * 4 + h
                    P.op(PE, lambda e, bo=bo, h=h, b_=b_: e.matmul(bo[:, h * 128:(h + 1) * 128], lhsT=bt["qg"][:, b_, :], rhs=self.Sb[:, h, :],
                                                                  start=True, stop=False), r=[Rb["qg"][ci], self.RSb[h]], w=[rbo])
                    P.op(PE, lambda e, bo=bo, h=h, b_=b_: e.matmul(bo[:, h * 128:(h + 1) * 128], lhsT=bt["QKT"][:, b_, :], rhs=bt["vnew"][:, h, :],
                                                                  start=False, stop=True), r=[Rb["QKT"][ci], Rb["vnew"]], w=[rbo])
                    P.op(PE, lambda e, bs_=bs_, h=h, b_=b_: e.matmul(bs_[:, h * 128:(h + 1) * 128], lhsT=bt["ktail"][:, b_, :], rhs=bt["vnew"][:, h, :],
                                                                    start=True, stop=True), r=[Rb["ktail"][ci], Rb["vnew"]], w=[rbs_])
                for h in range(4):
                    P.op(DVE, lambda e, bs_=bs_, h=h, n_=n_: e.scalar_tensor_tensor(
                        out=self.Sf[:, h, :], in0=self.Sf[:, h, :], scalar=egl[:, n_, h:h + 1], in1=bs_[:, h * 128:(h + 1) * 128],
                        op0=ALU.mult, op1=ALU.add), r=[rbs_, Rgc, self.RS[h]], w=[rbs_, self.RS[h]])
                P.op(ACT, lambda e: e.copy(out=self.Sb[:], in_=self.Sf[:]), r=self.RS, w=self.RSb)
                flush_tail()
                for h in range(4):
                    P.op(ACT, lambda e, bo=bo, h=h: e.activation(out=sqo[:, h, :], in_=bo[:, h * 128:(h + 1) * 128], func=AF.Square,
                                                              accum_out=ssq[:, h:h + 1]), r=[rbo], w=[rbo, Rsqo, Rssq])
                P.op(DVE, lambda e: e.tensor_scalar(out=rr[:], in0=ssq[:], scalar1=1.0 / 128.0, scalar2=1e-6, op0=ALU.mult, op1=ALU.add), r=[Rssq], w=[Rssq])
                P.op(ACT, lambda e: e.activation(out=rr[:], in_=rr[:], func=AF.Ln), r=[Rssq], w=[Rssq])
                P.op(ACT, lambda e: e.activation(out=rr[:], in_=rr[:], func=AF.Exp, scale=-0.5), r=[Rssq], w=[Rssq])
                P.op(DVE, lambda e, bo=bo: e.tensor_tensor(out=bt["on"][:], in0=v4(bo), in1=bc4(rr[:, :]), op=ALU.mult),
                     r=[rbo, Rssq], w=[rbo, Rb["on"]])
                pending_tail.append(cs)
        flush_tail()
        iz = 0
        for g in range(2):
            slot, rs = zslots[g]
            for jj in range(2):
                h = 2 * g + jj
                for (c0, n) in psubs:
                    bk, rb = self.bank()
                    for kc in range(8):
                        P.op(PE, lambda e, bk=bk, kc=kc, jj=jj, c0=c0, n=n, slot=slot: e.matmul(
                            bk[:, 0:n], lhsT=slot[:, kc, jj * 128:(jj + 1) * 128], rhs=self.xb[:, kc, c0:c0 + n],
                            start=(kc == 0), stop=(kc == 7)), r=[rs, self.xbr(kc, c0)], w=[rb])
                    q = iz % 2
                    iz += 1
                    P.op(ACT, lambda e, bk=bk, q=q, n=n: e.activation(out=zsil[:, q, 0:n], in_=bk[:, 0:n], func=AF.Silu), r=[rb], w=[rb, Rzs[q]])
                    P.op(DVE, lambda e, h=h, c0=c0, n=n, q=q: e.tensor_tensor(out=oc[:, h, c0:c0 + n], in0=oc[:, h, c0:c0 + n], in1=zsil[:, q, 0:n],
                                                                          op=ALU.mult), r=[Roc[h], Rzs[q]], w=[Roc[h]])
        if t == 1:
            P.store(SP, self.o["dn_S_prompt"].rearrange("h d v -> d h v"), self.Sf[:], self.RS[3])
            for g in range(3):
                i = self.istage % 2
                self.istage += 1
                Rs = self.Rstage[i]
                bk, rb = self.bank()
                for j in range(4):
                    c = g * 4 + j
                    P.op(PE, lambda e, bk=bk, j=j, c=c: e.transpose(bk[0:3, j * 128:(j + 1) * 128], dnlast[:, c, :], self.ident_f[:]),
                         r=[Rdnl, self.Rconst], w=[rb])
                P.op(DVE, lambda e, bk=bk, i=i: e.tensor_copy(out=self.stage[0:3, i, 0:512], in_=bk[0:3, :]), r=[rb], w=[rb, Rs])
                P.store(SP, self.o["dn_conv_prompt"][:, g * 512:(g + 1) * 512], self.stage[0:3, i, 0:512], Rs)

    def gdn_samples(self, oc, Roc):
        P = self.P
        w_in = self.d["ab_w_in"]
        o = 16640
        S0 = self.scr_view(o, [128, NS, 4, 128], F32); o += 32768
        stcb = self.scr_view(o, [NS, 3, 512], F32); o += 6144
        wb = self.scr_view(o, [NS, 4, 512], F32); o += 8192
        qraw = self.scr_view(o, [NS, 1536], F32); o += 6144
        qkvs = self.scr_view(o, [NS, 1536], F32); o += 6144
        tmp = self.scr_view(o, [NS, 2, 512], F32); zs = self.scr_view(o, [NS, 512], F32); o += 4096
        qkn = self.scr_view(o, [NS, 8, 128], F32); o += 4096
        sqs = self.scr_view(o, [NS, 8, 128], F32); o += 4096
        ba = self.scr_view(o, [NS, 8], F32); o += 32
        bsg = self.scr_view(o, [NS, 3, 4], F32); o += 48
        gt = self.scr_view(o, [NS, 4], F32); o += 16
        dtbb = self.scr_view(o, [NS, 4], F32); o += 16
        negAb = self.scr_view(o, [NS, 4], F32); o += 16
        ss = self.scr_view(o, [NS, 8], F32); o += 32
        id16 = self.scr_view(o, [NS, NS], F32); o += 64
        rhsb = self.scr_view(o, [NS, 12, NS], F32); o += 768
        fm = self.scr_view(o, [128, 12, NS], F32); o += 768
        zfm = self.scr_view(o, [128, 4, NS], F32); o += 256
        bcs = self.scr_view(o, [128, 12, NS], F32); o += 768
        t1 = self.scr_view(o, [128, 4, NS], F32); o += 256
        t2 = self.scr_view(o, [128, 4, NS], F32); o += 256
        vnw = self.scr_view(o, [128, 4, NS], F32); o += 256
        oT = self.scr_view(o, [128, 4, NS], F32); o += 256
        rrb = self.scr_view(o, [128, 4, NS], F32); o += 256
        normg = self.scr_view(o, [128, 1, 1], F32); o += 16
        vdg = self.scr_view(o, [128, 2, 128], F32); o += 1024
        tS = self.scr_view(o, [128, 2, 128], F32); o += 1024
        assert o <= self.SCR
        RS0 = [self.PR(f"S0_{s}") for s in range(NS)]
        Rstc, Rwb, Rqraw, Rqkvs, Rtmp, Rqkn, Rsq, Rsm, Rfm, Rbc, Rt, Rprm = (self.PR(n) for n in (
            "stcb", "wbs", "qraws", "qkvss", "tmps", "qkns", "sqss", "smalls", "fms", "bcs", "ts", "gsprm"))
        Rvdg = [self.PR("vdg0"), self.PR("vdg1")]
        RtS = [self.PR("tS0"), self.PR("tS1")]
        for s_ in range(NS):
            P.load(SP, S0[:, s_], self.d["state_dn_S"][s_].rearrange("h d v -> d h v"), RS0[s_])
        P.load(SP, dtbb[:], self.d["dn_dt_bias"].partition_broadcast(NS).rearrange("p a h -> p (a h)"), Rprm)
        P.load(SP, negAb[:], self.d["dn_A_log"].partition_broadcast(NS).rearrange("p a h -> p (a h)"), Rprm)
        P.op(ACT, lambda e: e.activation(out=negAb[:], in_=negAb[:], func=AF.Exp), r=[Rprm], w=[Rprm])
        P.op(DVE, lambda e: e.tensor_scalar(out=negAb[:], in0=negAb[:], scalar1=-1.0, scalar2=None, op0=ALU.mult), r=[Rprm], w=[Rprm])
        self.fm_load(self.d["dn_norm_g"], 1, 128, normg, Rprm)
        P.op(DVE, lambda e: e.tensor_copy(out=id16[:], in_=self.ident_f[0:NS, 0:NS]), r=[self.Rconst], w=[Rprm])
        for g in range(6):
            slot, rs = self.wloadA(w_in, g * 256, 256)
            bk, rb = self.bank()
            for kc in range(8):
                P.op(PE, lambda e, bk=bk, kc=kc, slot=slot: e.matmul(bk[0:NS, 0:256], lhsT=self.xb[:, kc, NT:NT + NS], rhs=slot[:, kc, 0:256],
                                                                    start=(kc == 0), stop=(kc == 7)), r=[rs, self.xbr(kc, NT)], w=[rb])
            P.op(ACT, lambda e, bk=bk, g=g: e.copy(out=qraw[:, g * 256:(g + 1) * 256], in_=bk[0:NS, 0:256]), r=[rb], w=[rb, Rqraw])
        slot, rs = self.wloadA(w_in, 2048, 8)
        bk, rb = self.bank()
        for kc in range(8):
            P.op(PE, lambda e, bk=bk, kc=kc, slot=slot: e.matmul(bk[0:NS, 0:8], lhsT=self.xb[:, kc, NT:NT + NS], rhs=slot[:, kc, 0:8],
                                                                start=(kc == 0), stop=(kc == 7)), r=[rs, self.xbr(kc, NT)], w=[rb])
        P.op(DVE, lambda e, bk=bk: e.tensor_copy(out=ba[:], in_=bk[0:NS, 0:8]), r=[rb], w=[rb, Rsm])
        P.d2d(SP, self.o["dn_conv_sample"][:, 0:2, :], self.d["state_dn_conv"][:, 1:3, :], self.Rd2d)
        P.store(SP, self.o["dn_conv_sample"][:, 2, :], qraw[:, :], Rqraw)
        P.op(ACT, lambda e: e.activation(out=bsg[:, 0, :], in_=ba[:, 0:4], func=AF.Sigmoid), r=[Rsm], w=[Rsm])
        P.op(DVE, lambda e: e.tensor_tensor(out=gt[:], in0=ba[:, 4:8], in1=dtbb[:], op=ALU.add), r=[Rsm, Rprm], w=[Rsm])
        P.op(ACT, lambda e: e.activation(out=gt[:], in_=gt[:], func=AF.Exp), r=[Rsm], w=[Rsm])
        P.op(DVE, lambda e: e.tensor_scalar(out=gt[:], in0=gt[:], scalar1=1.0, scalar2=None, op0=ALU.add), r=[Rsm], w=[Rsm])
        P.op(ACT, lambda e: e.activation(out=gt[:], in_=gt[:], func=AF.Ln), r=[Rsm], w=[Rsm])
        P.op(DVE, lambda e: e.tensor_tensor(out=gt[:], in0=gt[:], in1=negAb[:], op=ALU.mult), r=[Rsm, Rprm], w=[Rsm])
        P.op(ACT, lambda e: e.activation(out=bsg[:, 1, :], in_=gt[:], func=AF.Exp), r=[Rsm], w=[Rsm])
        for cb in range(3):
            csl = slice(cb * 512, (cb + 1) * 512)
            P.load(SP, stcb[:], self.d["state_dn_conv"][:, :, csl], Rstc)
            P.load(SP, wb[:], self.d["dn_conv_w"][:, csl].partition_broadcast(NS), Rwb)
            P.op(DVE, lambda e: e.tensor_tensor(out=tmp[:, 0, :], in0=stcb[:, 0, :], in1=wb[:, 0, :], op=ALU.mult), r=[Rstc, Rwb], w=[Rtmp])
            for w in (1, 2):
                P.op(DVE, lambda e, w=w: e.tensor_tensor(out=tmp[:, 1, :], in0=stcb[:, w, :], in1=wb[:, w, :], op=ALU.mult), r=[Rstc, Rwb], w=[Rtmp])
                P.op(DVE, lambda e: e.tensor_tensor(out=tmp[:, 0, :], in0=tmp[:, 0, :], in1=tmp[:, 1, :], op=ALU.add), r=[Rtmp], w=[Rtmp])
            P.op(DVE, lambda e, csl=csl: e.tensor_tensor(out=tmp[:, 1, :], in0=qraw[:, csl], in1=wb[:, 3, :], op=ALU.mult), r=[Rqraw, Rwb], w=[Rtmp])
            P.op(DVE, lambda e: e.tensor_tensor(out=tmp[:, 0, :], in0=tmp[:, 0, :], in1=tmp[:, 1, :], op=ALU.add), r=[Rtmp], w=[Rtmp])
            P.op(ACT, lambda e, csl=csl: e.activation(out=qkvs[:, csl], in_=tmp[:, 0, :], func=AF.Silu), r=[Rtmp], w=[Rqkvs])
        P.op(ACT, lambda e: e.activation(out=sqs[:], in_=qkvs[:, 0:1024].rearrange("p (c d) -> p c d", c=8), func=AF.Square), r=[Rqkvs], w=[Rsq])
        P.op(DVE, lambda e: e.tensor_reduce(out=ss[:], in_=sqs[:], axis=AX.X, op=ALU.add), r=[Rsq], w=[Rsm])
        P.op(DVE, lambda e: e.tensor_scalar(out=ss[:], in0=ss[:], scalar1=1e-6, scalar2=None, op0=ALU.add), r=[Rsm], w=[Rsm])
        P.op(ACT, lambda e: e.activation(out=ss[:], in_=ss[:], func=AF.Ln), r=[Rsm], w=[Rsm])
        P.op(ACT, lambda e: e.activation(out=ss[:], in_=ss[:], func=AF.Exp, scale=-0.5), r=[Rsm], w=[Rsm])
        P.op(DVE, lambda e: e.tensor_scalar(out=ss[:, 0:4], in0=ss[:, 0:4], scalar1=float(128 ** -0.5), scalar2=None, op0=ALU.mult), r=[Rsm], w=[Rsm])
        P.op(DVE, lambda e: e.tensor_tensor(out=qkn[:], in0=qkvs[:, 0:1024].rearrange("p (c d) -> p c d", c=8),
                                            in1=ss[:, :].unsqueeze(2).to_broadcast([NS, 8, 128]), op=ALU.mult), r=[Rqkvs, Rsm], w=[Rqkn])
        P.op(DVE, lambda e: e.tensor_tensor(out=sqs[:, 0:4, :], in0=qkn[:, 0:4, :], in1=qkn[:, 4:8, :], op=ALU.mult), r=[Rqkn], w=[Rsq])
        P.op(DVE, lambda e: e.tensor_reduce(out=bsg[:, 2, :], in_=sqs[:, 0:4, :], axis=AX.X, op=ALU.add), r=[Rsq], w=[Rsm])
        P.op(DVE, lambda e: e.tensor_tensor(out=rhsb[:], in0=id16[:, :].unsqueeze(1).to_broadcast([NS, 12, NS]),
                                            in1=bsg[:].rearrange("p a h -> p (a h)").unsqueeze(2).to_broadcast([NS, 12, NS]), op=ALU.mult),
             r=[Rsm, Rprm], w=[Rsm])
        bk, rb = self.bank()
        P.op(PE, lambda e, bk=bk: e.matmul(bk[:, 0:12 * NS], lhsT=self.ones_f[0:NS, :], rhs=rhsb[:].rearrange("p a s -> p (a s)"), start=True, stop=True),
             r=[Rsm, self.Rconst], w=[rb])
        P.op(DVE, lambda e, bk=bk: e.tensor_copy(out=bcs[:], in_=bk[:, 0:12 * NS].rearrange("p (a s) -> p a s", a=12)), r=[rb], w=[rb, Rbc])
        bk, rb = self.bank()
        for c in range(12):
            src = qkn[:, c, :] if c < 8 else qkvs[:, c * 128:(c + 1) * 128]
            P.op(PE, lambda e, bk=bk, c=c, src=src: e.transpose(bk[:, c * NS:(c + 1) * NS], src, self.ident_f[0:NS, 0:NS]),
                 r=[Rqkn, Rqkvs, self.Rconst], w=[rb])
        P.op(DVE, lambda e, bk=bk: e.tensor_copy(out=fm[:], in_=bk[:, 0:12 * NS].rearrange("p (c s) -> p c s", c=12)), r=[rb], w=[rb, Rfm])
        for g in range(2):
            slot, rs = self.wloadA(w_in, 1536 + g * 256, 256)
            bk, rb = self.bank()
            for kc in range(8):
                P.op(PE, lambda e, bk=bk, kc=kc, slot=slot: e.matmul(bk[0:NS, 0:256], lhsT=self.xb[:, kc, NT:NT + NS], rhs=slot[:, kc, 0:256],
                                                                    start=(kc == 0), stop=(kc == 7)), r=[rs, self.xbr(kc, NT)], w=[rb])
            P.op(ACT, lambda e, bk=bk, g=g: e.activation(out=zs[:, g * 256:(g + 1) * 256], in_=bk[0:NS, 0:256], func=AF.Silu), r=[rb], w=[rb, Rtmp])
        bk, rb = self.bank()
        for h in range(4):
            P.op(PE, lambda e, bk=bk, h=h: e.transpose(bk[:, h * NS:(h + 1) * NS], zs[:, h * 128:(h + 1) * 128], self.ident_f[0:NS, 0:NS]),
                 r=[Rtmp, self.Rconst], w=[rb])
        P.op(DVE, lambda e, bk=bk: e.tensor_copy(out=zfm[:], in_=bk[:, 0:4 * NS].rearrange("p (c s) -> p c s", c=4)), r=[rb], w=[rb, Rfm])
        kq, rkq = self.bank(reserve=True)
        for s_ in range(NS):
            for h in range(4):
                for a, c in ((0, 4 + h), (1, h)):
                    col = (a * 4 + h) * NS + s_
                    P.op(PE, lambda e, s_=s_, h=h, c=c, col=col: e.matmul(kq[:, col:col + 1], lhsT=S0[:, s_, h, :], rhs=fm[:, c, s_:s_ + 1],
                                                                        start=True, stop=True), r=[RS0[s_], Rfm], w=[rkq])
        kS = kq[:, 0:4 * NS].rearrange("p (h s) -> p h s", h=4)
        qS = kq[:, 4 * NS:8 * NS].rearrange("p (h s) -> p h s", h=4)
        bb, egb, qkb = bcs[:, 0:4, :], bcs[:, 4:8, :], bcs[:, 8:12, :]
        P.op(DVE, lambda e: e.tensor_tensor(out=t1[:], in0=kS, in1=egb, op=ALU.mult), r=[rkq, Rbc], w=[rkq, Rt])
        P.op(DVE, lambda e: e.tensor_tensor(out=t1[:], in0=fm[:, 8:12, :], in1=t1[:], op=ALU.subtract), r=[Rfm, Rt], w=[Rt])
        P.op(DVE, lambda e: e.tensor_tensor(out=vnw[:], in0=t1[:], in1=bb, op=ALU.mult), r=[Rt, Rbc], w=[Rt])
        P.op(DVE, lambda e: e.tensor_tensor(out=t2[:], in0=qS, in1=egb, op=ALU.mult), r=[rkq, Rbc], w=[rkq, Rt])
        P.op(DVE, lambda e: e.tensor_tensor(out=t1[:], in0=vnw[:], in1=qkb, op=ALU.mult), r=[Rt, Rbc], w=[Rt])
        P.op(DVE, lambda e: e.tensor_tensor(out=oT[:], in0=t1[:], in1=t2[:], op=ALU.add), r=[Rt], w=[Rt])
        self.release(kq)
        P.op(ACT, lambda e: e.activation(out=t1[:], in_=oT[:], func=AF.Square), r=[Rt], w=[Rt])
        bk, rb = self.bank()
        P.op(PE, lambda e, bk=bk: e.matmul(bk[:, 0:4 * NS], lhsT=self.ones_f[:], rhs=t1[:].rearrange("p h s -> p (h s)"), start=True, stop=True),
             r=[Rt, self.Rconst], w=[rb])
        P.op(DVE, lambda e, bk=bk: e.tensor_scalar(out=rrb[:], in0=bk[:, 0:4 * NS].rearrange("p (h s) -> p h s", h=4), scalar1=1.0 / 128.0, scalar2=1e-6,
                                                  op0=ALU.mult, op1=ALU.add), r=[rb], w=[rb, Rt])
        P.op(ACT, lambda e: e.activation(out=rrb[:], in_=rrb[:], func=AF.Ln), r=[Rt], w=[Rt])
        P.op(ACT, lambda e: e.activation(out=rrb[:], in_=rrb[:], func=AF.Exp, scale=-0.5), r=[Rt], w=[Rt])
        P.op(DVE, lambda e: e.tensor_tensor(out=t2[:], in0=oT[:], in1=rrb[:], op=ALU.mult), r=[Rt], w=[Rt])
        P.op(DVE, lambda e: e.tensor_tensor(out=t2[:], in0=t2[:], in1=zfm[:], op=ALU.mult), r=[Rt, Rfm], w=[Rt])
        P.op(DVE, lambda e: e.tensor_scalar(out=oc[:, 0:4, NT:NT + NS], in0=t2[:], scalar1=normg[:, 0, 0:1], scalar2=None, op0=ALU.mult),
             r=[Rt, Rprm], w=Roc[0:4])
        ip = 0
        for s_ in range(NS):
            for h in range(4):
                q = ip % 2
                ip += 1
                P.op(DVE, lambda e, q=q, h=h, s_=s_: e.tensor_scalar(out=vdg[:, q, :], in0=self.ident_f[:], scalar1=vnw[:, h, s_:s_ + 1], scalar2=None,
                                                                  op0=ALU.mult), r=[Rt, self.Rconst], w=[Rvdg[q]])
                bk, rb = self.bank()
                P.op(PE, lambda e, bk=bk, q=q: e.matmul(bk[:, 0:128], lhsT=self.ones_f[:], rhs=vdg[:, q, :], start=True, stop=True),
                     r=[Rvdg[q], self.Rconst], w=[rb])
                P.op(DVE, lambda e, q=q, h=h, s_=s_: e.tensor_scalar(out=tS[:, q, :], in0=S0[:, s_, h, :], scalar1=egb[:, h, s_:s_ + 1], scalar2=None,
                                                                  op0=ALU.mult), r=[RS0[s_], Rbc], w=[RtS[q]])
                P.op(DVE, lambda e, bk=bk, q=q, h=h, s_=s_: e.scalar_tensor_tensor(out=S0[:, s_, h, :], in0=bk[:, 0:128], scalar=fm[:, 4 + h, s_:s_ + 1],
                                                                                in1=tS[:, q, :], op0=ALU.mult, op1=ALU.add),
                     r=[rb, Rfm, RtS[q]], w=[rb, RS0[s_]])
            P.store(SP, self.o["dn_S_sample"][s_].rearrange("h d v -> d h v"), S0[:, s_], RS0[s_])

    def build(self):
        cfg = self.cfg
        self.declare_io()
        self.alloc_common()
        self.ln_off = 46 * 1024
        self.sg_off = 70 * 1024
        self.Rln = [Res(n) for n in ("zb", "zsq", "mean", "rstd")]
        self.Rlnt = [Res("lnt0"), Res("lnt1")]
        self.isg = 0
        self.build_consts()
        stages = cfg.get("stages", "all")
        on = lambda nm: stages == "all" or nm in stages
        if on("memkv") or on("xattn0") or on("xattn1"):
            self.mem_kv()
            self.phase_barrier()
        for t in cfg.get("tiles", (0, 1)):
            self.load_x(t)
            for l in range(2):
                if on(f"ffn{l}0"):
                    self.ffn(t, l, 0)
                    self.phase_barrier()
                if on(f"mix{l}"):
                    self.mixer(t, l)
                    self.phase_barrier()
                if on(f"xattn{l}"):
                    self.xattn(t, l)
                    self.phase_barrier()
                if on(f"ffn{l}1"):
                    self.ffn(t, l, 1)
                    self.phase_barrier()
            self.store_y(t)
        stats = self.P.emit()
        self.es.close()
        return stats


_OUT_NAMES = ["y_prompt", "y_sample", "mem_k", "mem_v", "dn_S_prompt", "dn_conv_prompt", "cc_conv_prompt",
              "dn_S_sample", "dn_conv_sample", "cc_conv_sample", "gm_v_sample"]


def make_in_maps(inputs, cores):
    f = lambda a: np.ascontiguousarray(np.asarray(a, dtype=np.float32))
    shared = {}
    shared["ln_g"] = f(inputs["ln_g"]).reshape(8, D)
    shared["ln_b"] = f(inputs["ln_b"]).reshape(8, D)
    shared["ffn_w_gate"] = f(inputs["ffn_w_gate"]).reshape(4, D, DFF)
    shared["ffn_w_up"] = f(inputs["ffn_w_up"]).reshape(4, D, DFF)
    shared["ffn_w_down"] = f(inputs["ffn_w_down"]).reshape(4, DFF, D)
    for n in ("xa_wq", "xa_wk", "xa_wv", "xa_wo", "ab_w_in", "dn_conv_w", "cc_conv_w", "ab_w_out", "gm_w_in", "gm_w_s",
              "gm_b_s", "gm_w_out"):
        shared[n] = f(inputs[n])
    for n in ("dn_A_log", "dn_dt_bias", "dn_norm_g", "cc_conv_b", "cc_ln_g", "cc_ln_b", "gm_ln_g", "gm_ln_b"):
        shared[n] = f(inputs[n]).reshape(1, -1)
    maps = []
    for c in cores:
        m = dict(shared)
        m["x_prompt"] = f(inputs["x_prompt"][c])
        m["x_sample"] = f(inputs["x_sample"][NS * c:NS * (c + 1), 0])
        m["mem_prompt"] = f(inputs["mem_prompt"][c])
        m["cache_mem_k"] = f(inputs["cache_mem_k"][:, NS * c:NS * (c + 1)]).reshape(2, NS, NMEM, D)
        m["cache_mem_v"] = f(inputs["cache_mem_v"][:, NS * c:NS * (c + 1)]).reshape(2, NS, NMEM, D)
        m["state_dn_S"] = f(inputs["state_dn_S"][NS * c:NS * (c + 1)])
        m["state_dn_conv"] = f(inputs["state_dn_conv"][NS * c:NS * (c + 1)])
        m["state_cc_conv"] = f(inputs["state_cc_conv"][NS * c:NS * (c + 1)])
        maps.append(m)
    return maps


def assemble(results):
    n = len(results)
    y_prompt = np.stack([r["y_prompt"] for r in results])
    y_sample = np.concatenate([r["y_sample"] for r in results])[:, None, :]
    mem_k = np.stack([r["mem_k"] for r in results], axis=1).reshape(2, n, NMEM, 4, 256)
    mem_v = np.stack([r["mem_v"] for r in results], axis=1).reshape(2, n, NMEM, 4, 256)
    dn_S_p = np.stack([r["dn_S_prompt"] for r in results])
    dn_c_p = np.stack([r["dn_conv_prompt"] for r in results])
    cc_c_p = np.stack([r["cc_conv_prompt"] for r in results])
    dn_S_s = np.concatenate([r["dn_S_sample"] for r in results])
    dn_c_s = np.concatenate([r["dn_conv_sample"] for r in results])
    cc_c_s = np.concatenate([r["cc_conv_sample"] for r in results])
    gm_v = np.concatenate([r["gm_v_sample"] for r in results])[:, None, :]
    outs = (y_prompt, y_sample, mem_k, mem_v, dn_S_p, dn_c_p, cc_c_p, dn_S_s, dn_c_s, cc_c_s, gm_v)
    return tuple(np.ascontiguousarray(o, dtype=np.float32) for o in outs)


def kernel(**inputs):
    kb = KB({})
    kb.build()
    cores = list(range(8))
    res = run_bass_kernel_spmd(kb.nc, make_in_maps(inputs, cores), core_ids=cores)
    return assemble(res.results)
```
